# Optimizing a Trainium2 kernel written in Bass

```python
import jax, jax.numpy as jnp
from jax import lax
import numpy as np

D_MODEL = 2048
BATCH = 16
SEQ = 2048
DEPTH = 2
DEC_BATCH = 32
DEC_SEQ = 64
PAST_LEN = 4096

CHUNK = 64
N_MIXERS = 2
N_CONV_LAYERS = (DEPTH + 1) // 2
N_ATTN_LAYERS = DEPTH // 2
CONV_WIDTH = 3
N_HEADS = 16
HEAD_DIM = D_MODEL // N_HEADS
PAST_CHUNKS = 8
PAST_WIN = PAST_CHUNKS * CHUNK
BAND = PAST_WIN + CHUNK
MAX_REL = 128
N_REL = 2 * MAX_REL + 1
N_MEM = 256
MEM_HEADS = 4
MEM_HEAD_DIM = 128
MEM_WIDTH = MEM_HEADS * MEM_HEAD_DIM
D_FF = 5632
N_NORMS = 8
EPS = 1e-6
NEG_INF = -1e30

kernel_name = "hybrid_streaming_conv_band_attn_step"


def rmsnorm(x, g):
    xf = x.astype(jnp.float32)
    y = xf * lax.rsqrt(jnp.mean(xf * xf, axis=-1, keepdims=True) + EPS)
    return y.astype(x.dtype) * g


def swiglu(x, w_in, w_out):
    gate, up = jnp.split(x @ w_in, 2, axis=-1)
    return (jax.nn.silu(gate) * up) @ w_out


def short_conv(xn, hist, w_in, w_dw, w_out):
    b, l, d = xn.shape
    bg, cg, xv = jnp.split(xn @ w_in, 3, axis=-1)
    u = cg * xv
    if hist is None:
        hist = jnp.zeros((b, CONV_WIDTH - 1, d), u.dtype)
    u_full = jnp.concatenate([hist.astype(u.dtype), u], axis=1)
    y = lax.conv_general_dilated(u_full, w_dw[:, None, :].astype(u.dtype), window_strides=(1,),
                                 padding='VALID', dimension_numbers=('NWC', 'WIO', 'NWC'),
                                 feature_group_count=d)
    out = (bg * y) @ w_out
    return out, u_full[:, -(CONV_WIDTH - 1):]


def band_attend(q, k, v, q_pos, k_pos, k_valid, rel_bias):
    s = jnp.einsum('bqhd,bkhd->bhqk', q, k).astype(jnp.float32) * (HEAD_DIM ** -0.5)
    rel = jnp.clip(q_pos[:, None] - k_pos[None, :], -MAX_REL, MAX_REL) + MAX_REL
    s = s + rel_bias[:, rel].astype(jnp.float32)[None]
    s = jnp.where(k_valid[None, None, None, :], s, NEG_INF)
    p = jax.nn.softmax(s, axis=-1).astype(v.dtype)
    return jnp.einsum('bhqk,bkhd->bqhd', p, v)


def band_attn_prompt(xn, w_qkv, rel_bias, w_o):
    b, s, _ = xn.shape
    qkv = (xn @ w_qkv).reshape(b, s, 3, N_HEADS, HEAD_DIM)
    q, k, v = qkv[:, :, 0], qkv[:, :, 1], qkv[:, :, 2]
    pad = ((0, 0), (PAST_WIN, 0), (0, 0), (0, 0))
    k_pad, v_pad = jnp.pad(k, pad), jnp.pad(v, pad)
    n_chunks = s // CHUNK

    def one_chunk(n):
        start = n * CHUNK
        qc = lax.dynamic_slice_in_dim(q, start, CHUNK, axis=1)
        kb = lax.dynamic_slice_in_dim(k_pad, start, BAND, axis=1)
        vb = lax.dynamic_slice_in_dim(v_pad, start, BAND, axis=1)
        q_pos = start + jnp.arange(CHUNK)
        k_pos = start - PAST_WIN + jnp.arange(BAND)
        return band_attend(qc, kb, vb, q_pos, k_pos, k_pos >= 0, rel_bias)

    o = lax.map(one_chunk, jnp.arange(n_chunks))
    o = jnp.transpose(o, (1, 0, 2, 3, 4)).reshape(b, s, D_MODEL)
    keep = min(PAST_WIN, s)
    return o @ w_o, k[:, s - keep:], v[:, s - keep:]


def band_attn_sample(xn, ck, cv, w_qkv, rel_bias, w_o):
    b, l, _ = xn.shape
    qkv = (xn @ w_qkv).reshape(b, l, 3, N_HEADS, HEAD_DIM)
    q, k, v = qkv[:, :, 0], qkv[:, :, 1], qkv[:, :, 2]
    pw = ck.shape[1]
    kk = jnp.concatenate([ck.astype(k.dtype), k], axis=1)
    vv = jnp.concatenate([cv.astype(v.dtype), v], axis=1)
    k_pos = PAST_LEN - pw + jnp.arange(pw + l)
    q_pos = PAST_LEN + jnp.arange(l)
    o = band_attend(q, kk, vv, q_pos, k_pos, jnp.ones((pw + l,), bool), rel_bias)
    return o.reshape(b, l, D_MODEL) @ w_o, k, v


def mem_project(mem, g, w_kv):
    b, m, _ = mem.shape
    kv = (rmsnorm(mem, g) @ w_kv).reshape(b, m, 2, MEM_HEADS, MEM_HEAD_DIM)
    return kv[:, :, 0], kv[:, :, 1]


def mem_attend(xn, mk, mv, w_q, w_o):
    b, l, _ = xn.shape
    q = (xn @ w_q).reshape(b, l, MEM_HEADS, MEM_HEAD_DIM)
    s = jnp.einsum('blhd,bmhd->bhlm', q, mk.astype(q.dtype)).astype(jnp.float32) * (MEM_HEAD_DIM ** -0.5)
    p = jax.nn.softmax(s, axis=-1).astype(q.dtype)
    o = jnp.einsum('bhlm,bmhd->blhd', p, mv.astype(q.dtype)).reshape(b, l, MEM_WIDTH)
    return o @ w_o


def trunk(x, conv_states, band_k, band_v, mem_k, mem_v, g_norm,
          w_ffn1_in, w_ffn1_out, w_ffn2_in, w_ffn2_out,
          w_conv_in, w_conv_dw, w_conv_out, w_attn_qkv, rel_bias, w_attn_o,
          w_mem_q, w_mem_o, prompt):
    new_conv, new_bk, new_bv = [], [], []
    for i in range(DEPTH):
        g = g_norm[i]
        h = swiglu(rmsnorm(x, g[0]), w_ffn1_in[i], w_ffn1_out[i])
        x = x + 0.5 * rmsnorm(h, g[1])
        hn = rmsnorm(x, g[2])
        j = i // N_MIXERS
        if i % N_MIXERS == 0:
            hist = None if prompt else conv_states[j]
            h, st = short_conv(hn, hist, w_conv_in[j], w_conv_dw[j], w_conv_out[j])
            new_conv.append(st)
        else:
            if prompt:
                h, nk, nv = band_attn_prompt(hn, w_attn_qkv[j], rel_bias[j], w_attn_o[j])
            else:
                h, nk, nv = band_attn_sample(hn, band_k[j], band_v[j], w_attn_qkv[j], rel_bias[j], w_attn_o[j])
            new_bk.append(nk)
            new_bv.append(nv)
        x = x + rmsnorm(h, g[3])
        h = mem_attend(rmsnorm(x, g[4]), mem_k[i], mem_v[i], w_mem_q[i], w_mem_o[i])
        x = x + rmsnorm(h, g[5])
        h = swiglu(rmsnorm(x, g[6]), w_ffn2_in[i], w_ffn2_out[i])
        x = x + 0.5 * rmsnorm(h, g[7])
    return x, jnp.stack(new_conv), jnp.stack(new_bk), jnp.stack(new_bv)


def setup_inputs(seed: int = 0) -> dict:
    key = jax.random.key(seed)
    ks = jax.random.split(key, 32)
    f32 = jnp.float32

    def nrm(k, shape, fan_in):
        return jax.random.normal(k, shape, f32) * (fan_in ** -0.5)

    pw = min(PAST_WIN, PAST_LEN)
    return {
        "x_prompt": jax.random.normal(ks[0], (BATCH, SEQ, D_MODEL), f32),
        "x_sample": jax.random.normal(ks[1], (DEC_BATCH, DEC_SEQ, D_MODEL), f32),
        "state_conv": jax.random.normal(ks[2], (N_CONV_LAYERS, DEC_BATCH, CONV_WIDTH - 1, D_MODEL), f32),
        "cache_band_k": jax.random.normal(ks[3], (N_ATTN_LAYERS, DEC_BATCH, pw, N_HEADS, HEAD_DIM), f32),
        "cache_band_v": jax.random.normal(ks[4], (N_ATTN_LAYERS, DEC_BATCH, pw, N_HEADS, HEAD_DIM), f32),
        "cache_mem_k": jax.random.normal(ks[5], (DEPTH, DEC_BATCH, N_MEM, MEM_HEADS, MEM_HEAD_DIM), f32),
        "cache_mem_v": jax.random.normal(ks[6], (DEPTH, DEC_BATCH, N_MEM, MEM_HEADS, MEM_HEAD_DIM), f32),
        "mem_prompt": jax.random.normal(ks[7], (BATCH, N_MEM, D_MODEL), f32),
        "g_norm": 1.0 + 0.02 * jax.random.normal(ks[8], (DEPTH, N_NORMS, D_MODEL), f32),
        "g_mem": 1.0 + 0.02 * jax.random.normal(ks[9], (DEPTH, D_MODEL), f32),
        "w_ffn1_in": nrm(ks[10], (DEPTH, D_MODEL, 2 * D_FF), D_MODEL),
        "w_ffn1_out": nrm(ks[11], (DEPTH, D_FF, D_MODEL), D_FF),
        "w_ffn2_in": nrm(ks[12], (DEPTH, D_MODEL, 2 * D_FF), D_MODEL),
        "w_ffn2_out": nrm(ks[13], (DEPTH, D_FF, D_MODEL), D_FF),
        "w_conv_in": nrm(ks[14], (N_CONV_LAYERS, D_MODEL, 3 * D_MODEL), D_MODEL),
        "w_conv_dw": nrm(ks[15], (N_CONV_LAYERS, CONV_WIDTH, D_MODEL), CONV_WIDTH),
        "w_conv_out": nrm(ks[16], (N_CONV_LAYERS, D_MODEL, D_MODEL), D_MODEL),
        "w_attn_qkv": nrm(ks[17], (N_ATTN_LAYERS, D_MODEL, 3 * D_MODEL), D_MODEL),
        "rel_bias": 0.5 * jax.random.normal(ks[18], (N_ATTN_LAYERS, N_HEADS, N_REL), f32),
        "w_attn_o": nrm(ks[19], (N_ATTN_LAYERS, D_MODEL, D_MODEL), D_MODEL),
        "w_mem_q": nrm(ks[20], (DEPTH, D_MODEL, MEM_WIDTH), D_MODEL),
        "w_mem_kv": nrm(ks[21], (DEPTH, D_MODEL, 2 * MEM_WIDTH), D_MODEL),
        "w_mem_o": nrm(ks[22], (DEPTH, MEM_WIDTH, D_MODEL), MEM_WIDTH),
    }


def reference(x_prompt, x_sample, state_conv, cache_band_k, cache_band_v, cache_mem_k, cache_mem_v,
              mem_prompt, g_norm, g_mem, w_ffn1_in, w_ffn1_out, w_ffn2_in, w_ffn2_out,
              w_conv_in, w_conv_dw, w_conv_out, w_attn_qkv, rel_bias, w_attn_o,
              w_mem_q, w_mem_kv, w_mem_o):
    mks, mvs = [], []
    for i in range(DEPTH):
        mk, mv = mem_project(mem_prompt, g_mem[i], w_mem_kv[i])
        mks.append(mk)
        mvs.append(mv)
    mem_k_p = jnp.stack(mks)
    mem_v_p = jnp.stack(mvs)

    y_prompt, conv_p, bk_p, bv_p = trunk(
        x_prompt, None, None, None, mem_k_p, mem_v_p, g_norm,
        w_ffn1_in, w_ffn1_out, w_ffn2_in, w_ffn2_out,
        w_conv_in, w_conv_dw, w_conv_out, w_attn_qkv, rel_bias, w_attn_o,
        w_mem_q, w_mem_o, True)

    y_sample, conv_s, bk_s, bv_s = trunk(
        x_sample, state_conv, cache_band_k, cache_band_v, cache_mem_k, cache_mem_v, g_norm,
        w_ffn1_in, w_ffn1_out, w_ffn2_in, w_ffn2_out,
        w_conv_in, w_conv_dw, w_conv_out, w_attn_qkv, rel_bias, w_attn_o,
        w_mem_q, w_mem_o, False)

    return (y_prompt, y_sample, conv_p, bk_p, bv_p, mem_k_p, mem_v_p, conv_s, bk_s, bv_s)
```

```python
import numpy as np
from contextlib import ExitStack
import concourse.bass as bass
import concourse.mybir as mybir
from concourse.bass_utils import run_bass_kernel_spmd

F32 = mybir.dt.float32
BF16 = mybir.dt.bfloat16
AF = mybir.ActivationFunctionType
ALU = mybir.AluOpType

NCORES = 8
D = 2048
DC = 16
DFF = 5632
FC = 44
SEQ = 2048
NPS = 2
NSS = 4
LS = 64
NMEM = 256
MW = 512
EPS = 1e-6
RING = 18432
NWSEM = 8
SCALE = 128 ** -0.5
NEG = -1e30


class Sched:
    ENG = ("pe", "act", "dve", "pool", "sp")

    def __init__(self):
        self.q = {k: [] for k in self.ENG}
        self.ecnt = {k: 0 for k in self.ENG}
        self.dcnt = {}
        self.seen = {k: {} for k in self.ENG}
        self.W = {}
        self.R = {}

    def op(self, eng, fn, reads=(), writes=(), dma=None):
        waits = {}
        isdma = dma is not None
        writes = list(writes) + [r for r in reads if r.startswith("ps") and r not in writes]

        def need(tok, kind):
            sk, val, peng = tok
            if not isdma and not sk.startswith("d:") and peng == eng:
                if eng == "pe":
                    return
            if self.seen[eng].get(sk, 0) >= val:
                return
            if waits.get(sk, 0) < val:
                waits[sk] = val

        for r in reads:
            t = self.W.get(r)
            if t is not None:
                need(t, "raw")
        for w in writes:
            t = self.W.get(w)
            if t is not None:
                need(t, "waw")
            for sk, (v, pe) in self.R.get(w, {}).items():
                need((sk, v, pe), "war")
        for sk, v in waits.items():
            self.seen[eng][sk] = v
        if isdma:
            self.dcnt[dma] = self.dcnt.get(dma, 0) + 1
            tok = ("d:" + dma, 16 * self.dcnt[dma], eng)
        else:
            self.ecnt[eng] += 1
            tok = ("e:" + eng, self.ecnt[eng], eng)
        for r in reads:
            self.R.setdefault(r, {})[tok[0]] = (tok[1], eng)
        for w in writes:
            self.W[w] = tok
            self.R[w] = {}
        self.q[eng].append((list(waits.items()), fn, tok))
        return tok


def rn(prefix, a, n=1):
    return [f"{prefix}{i}" for i in range(a, a + n)]


class Builder:
    def __init__(self, tiles=None, nlayers=2, stage=99):
        self.stage = stage
        import os as _os
        self.sub = int(_os.environ.get("KSUB", "99"))
        self.nc = bass.Bass("TRN2", target_bir_lowering=False)
        self.cfg_tiles = tiles
        self.nlayers = nlayers
        self.S = Sched()
        self.es = ExitStack()
        self.bank_i = 0
        self.pool_ok = False
        self.alt = 0
        self.uidx = 0
        self.ring_ptr = 0
        self.live = []
        self.unit_off = {}
        self.scr_ptr = 0
        self.tmp_i = 0
        self.sq_i = 0
        self.sg_i = 0
        self.ub_i = 0

    def dram(self):
        nc = self.nc
        I = lambda n, s: nc.dram_tensor(n, s, F32, kind="ExternalInput").ap()
        O = lambda n, s: nc.dram_tensor(n, s, F32, kind="ExternalOutput").ap()
        self.x_p = I("x_p", [NPS, SEQ, D])
        self.x_s = I("x_s", [NSS * LS, D])
        self.st_conv = I("st_conv", [NSS, 2, D])
        self.c_bk = I("c_bk", [NSS, 512, D])
        self.c_bv = I("c_bv", [NSS, 512, D])
        self.c_mk = I("c_mk", [2, NSS, NMEM, MW])
        self.c_mv = I("c_mv", [2, NSS, NMEM, MW])
        self.mem_p = I("mem_p", [NPS, NMEM, D])
        self.g_norm = I("g_norm", [2, 8, D])
        self.g_mem = I("g_mem", [2, D])
        self.w_f1i = I("w_f1i", [2, D, 2 * DFF])
        self.w_f1o = I("w_f1o", [2, DFF, D])
        self.w_f2i = I("w_f2i", [2, D, 2 * DFF])
        self.w_f2o = I("w_f2o", [2, DFF, D])
        self.w_ci = I("w_ci", [D, 3 * D])
        self.w_cdw = I("w_cdw", [3, D])
        self.w_co = I("w_co", [D, D])
        self.w_qkv = I("w_qkv", [D, 3 * D])
        self.relb = I("relb", [16, 257])
        self.w_ao = I("w_ao", [D, D])
        self.w_mq = I("w_mq", [2, D, MW])
        self.w_mkv = I("w_mkv", [2, D, 2 * MW])
        self.w_mo = I("w_mo", [2, MW, D])
        self.ident_in = I("ident_in", [128, 128])
        self.y_p = O("y_p", [NPS, SEQ, D])
        self.y_s = O("y_s", [NSS * LS, D])
        self.o_conv_p = O("o_conv_p", [NPS, 2, D])
        self.o_bk_p = O("o_bk_p", [NPS, 512, D])
        self.o_bv_p = O("o_bv_p", [NPS, 512, D])
        self.o_mk_p = O("o_mk_p", [2, NPS, NMEM, MW])
        self.o_mv_p = O("o_mv_p", [2, NPS, NMEM, MW])
        self.o_conv_s = O("o_conv_s", [NSS, 2, D])
        self.o_bk_s = O("o_bk_s", [NSS * LS, D])
        self.o_bv_s = O("o_bv_s", [NSS * LS, D])
        per_layer = 2 * (DC * 2 * DFF + DFF * D) + (DC * 3 * D + D * D // 8) * 0
        tot = 2 * (2 * (D * 2 * DFF + DFF * D) + D * MW + D * 2 * MW + MW * D) + 2 * (D * 3 * D + D * D)
        self.SCR_EL = 60 * 1024 * 1024
        self.wscr = [nc.dram_tensor(f"wscr{i}", [self.SCR_EL // 128, 128], BF16, kind="Internal") for i in range(4)]
        self.scr_t = 0
        self.vp_d = nc.dram_tensor("vp_d", [128, 16 * 768], F32, kind="Internal")
        self.wb_d = nc.dram_tensor("wb_d", [16, 128, 640], F32, kind="Internal").ap()

    def sb(self, name, shape, dt):
        return self.es.enter_context(self.nc.sbuf_tensor(name, shape, dt))

    def sbuf(self):
        self.xres = self.sb("xres", [128, DC, 512], F32)
        self.xn = self.sb("xn", [128, DC, 512], BF16)
        self.H = self.sb("H", [128, 48, 512], BF16)
        self.kTp = self.sb("kTp", [128, 16, 512], BF16)
        self.Vp = self.sb("Vp", [128, 4, 2048], BF16)
        self.ring = self.sb("ring", [128, RING], BF16)
        self.WBt = self.sb("WBt", [128, 640], F32)
        self.PT = self.sb("PT", [128, 2560], BF16)
        self.sct = self.sb("sct", [128, 640], F32)
        self.sq = self.sb("sq", [128, 2, 512], BF16)
        self.rs = self.sb("rs", [128, 2, 512], F32)
        self.tmpf = self.sb("tmpf", [128, 3, 512], F32)
        self.sg = self.sb("sg", [128, 2, 512], BF16)
        self.CA = self.sb("CA", [128, 1568], F32)
        self.ub = self.CA[:, 0:1056].rearrange("p (a b) -> p a b", a=2)
        self.yt = self.CA[:, 1056:1568]
        self.PT2 = self.CA[:, 0:1280].bitcast(BF16)
        self.uh = self.sb("uh", [128, DC, 4, 2], F32)
        self.memKT = self.sb("memKT", [128, 2, 4, 256], BF16)
        self.memV = self.sb("memV", [128, 2, 2, 512], BF16)
        self.G = self.sb("G", [128, 16, DC], F32)
        self.GM = self.sb("GM", [128, 2, DC], F32)
        self.CW = self.sb("CW", [128, 3, DC], F32)
        self.idf = self.sb("idf", [128, 128], F32)
        self.idb = self.sb("idb", [128, 128], BF16)
        self.ones = self.sb("ones", [128, 128], BF16)
        self.eps = self.sb("eps", [128, 1], F32)
        self.ss = self.sb("ss", [128, 8], F32)
        self.ps = self.es.enter_context(self.nc.psum_tensor("ps", [128, 8, 512], F32))

    def bank(self):
        b = self.bank_i
        self.bank_i = (b + 1) % 7
        return b

    def ve(self):
        if not self.pool_ok:
            return "dve"
        self.alt ^= 1
        return "pool" if self.alt else "dve"

    def join(self, eng, res):
        self.S.op(eng, lambda e: e.nop(), reads=list(res), writes=list(res))

    def Hf(self, a, n):
        return self.H[:, a:a + n, :].rearrange("p a b -> p (a b)").bitcast(F32)

    def Hflat(self, a, n):
        return self.H[:, a:a + n, :].rearrange("p a b -> p (a b)")

    def unit(self, key, kc, nw, parts):
        S = self.S
        U = kc * nw
        if self.ring_ptr + U > RING:
            self.ring_ptr = 0
        a, b = self.ring_ptr, self.ring_ptr + U
        self.ring_ptr = b
        over = [l for l in self.live if l[0] < b and l[1] > a]
        self.live = [l for l in self.live if not (l[0] < b and l[1] > a)]
        ui = self.uidx
        self.uidx += 1
        res = f"wu{ui}"
        oldres = [r for l in over for r in l[2]]
        view = self.ring[:, a:b].rearrange("p (k n) -> p k n", k=kc)
        first = key not in self.unit_off
        if first:
            if self.scr_ptr + U * 128 > self.SCR_EL:
                self.scr_t += 1
                self.scr_ptr = 0
            off = (self.scr_t, self.scr_ptr)
            self.unit_off[key] = off
            self.scr_ptr += U * 128
            myres = []
            for pi, (src, co) in enumerate(parts):
                ncol = src.shape[1]
                r = f"{res}p{pi}"
                myres.append(r)
                dst = view[:, :, co:co + ncol]
                srcv = src.rearrange("(k p) n -> p k n", p=128)
                S.op("pool", (lambda e, dst=dst, srcv=srcv: e.dma_start(out=dst, in_=srcv)),
                     writes=[r] + (oldres if pi == 0 else []), dma=f"pw{ui % NWSEM}")
            scr = bass.AP(self.wscr[off[0]], off[1], [[U, 128], [1, U]])
            flat = self.ring[:, a:b]
            S.op("sp", (lambda e, scr=scr, flat=flat: e.dma_start(out=scr, in_=flat)),
                 reads=myres, writes=[f"scr{key}"], dma=f"s{ui % NWSEM}")
        else:
            off = self.unit_off[key]
            scr = bass.AP(self.wscr[off[0]], off[1], [[U, 128], [1, U]])
            flat = self.ring[:, a:b]
            myres = [res]
            S.op("sp", (lambda e, scr=scr, flat=flat: e.dma_start(out=flat, in_=scr)),
                 reads=[f"scr{key}"], writes=[res] + oldres, dma=f"w{ui % NWSEM}")
        self.live.append((a, b, myres))
        return view, myres

    def mm_group(self, out_ap, pairs, reads, writes):
        n = len(pairs)

        def fn(e, out_ap=out_ap, pairs=pairs, n=n):
            ins = None
            for i, (l, r) in enumerate(pairs):
                ins = e.matmul(out_ap, lhsT=l, rhs=r, start=(i == 0), stop=(i == n - 1))
            return ins
        return self.S.op("pe", fn, reads=reads, writes=writes)

    def sums_to_rstd(self, bank, nt, scale_mean):
        S = self.S
        ps, rs, eps = self.ps, self.rs, self.eps
        S.op("act", lambda e: e.activation(out=rs[:, 0, 0:nt], in_=ps[:, bank, 0:nt], func=AF.Sqrt,
                                           scale=scale_mean, bias=eps[:, 0:1]),
             reads=[f"ps{bank}"], writes=["rs0"])
        S.op("dve", lambda e: e.reciprocal(out=rs[:, 1, 0:nt], in_=rs[:, 0, 0:nt]), reads=["rs0"], writes=["rs1"])

    def prenorm(self, gi, nt):
        S = self.S
        xres, xn, sq, ones, ps, G, rs = self.xres, self.xn, self.sq, self.ones, self.ps, self.G, self.rs
        bank = 7
        for c in range(DC):
            si = self.sq_i
            self.sq_i ^= 1
            S.op("act", (lambda e, c=c, si=si: e.activation(out=sq[:, si, 0:nt], in_=xres[:, c, 0:nt], func=AF.Square)),
                 reads=[f"xres{c}"], writes=[f"sq{si}"])
            S.op("pe", (lambda e, c=c, si=si: e.matmul(ps[:, bank, 0:nt], lhsT=ones[:], rhs=sq[:, si, 0:nt],
                                                      start=(c == 0), stop=(c == DC - 1))),
                 reads=[f"sq{si}"], writes=[f"ps{bank}"])
        self.sums_to_rstd(bank, nt, 1.0 / D)
        for c in range(DC):
            eng = "dve"
            S.op(eng, (lambda e, c=c: e.scalar_tensor_tensor(out=xn[:, c, 0:nt], in0=xres[:, c, 0:nt],
                                                            scalar=G[:, gi, c:c + 1], in1=rs[:, 1, 0:nt],
                                                            op0=ALU.mult, op1=ALU.mult)),
                 reads=[f"xres{c}", "rs1"], writes=[f"xn{c}"])

    def post_begin(self, gi):
        self.post_bank = 7
        self.post_cnt = 0
        self.post_gi = gi
        self.post_pending = None

    def post_flush(self):
        if self.post_pending is not None:
            fn, si = self.post_pending
            self.S.op("pe", fn, reads=[f"sq{si}"], writes=[f"ps{self.post_bank}"])
            self.post_pending = None

    def post_chunk(self, j, bank, nt):
        S = self.S
        ps, xn, sq, ones = self.ps, self.xn, self.sq, self.ones
        pb = self.post_bank
        gi = self.post_gi
        self.post_flush()
        S.op("act", (lambda e: e.activation(out=xn[:, j, 0:nt], in_=ps[:, bank, 0:nt], func=AF.Copy, scale=self.G[:, gi, j:j + 1])),
             reads=[f"ps{bank}"], writes=[f"xn{j}"])
        si = self.sq_i
        self.sq_i ^= 1
        S.op("act", (lambda e: e.activation(out=sq[:, si, 0:nt], in_=ps[:, bank, 0:nt], func=AF.Square)),
             reads=[f"ps{bank}"], writes=[f"sq{si}"])
        first = self.post_cnt == 0
        last = self.post_cnt == DC - 1
        self.post_cnt += 1
        self.post_pending = ((lambda e: e.matmul(ps[:, pb, 0:nt], lhsT=ones[:], rhs=sq[:, si, 0:nt], start=first, stop=last)), si)

    def post_end(self, gi, nt):
        S = self.S
        xres, xn, G, rs, tmpf = self.xres, self.xn, self.G, self.rs, self.tmpf
        assert gi == self.post_gi
        self.post_flush()
        self.sums_to_rstd(self.post_bank, nt, 1.0 / D)
        for c in range(DC):
            eng = self.ve()
            ti = self.tmp_i
            self.tmp_i = (ti + 1) % 3
            S.op("dve", (lambda e, c=c, ti=ti: e.tensor_tensor(out=tmpf[:, ti, 0:nt], in0=xn[:, c, 0:nt], in1=rs[:, 1, 0:nt], op=ALU.mult)),
                 reads=[f"xn{c}", "rs1"], writes=[f"tmpf{ti}"])
            eng = "pool" if (self.pool_ok and c % 4 == 3) else "dve"
            S.op(eng, (lambda e, c=c, ti=ti: e.tensor_tensor(out=xres[:, c, 0:nt], in0=xres[:, c, 0:nt],
                                                            in1=tmpf[:, ti, 0:nt], op=ALU.add)),
                 reads=[f"xres{c}", f"tmpf{ti}"], writes=[f"xres{c}"])

    def ffn(self, l, which, nt):
        S = self.S
        w_in = (self.w_f1i if which == 1 else self.w_f2i)[l]
        w_out = (self.w_f1o if which == 1 else self.w_f2o)[l]
        gpre, gpost = (0, 1) if which == 1 else (6, 7)
        ps, xn, H, sg = self.ps, self.xn, self.H, self.sg
        self.prenorm(l * 8 + gpre, nt)
        for c in range(FC):
            u, ures = self.unit(f"f{which}i{l}_{c}", DC, 256,
                                [(w_in[:, c * 128:(c + 1) * 128], 0), (w_in[:, DFF + c * 128:DFF + (c + 1) * 128], 128)])
            bg, bu = self.bank(), self.bank()
            self.mm_group(ps[:, bg, 0:nt], [(u[:, k, 0:128], xn[:, k, 0:nt]) for k in range(DC)],
                          reads=ures + rn("xn", 0, DC), writes=[f"ps{bg}"])
            self.mm_group(ps[:, bu, 0:nt], [(u[:, k, 128:256], xn[:, k, 0:nt]) for k in range(DC)],
                          reads=ures + rn("xn", 0, DC), writes=[f"ps{bu}"])
            si = self.sg_i
            self.sg_i ^= 1
            S.op("act", (lambda e, bg=bg, si=si: e.activation(out=sg[:, si, 0:nt], in_=ps[:, bg, 0:nt], func=AF.Silu)),
                 reads=[f"ps{bg}"], writes=[f"sg{si}"])
            S.op("dve", (lambda e, bu=bu, si=si, c=c: e.tensor_tensor(out=H[:, c, 0:nt], in0=sg[:, si, 0:nt],
                                                                     in1=ps[:, bu, 0:nt], op=ALU.mult)),
                 reads=[f"ps{bu}", f"sg{si}"], writes=[f"H{c}"])
        self.post_begin(l * 8 + gpost)
        for j in range(DC):
            u, ures = self.unit(f"f{which}o{l}_{j}", FC, 128, [(w_out[:, j * 128:(j + 1) * 128], 0)])
            b = self.bank()
            self.mm_group(ps[:, b, 0:nt], [(u[:, k, :], H[:, k, 0:nt]) for k in range(FC)],
                          reads=ures + rn("H", 0, FC), writes=[f"ps{b}"])
            self.post_chunk(j, b, nt)
        self.post_end(l * 8 + gpost, nt)

    def proj_out(self, key, w, kchunks, hslot0, gi, nt):
        ps, H = self.ps, self.H
        self.post_begin(gi)
        for ub_ in range(8):
            u, ures = self.unit(f"{key}_{ub_}", kchunks, 256, [(w[:, ub_ * 256:(ub_ + 1) * 256], 0)])
            for jj in range(2):
                j = ub_ * 2 + jj
                b = self.bank()
                self.mm_group(ps[:, b, 0:nt], [(u[:, k, jj * 128:(jj + 1) * 128], H[:, hslot0 + k, 0:nt]) for k in range(kchunks)],
                              reads=ures + rn("H", hslot0, kchunks), writes=[f"ps{b}"])
                self.post_chunk(j, b, nt)
        self.post_end(gi, nt)

    def conv(self, tile):
        S = self.S
        nt, nseg = tile["nt"], tile["nseg"]
        L = nt // nseg
        ps, xn, H, ub, uh, CW, tmpf, yt = self.ps, self.xn, self.H, self.ub, self.uh, self.CW, self.tmpf, self.yt
        self.prenorm(2, nt)
        w = self.w_ci
        for c in range(DC):
            u, ures = self.unit(f"ci_{c}", DC, 384, [(w[:, c * 128:(c + 1) * 128], 0),
                                                    (w[:, D + c * 128:D + (c + 1) * 128], 128),
                                                    (w[:, 2 * D + c * 128:2 * D + (c + 1) * 128], 256)])
            bb, bc_, bx = self.bank(), self.bank(), self.bank()
            for bnk, co in ((bb, 0), (bc_, 128), (bx, 256)):
                self.mm_group(ps[:, bnk, 0:nt], [(u[:, k, co:co + 128], xn[:, k, 0:nt]) for k in range(DC)],
                              reads=ures + rn("xn", 0, DC), writes=[f"ps{bnk}"])
            ui = self.ub_i
            self.ub_i ^= 1
            uv = ub[:, ui, 0:nseg * (L + 2)].rearrange("p (s l) -> p s l", s=nseg)
            S.op("dve", (lambda e, uv=uv, c=c: e.tensor_copy(out=uv[:, :, 0:2], in_=uh[:, c, 0:nseg, :])),
                 reads=[f"uh{c}"], writes=[f"ub{ui}h"])
            ti = self.tmp_i
            self.tmp_i = (ti + 1) % 3
            S.op("act", (lambda e, bc_=bc_, ti=ti: e.activation(out=tmpf[:, ti, 0:nt], in_=ps[:, bc_, 0:nt], func=AF.Copy)),
                 reads=[f"ps{bc_}"], writes=[f"tmpf{ti}"])
            t3 = tmpf[:, ti, 0:nt].rearrange("p (s l) -> p s l", s=nseg)
            px = ps[:, bx, 0:nt].rearrange("p (s l) -> p s l", s=nseg)
            S.op("dve", (lambda e, uv=uv, t3=t3, px=px: e.tensor_tensor(out=uv[:, :, 2:L + 2], in0=t3, in1=px, op=ALU.mult)),
                 reads=[f"tmpf{ti}", f"ps{bx}"], writes=[f"ub{ui}"])
            y3 = yt[:, 0:nt].rearrange("p (s l) -> p s l", s=nseg)
            eng = "dve"
            S.op(eng, (lambda e, uv=uv, y3=y3, c=c: e.tensor_scalar(out=y3, in0=uv[:, :, 0:L], scalar1=CW[:, 0, c:c + 1],
                                                                   scalar2=0.0, op0=ALU.mult, op1=ALU.add)),
                 reads=[f"ub{ui}", f"ub{ui}h"], writes=["yt"])
            S.op(eng, (lambda e, uv=uv, y3=y3, c=c: e.scalar_tensor_tensor(out=y3, in0=uv[:, :, 1:L + 1], scalar=CW[:, 1, c:c + 1],
                                                                          in1=y3, op0=ALU.mult, op1=ALU.add)),
                 reads=[f"ub{ui}", f"ub{ui}h", "yt"], writes=["yt"])
            S.op(eng, (lambda e, uv=uv, y3=y3, c=c: e.scalar_tensor_tensor(out=y3, in0=uv[:, :, 2:L + 2], scalar=CW[:, 2, c:c + 1],
                                                                          in1=y3, op0=ALU.mult, op1=ALU.add)),
                 reads=[f"ub{ui}", "yt"], writes=["yt"])
            S.op("dve", (lambda e, bb=bb, c=c: e.tensor_tensor(out=H[:, c, 0:nt], in0=yt[:, 0:nt], in1=ps[:, bb, 0:nt], op=ALU.mult)),
                 reads=["yt", f"ps{bb}"], writes=[f"H{c}"])
            S.op(self.ve(), (lambda e, uv=uv, c=c: e.tensor_copy(out=uh[:, c, 0:nseg, :], in_=uv[:, :, L:L + 2])),
                 reads=[f"ub{ui}"], writes=[f"uh{c}"])
        if tile["kind"] == "s" or tile["t"] == 3:
            for sgi in range(nseg):
                dst = (self.o_conv_s[sgi] if tile["kind"] == "s" else self.o_conv_p[tile["b"]])
                dstv = dst.rearrange("t (c p) -> p c t", p=128)
                for c in range(DC):
                    S.op("act", (lambda e, dstv=dstv, c=c, sgi=sgi: e.dma_start(out=dstv[:, c, :], in_=uh[:, c, sgi, :],
                                                                               allow_slow_non_contiguous=True)),
                         reads=[f"uh{c}"], writes=[], dma="oc")
        self.proj_out("co", self.w_co, DC, 0, 3, nt)

    def attn(self, tile):
        S = self.S
        nt, nseg, kind = tile["nt"], tile["nseg"], tile["kind"]
        ps, xn, H, PT, sct, WBt, kTp, Vp, ones, tmpf = self.ps, self.xn, self.H, self.PT, self.sct, self.WBt, self.kTp, self.Vp, self.ones, self.tmpf
        self.prenorm(8 + 2, nt)
        w = self.w_qkv
        want_out = kind == "s" or tile["t"] == 3
        nsub = nt // 128 if kind == "p" else nseg
        rows = 128 if kind == "p" else LS

        def qslot(h):
            return H[:, h, 0:nt]

        def kslot(h):
            return H[:, 16 + h, 0:nt] if kind == "p" else H[:, h, 256:256 + nt]

        def kres(h):
            return f"H{16 + h}" if kind == "p" else f"H{h}"

        def qres(h):
            return f"H{h}"

        def vcur(s):
            return self.Hflat(32 + 4 * s, 4)[0:rows, :]

        for which, slotf, resf, base in (("q", qslot, qres, 0), ("k", kslot, kres, D)):
            for ub_ in range(8):
                u, ures = self.unit(f"qkv{which}_{ub_}", DC, 256, [(w[:, base + ub_ * 256:base + (ub_ + 1) * 256], 0)])
                for jj in range(2):
                    h = ub_ * 2 + jj
                    b = self.bank()
                    self.mm_group(ps[:, b, 0:nt], [(u[:, k, jj * 128:(jj + 1) * 128], xn[:, k, 0:nt]) for k in range(DC)],
                                  reads=ures + rn("xn", 0, DC), writes=[f"ps{b}"])
                    dst = slotf(h)
                    S.op("act", (lambda e, dst=dst, b=b: e.activation(out=dst, in_=ps[:, b, 0:nt], func=AF.Copy)),
                         reads=[f"ps{b}"], writes=[resf(h)])
                if which == "k" and want_out:
                    for s in range(nsub):
                        b = self.bank()
                        self.mm_group(ps[0:rows, b, 0:256], [(xn[:, k, s * rows:(s + 1) * rows], u[:, k, :]) for k in range(DC)],
                                      reads=ures + rn("xn", 0, DC), writes=[f"ps{b}"])
                        ti = self.tmp_i
                        self.tmp_i = (ti + 1) % 3
                        S.op("act", (lambda e, b=b, ti=ti: e.activation(out=tmpf[0:rows, ti, 0:256], in_=ps[0:rows, b, 0:256], func=AF.Copy)),
                             reads=[f"ps{b}"], writes=[f"tmpf{ti}"])
                        if kind == "p":
                            dst = self.o_bk_p[tile["b"], s * 128:(s + 1) * 128, ub_ * 256:(ub_ + 1) * 256]
                        else:
                            dst = self.o_bk_s[s * LS:(s + 1) * LS, ub_ * 256:(ub_ + 1) * 256]
                        S.op("act", (lambda e, dst=dst, ti=ti: e.dma_start(out=dst, in_=tmpf[0:rows, ti, 0:256])),
                             reads=[f"tmpf{ti}"], writes=[], dma=f"ot{ti}")
        for ub_ in range(8):
            u, ures = self.unit(f"qkvv_{ub_}", DC, 256, [(w[:, 2 * D + ub_ * 256:2 * D + (ub_ + 1) * 256], 0)])
            for s in range(nsub):
                b = self.bank()
                self.mm_group(ps[0:rows, b, 0:256], [(xn[:, k, s * rows:(s + 1) * rows], u[:, k, :]) for k in range(DC)],
                              reads=ures + rn("xn", 0, DC), writes=[f"ps{b}"])
                vdst = vcur(s)[:, ub_ * 256:(ub_ + 1) * 256]
                S.op("dve", (lambda e, vdst=vdst, b=b: e.tensor_copy(out=vdst, in_=ps[0:rows, b, 0:256])),
                     reads=[f"ps{b}"], writes=[f"Vc{s}_{ub_}"])
                if want_out:
                    ti = self.tmp_i
                    self.tmp_i = (ti + 1) % 3
                    S.op("act", (lambda e, b=b, ti=ti: e.activation(out=tmpf[0:rows, ti, 0:256], in_=ps[0:rows, b, 0:256], func=AF.Copy)),
                         reads=[f"ps{b}"], writes=[f"tmpf{ti}"])
                    if kind == "p":
                        dst = self.o_bv_p[tile["b"], s * 128:(s + 1) * 128, ub_ * 256:(ub_ + 1) * 256]
                    else:
                        dst = self.o_bv_s[s * LS:(s + 1) * LS, ub_ * 256:(ub_ + 1) * 256]
                    S.op("act", (lambda e, dst=dst, ti=ti: e.dma_start(out=dst, in_=tmpf[0:rows, ti, 0:256])),
                         reads=[f"tmpf{ti}"], writes=[], dma=f"ot{ti}")
        vres_all = lambda s: [f"Vc{s}_{i}" for i in range(8)]

        PTs = [PT, self.PT2]

        def core_A(h, q_ap, qr, items, buf):
            PTb = PTs[buf]
            S.op("sp", (lambda e, h=h: e.dma_start(out=WBt[:], in_=self.wb_d[h])), reads=["wbd"], writes=["WBt"], dma="wb")
            off = 0
            pts = []
            for (kT, kr, V, vr, wc, q0, qn, nk) in items:
                b = self.bank()
                S.op("pe", (lambda e, b=b, kT=kT, q0=q0, qn=qn, nk=nk: e.matmul(ps[0:nk, b, 0:qn], lhsT=kT, rhs=q_ap[:, q0:q0 + qn],
                                                                              start=True, stop=True)),
                     reads=kr + qr, writes=[f"ps{b}"])
                S.op("dve", (lambda e, b=b, wc=wc, qn=qn, nk=nk: e.scalar_tensor_tensor(out=sct[0:nk, 0:qn], in0=ps[0:nk, b, 0:qn], scalar=SCALE,
                                                                                        in1=WBt[0:nk, wc:wc + qn], op0=ALU.mult, op1=ALU.add)),
                     reads=[f"ps{b}", "WBt"], writes=["sct"])
                pt = PTb[0:nk, off:off + qn]
                pres = [f"PT{buf}b{k}" for k in range(off // 128, (off + qn - 1) // 128 + 1)]
                S.op("act", (lambda e, pt=pt, qn=qn, nk=nk: e.activation(out=pt, in_=sct[0:nk, 0:qn], func=AF.Exp)),
                     reads=["sct"], writes=pres)
                pts.append((pt, pres))
                off += qn
            return (h, items, pts)

        def core_B(state, nq, out_ap, out_res):
            h, items, pts = state
            bo, bd = self.bank(), self.bank()
            n = len(items)

            def pv(e):
                ins = None
                for i, (kT, kr, V, vr, wc, q0, qn, nk) in enumerate(items):
                    ins = e.matmul(ps[:, bo, q0:q0 + qn], lhsT=V, rhs=pts[i][0], start=(i == 0), stop=(i == n - 1))
                return ins

            def dn(e):
                ins = None
                for i, (kT, kr, V, vr, wc, q0, qn, nk) in enumerate(items):
                    ins = e.matmul(ps[:, bd, q0:q0 + qn], lhsT=ones[0:nk, :], rhs=pts[i][0], start=(i == 0), stop=(i == n - 1))
                return ins
            allv = [r for it in items for r in it[3]]
            allp = sorted(set(r for p in pts for r in p[1]))
            S.op("pe", pv, reads=allv + allp, writes=[f"ps{bo}"])
            S.op("pe", dn, reads=allp, writes=[f"ps{bd}"])
            ti = self.tmp_i
            self.tmp_i = (ti + 1) % 3
            S.op("dve", (lambda e, bd=bd, ti=ti: e.reciprocal(out=tmpf[:, ti, 0:nq], in_=ps[:, bd, 0:nq])), reads=[f"ps{bd}"], writes=[f"tmpf{ti}"])
            S.op("dve", (lambda e, bo=bo, ti=ti: e.tensor_tensor(out=out_ap, in0=ps[:, bo, 0:nq], in1=tmpf[:, ti, 0:nq], op=ALU.mult)),
                 reads=[f"ps{bo}", f"tmpf{ti}"], writes=[out_res])

        if kind == "p":
            t = tile["t"]
            pend = None
            for h in range(16):
                items = []
                order = [("c", 0)] + ([("p", i) for i in range(4)] if t > 0 else []) + [("c", i) for i in range(1, 4)]
                for (src, i) in order:
                    if src == "c":
                        kT = H[:, 16 + h, i * 128:(i + 1) * 128]
                        V = self.Hflat(32 + 4 * i, 4)[:, h * 128:(h + 1) * 128]
                        items.append((kT, [kres(h)], V, vres_all(i), 0, i * 128, (4 - i) * 128, 128))
                    else:
                        kT = kTp[:, h, i * 128:(i + 1) * 128]
                        V = Vp[:, i, h * 128:(h + 1) * 128]
                        items.append((kT, ["kTp"], V, ["Vp"], (4 - i) * 128, 0, (i + 1) * 128, 128))
                st = core_A(h, H[:, h, 0:nt], [qres(h)], items, h % 2)
                if pend is not None:
                    core_B(pend, nt, H[:, pend[0], 0:nt], qres(pend[0]))
                pend = st
            core_B(pend, nt, H[:, pend[0], 0:nt], qres(pend[0]))
            if t < 3:
                for h in range(16):
                    S.op(self.ve(), (lambda e, h=h: e.tensor_copy(out=kTp[:, h, :], in_=H[:, 16 + h, :])),
                         reads=[kres(h)], writes=["kTp"])
                for i in range(4):
                    S.op(self.ve(), (lambda e, i=i: e.tensor_copy(out=Vp[:, i, :], in_=self.Hflat(32 + 4 * i, 4))),
                         reads=vres_all(i), writes=["Vp"])
        else:
            for s in range(nseg):
                kst = self.Hflat(16, 16).rearrange("p (i d) -> p i d", i=4)
                S.op("pool", (lambda e, s=s, kst=kst: e.dma_start(out=kst, in_=self.c_bk[s].rearrange("(i p) d -> p i d", p=128))),
                     reads=[], writes=rn("H", 16, 16), dma="ck")
                S.op("pool", (lambda e, s=s: e.dma_start(out=Vp[:], in_=self.c_bv[s].rearrange("(i p) d -> p i d", p=128))),
                     reads=[], writes=["Vp"], dma="cv")
                cnt = 0
                for i in range(4):
                    for h4 in range(4):
                        b = self.bank()
                        pbf = ps[:, b, 0:256].bitcast(BF16)

                        def tr(e, i=i, h4=h4, pbf=pbf, kst=kst):
                            ins = None
                            for hh in range(4):
                                h = h4 * 4 + hh
                                ins = e.transpose(pbf[:, hh * 128:(hh + 1) * 128], kst[:, i, h * 128:(h + 1) * 128], self.idb[:])
                            return ins
                        S.op("pe", tr, reads=rn("H", 16, 16), writes=[f"ps{b}"])
                        eng = "act" if cnt % 2 else "dve"
                        cnt += 1
                        dst = kTp[:, h4 * 4:h4 * 4 + 4, i * 128:(i + 1) * 128]
                        src = pbf.rearrange("p (h k) -> p h k", h=4)
                        if eng == "act":
                            S.op("act", (lambda e, dst=dst, src=src: e.activation(out=dst, in_=src, func=AF.Copy)),
                                 reads=[f"ps{b}"], writes=["kTp"])
                        else:
                            S.op("dve", (lambda e, dst=dst, src=src: e.tensor_copy(out=dst, in_=src)),
                                 reads=[f"ps{b}"], writes=["kTp"])
                pend = None
                for h in range(16):
                    items = []
                    for i in range(4):
                        items.append((kTp[:, h, i * 128:(i + 1) * 128], ["kTp"], Vp[:, i, h * 128:(h + 1) * 128], ["Vp"],
                                      (4 - i) * 128, 0, LS, 128))
                    kT = H[:, h, 256 + s * LS:256 + (s + 1) * LS]
                    V = vcur(s)[:, h * 128:(h + 1) * 128]
                    items.append((kT, [kres(h)], V, vres_all(s), 0, 0, LS, LS))
                    st = core_A(h, H[:, h, s * LS:(s + 1) * LS], [qres(h)], items, h % 2)
                    if pend is not None:
                        core_B(pend, LS, H[:, pend[0], s * LS:(s + 1) * LS], qres(pend[0]))
                    pend = st
                core_B(pend, LS, H[:, pend[0], s * LS:(s + 1) * LS], qres(pend[0]))
        self.proj_out("ao", self.w_ao, DC, 0, 8 + 3, nt)

    def mem_attn(self, l, tile):
        S = self.S
        nt, nseg, kind = tile["nt"], tile["nseg"], tile["kind"]
        ps, xn, H, PT, sct, ones, memKT, memV = self.ps, self.xn, self.H, self.PT, self.sct, self.ones, self.memKT, self.memV
        self.prenorm(l * 8 + 4, nt)
        w = self.w_mq[l]
        for ub_ in range(2):
            u, ures = self.unit(f"mq{l}_{ub_}", DC, 256, [(w[:, ub_ * 256:(ub_ + 1) * 256], 0)])
            for jj in range(2):
                hm = ub_ * 2 + jj
                b = self.bank()
                self.mm_group(ps[:, b, 0:nt], [(u[:, k, jj * 128:(jj + 1) * 128], xn[:, k, 0:nt]) for k in range(DC)],
                              reads=ures + rn("xn", 0, DC), writes=[f"ps{b}"])
                S.op("act", (lambda e, hm=hm, b=b: e.activation(out=H[:, hm, 0:nt], in_=ps[:, b, 0:nt], func=AF.Copy)),
                     reads=[f"ps{b}"], writes=[f"H{hm}"])
        L = nt // nseg
        for s in range(nseg):
            if kind == "s":
                self.load_mem_cache(l, s)
            pend = None
            for hm in range(4):
                q = H[:, hm, s * L:(s + 1) * L]
                buf = hm % 2
                PTb = [PT, self.PT2][buf]
                pts = []
                for mt in range(2):
                    b = self.bank()
                    S.op("pe", (lambda e, b=b, mt=mt, hm=hm, q=q: e.matmul(ps[:, b, 0:L], lhsT=memKT[:, l, hm, mt * 128:(mt + 1) * 128], rhs=q,
                                                                       start=True, stop=True)),
                         reads=[f"mKT{l}", f"H{hm}"], writes=[f"ps{b}"])
                    pt = PTb[:, mt * 512:mt * 512 + L]
                    S.op("act", (lambda e, b=b, pt=pt: e.activation(out=pt, in_=ps[:, b, 0:L], func=AF.Exp, scale=SCALE)),
                         reads=[f"ps{b}"], writes=rn(f"PT{buf}b", mt * 4, 4))
                    pts.append(pt)
                st = (hm, pts, buf)
                if pend is not None:
                    self.mem_B(l, pend, s, L)
                pend = st
            self.mem_B(l, pend, s, L)
        wo = self.w_mo[l]
        self.post_begin(l * 8 + 5)
        for ub_ in range(2):
            u, ures = self.unit(f"mo{l}_{ub_}", 4, 1024, [(wo[:, ub_ * 1024:(ub_ + 1) * 1024], 0)])
            for jj in range(8):
                j = ub_ * 8 + jj
                b = self.bank()
                self.mm_group(ps[:, b, 0:nt], [(u[:, k, jj * 128:(jj + 1) * 128], H[:, 4 + k, 0:nt]) for k in range(4)],
                              reads=ures + rn("H", 4, 4), writes=[f"ps{b}"])
                self.post_chunk(j, b, nt)
        self.post_end(l * 8 + 5, nt)

    def mem_B(self, l, st, s, L):
        S = self.S
        ps, H, ones, memV, tmpf = self.ps, self.H, self.ones, self.memV, self.tmpf
        hm, pts, buf = st
        bo, bd = self.bank(), self.bank()

        def pv(e):
            e.matmul(ps[:, bo, 0:L], lhsT=memV[:, l, 0, hm * 128:(hm + 1) * 128], rhs=pts[0], start=True, stop=False)
            return e.matmul(ps[:, bo, 0:L], lhsT=memV[:, l, 1, hm * 128:(hm + 1) * 128], rhs=pts[1], start=False, stop=True)

        def dn(e):
            e.matmul(ps[:, bd, 0:L], lhsT=ones[:], rhs=pts[0], start=True, stop=False)
            return e.matmul(ps[:, bd, 0:L], lhsT=ones[:], rhs=pts[1], start=False, stop=True)
        S.op("pe", pv, reads=[f"mV{l}"] + rn(f"PT{buf}b", 0, 8), writes=[f"ps{bo}"])
        S.op("pe", dn, reads=rn(f"PT{buf}b", 0, 8), writes=[f"ps{bd}"])
        ti = self.tmp_i
        self.tmp_i = (ti + 1) % 3
        S.op("dve", (lambda e: e.reciprocal(out=tmpf[:, ti, 0:L], in_=ps[:, bd, 0:L])), reads=[f"ps{bd}"], writes=[f"tmpf{ti}"])
        S.op("dve", (lambda e: e.tensor_tensor(out=H[:, 4 + hm, s * L:(s + 1) * L], in0=ps[:, bo, 0:L],
                                                in1=tmpf[:, ti, 0:L], op=ALU.mult)),
             reads=[f"ps{bo}", f"tmpf{ti}"], writes=[f"H{4 + hm}"])

    def kT_from_tokmajor(self, l, src_bf, src_res):
        S = self.S
        ps, memKT = self.ps, self.memKT
        for mt in range(2):
            b = self.bank()
            pbf = ps[:, b, 0:256].bitcast(BF16)

            def tr(e, mt=mt, pbf=pbf):
                ins = None
                for hm in range(4):
                    ins = e.transpose(pbf[:, hm * 128:(hm + 1) * 128], src_bf[:, mt, hm * 128:(hm + 1) * 128], self.idb[:])
                return ins
            S.op("pe", tr, reads=src_res, writes=[f"ps{b}"])
            S.op("dve", (lambda e, mt=mt, pbf=pbf: e.tensor_copy(out=memKT[:, l, :, mt * 128:(mt + 1) * 128],
                                                                in_=pbf.rearrange("p (h k) -> p h k", h=4))),
                 reads=[f"ps{b}"], writes=[f"mKT{l}"])

    def load_mem_cache(self, l, s):
        S = self.S
        kst = self.Hflat(8, 2).rearrange("p (m d) -> p m d", m=2)
        S.op("pool", (lambda e: e.dma_start(out=kst, in_=self.c_mk[l, s].rearrange("(m p) d -> p m d", p=128))),
             reads=[], writes=rn("H", 8, 2), dma="mk")
        S.op("pool", (lambda e: e.dma_start(out=self.memV[:, l], in_=self.c_mv[l, s].rearrange("(m p) d -> p m d", p=128))),
             reads=[], writes=[f"mV{l}"], dma="mv")
        self.kT_from_tokmajor(l, kst, rn("H", 8, 2))

    def mem_project(self, b_):
        S = self.S
        ps, H, ss, GM, tmpf = self.ps, self.H, self.ss, self.GM, self.tmpf
        xs = self.Hf(0, 16).rearrange("p (s d) -> p s d", s=2)
        junk = self.Hflat(16, 4)
        mn = self.Hflat(20, 8).rearrange("p (s d) -> p s d", s=2)
        mnT = self.Hflat(32, 8).rearrange("p (c t) -> p c t", c=DC)
        mT = self.Hflat(40, 8).rearrange("p (c t) -> p c t", c=DC)
        S.op("dve", lambda e: e.memset(ss[:, 0:2], 0.0), writes=["ss0", "ss1"])
        for s in range(2):
            S.op("sp", (lambda e, s=s: e.dma_start(out=xs[:, s, :], in_=self.mem_p[b_, s * 128:(s + 1) * 128, :])),
                 reads=[], writes=rn("H", 8 * s, 8), dma=f"xl{s}")
            S.op("act", (lambda e, s=s: e.activation(out=junk, in_=xs[:, s, :], func=AF.Square, accum_out=ss[:, s:s + 1])),
                 reads=rn("H", 8 * s, 8), writes=rn("H", 16, 4) + [f"ss{s}"])
            S.op("act", (lambda e, s=s: e.activation(out=ss[:, 2 + s:3 + s], in_=ss[:, s:s + 1], func=AF.Sqrt, scale=1.0 / D, bias=self.eps[:, 0:1])),
                 reads=[f"ss{s}"], writes=[f"ss{2 + s}"])
            S.op("dve", (lambda e, s=s: e.reciprocal(out=ss[:, 4 + s:5 + s], in_=ss[:, 2 + s:3 + s])), reads=[f"ss{2 + s}"], writes=[f"ss{4 + s}"])
            S.op("dve", (lambda e, s=s: e.tensor_scalar(out=mn[:, s, :], in0=xs[:, s, :], scalar1=ss[:, 4 + s:5 + s], scalar2=0.0, op0=ALU.mult, op1=ALU.add)),
                 reads=rn("H", 8 * s, 8) + [f"ss{4 + s}"], writes=rn("H", 20 + 4 * s, 4))
            if self.sub < 2:
                continue
            for c4 in range(4):
                b = self.bank()
                pbf = ps[:, b, 0:256].bitcast(BF16)

                def tr(e, s=s, c4=c4, pbf=pbf):
                    ins = None
                    for cc in range(4):
                        c = c4 * 4 + cc
                        ins = e.transpose(pbf[:, cc * 128:(cc + 1) * 128], mn[:, s, c * 128:(c + 1) * 128], self.idb[:])
                    return ins
                S.op("pe", tr, reads=rn("H", 20 + 4 * s, 4), writes=[f"ps{b}"])
                S.op("dve", (lambda e, s=s, c4=c4, pbf=pbf: e.tensor_copy(out=mnT[:, c4 * 4:c4 * 4 + 4, s * 128:(s + 1) * 128],
                                                                         in_=pbf.rearrange("p (c t) -> p c t", c=4))),
                     reads=[f"ps{b}"], writes=rn("H", 32, 8))
        if self.sub < 3:
            return
        for l in range(2):
            for c in range(DC):
                S.op("dve", (lambda e, c=c, l=l: e.tensor_scalar(out=mT[:, c, :], in0=mnT[:, c, :], scalar1=GM[:, l, c:c + 1], scalar2=0.0, op0=ALU.mult, op1=ALU.add)),
                     reads=rn("H", 32, 8), writes=rn("H", 40, 8))
            kb = self.Hflat(16, 2).rearrange("p (m d) -> p m d", m=2)
            for ub_ in range(4):
                if self.sub < 4:
                    continue
                u, ures = self.unit(f"mkv{l}_{ub_}", DC, 256, [(self.w_mkv[l][:, ub_ * 256:(ub_ + 1) * 256], 0)])
                for s in range(2):
                    if self.sub < 5:
                        continue
                    b = self.bank()
                    self.mm_group(ps[:, b, 0:256], [(mT[:, k, s * 128:(s + 1) * 128], u[:, k, :]) for k in range(DC)],
                                  reads=ures + rn("H", 40, 8), writes=[f"ps{b}"])
                    ti = self.tmp_i
                    self.tmp_i = (ti + 1) % 3
                    S.op("act", (lambda e, b=b, ti=ti: e.activation(out=tmpf[:, ti, 0:256], in_=ps[:, b, 0:256], func=AF.Copy)),
                         reads=[f"ps{b}"], writes=[f"tmpf{ti}"])
                    if ub_ < 2:
                        dst = self.o_mk_p[l, b_, s * 128:(s + 1) * 128, ub_ * 256:(ub_ + 1) * 256]
                        bdst = kb[:, s, ub_ * 256:(ub_ + 1) * 256]
                        bres = [f"H{16 + s}"]
                    else:
                        dst = self.o_mv_p[l, b_, s * 128:(s + 1) * 128, (ub_ - 2) * 256:(ub_ - 1) * 256]
                        bdst = self.memV[:, l, s, (ub_ - 2) * 256:(ub_ - 1) * 256]
                        bres = [f"mV{l}"]
                    if self.sub >= 6:
                        S.op("act", (lambda e, dst=dst, ti=ti: e.dma_start(out=dst, in_=tmpf[:, ti, 0:256])),
                             reads=[f"tmpf{ti}"], writes=[], dma=f"ot{ti}")
                    S.op("dve", (lambda e, bdst=bdst, b=b: e.tensor_copy(out=bdst, in_=ps[:, b, 0:256])),
                         reads=[f"ps{b}"], writes=bres)
            if self.sub >= 7:
                self.kT_from_tokmajor(l, kb, rn("H", 16, 2))

    def load_x(self, tile):
        S = self.S
        ps, xres = self.ps, self.xres
        nt = tile["nt"]
        nsub = nt // 128
        for s in range(nsub):
            xs = self.Hf(8 * s, 8)
            if tile["kind"] == "p":
                src = self.x_p[tile["b"], tile["t"] * 512 + s * 128: tile["t"] * 512 + (s + 1) * 128, :]
            else:
                src = self.x_s[s * 128:(s + 1) * 128, :]
            S.op("sp", (lambda e, xs=xs, src=src: e.dma_start(out=xs, in_=src)), reads=[], writes=rn("H", 8 * s, 8), dma=f"xl{s}")
            for c4 in range(4):
                b = self.bank()

                def tr(e, xs=xs, c4=c4, b=b):
                    ins = None
                    for cc in range(4):
                        c = c4 * 4 + cc
                        ins = e.transpose(ps[:, b, cc * 128:(cc + 1) * 128], xs[:, c * 128:(c + 1) * 128], self.idf[:])
                    return ins
                S.op("pe", tr, reads=rn("H", 8 * s, 8), writes=[f"ps{b}"])
                eng = "act" if c4 % 2 else "dve"
                dst = xres[:, c4 * 4:c4 * 4 + 4, s * 128:(s + 1) * 128]
                src3 = ps[:, b, :].rearrange("p (c t) -> p c t", c=4)
                if eng == "act":
                    S.op("act", (lambda e, dst=dst, src3=src3: e.activation(out=dst, in_=src3, func=AF.Copy)),
                         reads=[f"ps{b}"], writes=rn("xres", c4 * 4, 4))
                else:
                    S.op("dve", (lambda e, dst=dst, src3=src3: e.tensor_copy(out=dst, in_=src3)),
                         reads=[f"ps{b}"], writes=rn("xres", c4 * 4, 4))

    def store_y(self, tile):
        S = self.S
        ps, xres = self.ps, self.xres
        nt = tile["nt"]
        nsub = nt // 128
        for s in range(nsub):
            ys = self.Hf(8 * s, 8)
            for c4 in range(4):
                b = self.bank()

                def tr(e, c4=c4, b=b, s=s):
                    ins = None
                    for cc in range(4):
                        c = c4 * 4 + cc
                        ins = e.transpose(ps[:, b, cc * 128:(cc + 1) * 128], xres[:, c, s * 128:(s + 1) * 128], self.idf[:])
                    return ins
                S.op("pe", tr, reads=rn("xres", c4 * 4, 4), writes=[f"ps{b}"])
                dst = ys[:, c4 * 512:(c4 + 1) * 512]
                if c4 % 2:
                    S.op("act", (lambda e, dst=dst, b=b: e.activation(out=dst, in_=ps[:, b, :], func=AF.Copy)),
                         reads=[f"ps{b}"], writes=rn("H", 8 * s + 2 * c4, 2))
                else:
                    S.op("dve", (lambda e, dst=dst, b=b: e.tensor_copy(out=dst, in_=ps[:, b, :])),
                         reads=[f"ps{b}"], writes=rn("H", 8 * s + 2 * c4, 2))
            if tile["kind"] == "p":
                dstd = self.y_p[tile["b"], tile["t"] * 512 + s * 128: tile["t"] * 512 + (s + 1) * 128, :]
            else:
                dstd = self.y_s[s * 128:(s + 1) * 128, :]
            S.op("act", (lambda e, ys=ys, dstd=dstd: e.dma_start(out=dstd, in_=ys)), reads=rn("H", 8 * s, 8), writes=[], dma=f"ys{s}")

    def prologue(self):
        S = self.S
        nc = self.nc
        S.op("sp", lambda e: e.dma_start(out=self.idf[:], in_=self.ident_in[:, :]), writes=["idf"], dma="c0")
        S.op("dve", lambda e: e.tensor_copy(out=self.idb[:], in_=self.idf[:]), reads=["idf"], writes=["idb"])
        S.op("dve", lambda e: e.memset(self.ones[:], 1.0), writes=["ones"])
        S.op("dve", lambda e: e.memset(self.eps[:], EPS), writes=["eps"])
        S.op("dve", lambda e: e.memset(self.uh[:].rearrange("p a b c -> p (a b c)"), 0.0), writes=rn("uh", 0, DC))
        for l in range(2):
            for n in range(8):
                S.op("sp", (lambda e, l=l, n=n: e.dma_start(out=self.G[:, l * 8 + n, :], in_=self.g_norm[l, n].rearrange("(c p) -> p c", p=128),
                                                           allow_slow_non_contiguous=True)), writes=[f"G{l * 8 + n}"], dma="c1")
            S.op("sp", (lambda e, l=l: e.dma_start(out=self.GM[:, l, :], in_=self.g_mem[l].rearrange("(c p) -> p c", p=128),
                                                  allow_slow_non_contiguous=True)), writes=[f"GM{l}"], dma="c1")
        for k in range(3):
            S.op("sp", (lambda e, k=k: e.dma_start(out=self.CW[:, k, :], in_=self.w_cdw[k].rearrange("(c p) -> p c", p=128),
                                                  allow_slow_non_contiguous=True)), writes=[f"CW{k}"], dma="c1")
        allc = [f"G{i}" for i in range(16)] + ["GM0", "GM1", "CW0", "CW1", "CW2"]
        self.join("dve", allc)
        for gi in (1, 7, 9, 15):
            S.op("dve", (lambda e, gi=gi: e.tensor_scalar(out=self.G[:, gi, :], in0=self.G[:, gi, :], scalar1=0.5, scalar2=0.0, op0=ALU.mult, op1=ALU.add)),
                 reads=[f"G{gi}"], writes=[f"G{gi}"])
        S.op("dve", lambda e: e.memset(self.ss[:], 0.0), reads=allc + ["ones", "eps", "idb", "idf"], writes=["consts"] + [f"ss{i}" for i in range(8)])
        for eng in ("pe", "act", "pool"):
            S.op(eng, lambda e: e.nop(), reads=["consts"])
        if self.stage < 2:
            return
        VPB = self.Hf(0, 48).rearrange("p (h l) -> p h l", h=16)
        S.op("sp", lambda e: e.dma_start(out=VPB[:, :, 0:257], in_=bass.AP(self.relb.tensor, 0, [[0, 128], [257, 16], [1, 257]])),
             writes=rn("H", 0, 48), dma="c2")
        S.op("dve", lambda e: e.tensor_copy(out=VPB[:, :, 257:768], in_=VPB[:, :, 256:257].broadcast_to([128, 16, 511])),
             reads=rn("H", 0, 48), writes=["vpb"])
        S.op("sp", lambda e: e.dma_start(out=bass.AP(self.vp_d, 0, [[16 * 768, 128], [1, 16 * 768]]), in_=self.Hf(0, 48)),
             reads=["vpb"] + rn("H", 0, 48), writes=["vpd"], dma="c3")
        WBA = self.xres[:].rearrange("p a b -> p (a b)")[:, 0:8 * 640].rearrange("p (h c) -> p h c", h=8)
        for half in range(2):
            S.op("sp", (lambda e, half=half: e.dma_start(out=WBA, in_=bass.AP(self.vp_d, half * 8 * 768 + 128,
                                                                             [[16 * 768 - 1, 128], [768, 8], [1, 640]]))),
                 reads=["vpd"], writes=["wba"], dma="c4")
            S.op("dve", lambda e: e.memset(WBA[64:128, :, 0:64], NEG), reads=["wba"], writes=["wba"])
            S.op("dve", lambda e: e.memset(WBA[0:64, :, 576:640], NEG), reads=["wba"], writes=["wba"])
            S.op("sp", (lambda e, half=half: e.dma_start(out=self.wb_d[half * 8:(half + 1) * 8].rearrange("h p c -> p h c"), in_=WBA)),
                 reads=["wba"], writes=["wbd"], dma="c5")
        S.op("dve", lambda e: e.nop(), reads=["wbd", "wba", "vpb"], writes=rn("xres", 0, DC) + rn("H", 0, 48))

    def build(self):
        self.dram()
        self.sbuf()
        S = self.S
        self.prologue()
        tiles = []
        for b in range(NPS):
            for t in range(4):
                tiles.append(dict(kind="p", b=b, t=t, nt=512, nseg=1))
        tiles.append(dict(kind="s", nt=256, nseg=NSS))
        if self.cfg_tiles is not None:
            tiles = [tiles[i] for i in self.cfg_tiles]
        for ti_, tile in enumerate(tiles):
            self.pool_ok = ti_ > 0
            nt = tile["nt"]
            if self.stage < 3:
                break
            if tile["kind"] == "p" and tile["t"] == 0:
                self.mem_project(tile["b"])
                S.op("dve", lambda e: e.memset(self.uh[:].rearrange("p a b c -> p (a b c)"), 0.0), writes=rn("uh", 0, DC))
            if tile["kind"] == "s":
                self.join("dve", rn("uh", 0, DC))
                for c in range(DC):
                    S.op("sp", (lambda e, c=c: e.dma_start(out=self.uh[:, c, :, :],
                                                          in_=self.st_conv[:, :, c * 128:(c + 1) * 128].rearrange("s t p -> p s t"),
                                                          allow_slow_non_contiguous=True)),
                         writes=[f"uh{c}"], dma="hs")
                self.join("dve", rn("uh", 0, DC))
            if self.stage < 4:
                break
            self.load_x(tile)
            for l in range(self.nlayers):
                self.ffn(l, 1, nt)
                if l == 0:
                    self.conv(tile)
                else:
                    self.attn(tile)
                self.mem_attn(l, tile)
                self.ffn(l, 2, nt)
            self.store_y(tile)
        finals = [(k, 16 * v) for k, v in S.dcnt.items()]
        nc = self.nc
        sems = {}
        for k in list(S.dcnt.keys()):
            sems["d:" + k] = self.es.enter_context(nc.semaphore("d_" + k))
        for k in ("pe", "act", "dve", "pool"):
            sems["e:" + k] = self.es.enter_context(nc.semaphore("e_" + k))
        engmap = {"pe": "tensor", "act": "scalar", "dve": "vector", "pool": "gpsimd", "sp": "sync"}
        with nc.Block() as block:
            for ek, attr in engmap.items():
                items = S.q[ek]

                def body(e, items=items, ek=ek):
                    for waits, fn, tok in items:
                        for sk, v in waits:
                            e.wait_ge(sems[sk], v)
                        ins = fn(e)
                        ins.then_inc(sems[tok[0]], 16 if tok[0].startswith("d:") else 1)
                    if ek == "sp":
                        for k, v in finals:
                            e.wait_ge(sems["d:" + k], v)
                getattr(block, attr)(body)
        return nc


_CACHE = {}


def kernel(x_prompt, x_sample, state_conv, cache_band_k, cache_band_v, cache_mem_k, cache_mem_v,
           mem_prompt, g_norm, g_mem, w_ffn1_in, w_ffn1_out, w_ffn2_in, w_ffn2_out,
           w_conv_in, w_conv_dw, w_conv_out, w_attn_qkv, rel_bias, w_attn_o,
           w_mem_q, w_mem_kv, w_mem_o):
    f = lambda a: np.ascontiguousarray(np.asarray(a), dtype=np.float32)
    B = Builder()
    nc = B.build()
    x_prompt, x_sample = f(x_prompt), f(x_sample)
    shared = dict(
        g_norm=f(g_norm), g_mem=f(g_mem), w_f1i=f(w_ffn1_in), w_f1o=f(w_ffn1_out), w_f2i=f(w_ffn2_in), w_f2o=f(w_ffn2_out),
        w_ci=f(w_conv_in)[0], w_cdw=f(w_conv_dw)[0], w_co=f(w_conv_out)[0], w_qkv=f(w_attn_qkv)[0], relb=f(rel_bias)[0],
        w_ao=f(w_attn_o)[0], w_mq=f(w_mem_q), w_mkv=f(w_mem_kv), w_mo=f(w_mem_o),
        ident_in=np.eye(128, dtype=np.float32),
    )
    state_conv, cache_band_k, cache_band_v = f(state_conv), f(cache_band_k), f(cache_band_v)
    cache_mem_k, cache_mem_v, mem_prompt = f(cache_mem_k), f(cache_mem_v), f(mem_prompt)
    in_maps = []
    for i in range(NCORES):
        ps_, ss_ = slice(NPS * i, NPS * (i + 1)), slice(NSS * i, NSS * (i + 1))
        m = dict(shared)
        m["x_p"] = x_prompt[ps_]
        m["x_s"] = x_sample[ss_].reshape(NSS * LS, D)
        m["st_conv"] = state_conv[0, ss_]
        m["c_bk"] = cache_band_k[0, ss_].reshape(NSS, 512, D)
        m["c_bv"] = cache_band_v[0, ss_].reshape(NSS, 512, D)
        m["c_mk"] = cache_mem_k[:, ss_].reshape(2, NSS, NMEM, MW)
        m["c_mv"] = cache_mem_v[:, ss_].reshape(2, NSS, NMEM, MW)
        m["mem_p"] = mem_prompt[ps_]
        in_maps.append({k: np.ascontiguousarray(v) for k, v in m.items()})
    res = run_bass_kernel_spmd(nc, in_maps, core_ids=list(range(NCORES)))
    R = res.results
    cat = lambda k, ax=0: np.concatenate([np.asarray(r[k]) for r in R], axis=ax)
    y_prompt = cat("y_p")
    y_sample = cat("y_s").reshape(NCORES * NSS, LS, D)
    conv_p = cat("o_conv_p")[None]
    bk_p = cat("o_bk_p").reshape(1, NCORES * NPS, 512, 16, 128)
    bv_p = cat("o_bv_p").reshape(1, NCORES * NPS, 512, 16, 128)
    mk_p = cat("o_mk_p", 1).reshape(2, NCORES * NPS, NMEM, 4, 128)
    mv_p = cat("o_mv_p", 1).reshape(2, NCORES * NPS, NMEM, 4, 128)
    conv_s = cat("o_conv_s")[None]
    bk_s = cat("o_bk_s").reshape(1, NCORES * NSS, LS, 16, 128)
    bv_s = cat("o_bv_s").reshape(1, NCORES * NSS, LS, 16, 128)
    return (y_prompt, y_sample, conv_p, bk_p, bv_p, mk_p, mv_p, conv_s, bk_s, bv_s)
```

```python
import numpy as np
from contextlib import ExitStack
import concourse.bass as bass
import concourse.mybir as mybir
from concourse.bass_utils import run_bass_kernel_spmd

F32 = mybir.dt.float32
BF16 = mybir.dt.bfloat16
AF = mybir.ActivationFunctionType
ALU = mybir.AluOpType

NCORES = 8
D = 2048
DC = 16
DFF = 5632
FC = 44
SEQ = 2048
NPS = 2
NSS = 4
LS = 64
NMEM = 256
MW = 512
EPS = 1e-6
RING = 18432
NWSEM = 8
SCALE = 128 ** -0.5
NEG = -1e30


class Sched:
    ENG = ("pe", "act", "dve", "pool", "sp")

    def __init__(self):
        self.q = {k: [] for k in self.ENG}
        self.ecnt = {k: 0 for k in self.ENG}
        self.dcnt = {}
        self.seen = {k: {} for k in self.ENG}
        self.W = {}
        self.R = {}

    def op(self, eng, fn, reads=(), writes=(), dma=None):
        waits = {}
        isdma = dma is not None
        writes = list(writes) + [r for r in reads if r.startswith("ps") and r not in writes]

        def need(tok, kind):
            sk, val, peng = tok
            if not isdma and not sk.startswith("d:") and peng == eng:
                if eng == "pe":
                    return
            if self.seen[eng].get(sk, 0) >= val:
                return
            if waits.get(sk, 0) < val:
                waits[sk] = val

        for r in reads:
            t = self.W.get(r)
            if t is not None:
                need(t, "raw")
        for w in writes:
            t = self.W.get(w)
            if t is not None:
                need(t, "waw")
            for sk, (v, pe) in self.R.get(w, {}).items():
                need((sk, v, pe), "war")
        for sk, v in waits.items():
            self.seen[eng][sk] = v
        if isdma:
            self.dcnt[dma] = self.dcnt.get(dma, 0) + 1
            tok = ("d:" + dma, 16 * self.dcnt[dma], eng)
        else:
            self.ecnt[eng] += 1
            tok = ("e:" + eng, self.ecnt[eng], eng)
        for r in reads:
            self.R.setdefault(r, {})[tok[0]] = (tok[1], eng)
        for w in writes:
            self.W[w] = tok
            self.R[w] = {}
        self.q[eng].append((list(waits.items()), fn, tok))
        return tok


def rn(prefix, a, n=1):
    return [f"{prefix}{i}" for i in range(a, a + n)]


class Builder:
    def __init__(self, tiles=None, nlayers=2, stage=99):
        self.stage = stage
        import os as _os
        self.sub = int(_os.environ.get("KSUB", "99"))
        self.nc = bass.Bass("TRN2", target_bir_lowering=False)
        self.cfg_tiles = tiles
        self.nlayers = nlayers
        self.S = Sched()
        self.es = ExitStack()
        self.bank_i = 0
        self.pool_ok = False
        self.alt = 0
        self.uidx = 0
        self.ring_ptr = 0
        self.live = []
        self.unit_off = {}
        self.scr_ptr = 0
        self.tmp_i = 0
        self.sq_i = 0
        self.sg_i = 0
        self.ub_i = 0
        self.sct_i = 0

    def dram(self):
        nc = self.nc
        I = lambda n, s: nc.dram_tensor(n, s, F32, kind="ExternalInput").ap()
        O = lambda n, s: nc.dram_tensor(n, s, F32, kind="ExternalOutput").ap()
        self.x_p = I("x_p", [NPS, SEQ, D])
        self.x_s = I("x_s", [NSS * LS, D])
        self.st_conv = I("st_conv", [NSS, 2, D])
        self.c_bk = I("c_bk", [NSS, 512, D])
        self.c_bv = I("c_bv", [NSS, 512, D])
        self.c_mk = I("c_mk", [2, NSS, NMEM, MW])
        self.c_mv = I("c_mv", [2, NSS, NMEM, MW])
        self.mem_p = I("mem_p", [NPS, NMEM, D])
        self.g_norm = I("g_norm", [2, 8, D])
        self.g_mem = I("g_mem", [2, D])
        self.w_f1i = I("w_f1i", [2, D, 2 * DFF])
        self.w_f1o = I("w_f1o", [2, DFF, D])
        self.w_f2i = I("w_f2i", [2, D, 2 * DFF])
        self.w_f2o = I("w_f2o", [2, DFF, D])
        self.w_ci = I("w_ci", [D, 3 * D])
        self.w_cdw = I("w_cdw", [3, D])
        self.w_co = I("w_co", [D, D])
        self.w_qkv = I("w_qkv", [D, 3 * D])
        self.relb = I("relb", [16, 257])
        self.w_ao = I("w_ao", [D, D])
        self.w_mq = I("w_mq", [2, D, MW])
        self.w_mkv = I("w_mkv", [2, D, 2 * MW])
        self.w_mo = I("w_mo", [2, MW, D])
        self.ident_in = I("ident_in", [128, 128])
        self.y_p = O("y_p", [NPS, SEQ, D])
        self.y_s = O("y_s", [NSS * LS, D])
        self.o_conv_p = O("o_conv_p", [NPS, 2, D])
        self.o_bk_p = O("o_bk_p", [NPS, 512, D])
        self.o_bv_p = O("o_bv_p", [NPS, 512, D])
        self.o_mk_p = O("o_mk_p", [2, NPS, NMEM, MW])
        self.o_mv_p = O("o_mv_p", [2, NPS, NMEM, MW])
        self.o_conv_s = O("o_conv_s", [NSS, 2, D])
        self.o_bk_s = O("o_bk_s", [NSS * LS, D])
        self.o_bv_s = O("o_bv_s", [NSS * LS, D])
        per_layer = 2 * (DC * 2 * DFF + DFF * D) + (DC * 3 * D + D * D // 8) * 0
        tot = 2 * (2 * (D * 2 * DFF + DFF * D) + D * MW + D * 2 * MW + MW * D) + 2 * (D * 3 * D + D * D)
        self.SCR_EL = 60 * 1024 * 1024
        self.wscr = [nc.dram_tensor(f"wscr{i}", [self.SCR_EL // 128, 128], BF16, kind="Internal") for i in range(4)]
        self.scr_t = 0
        self.vp_d = nc.dram_tensor("vp_d", [128, 16 * 768], F32, kind="Internal")
        self.wb_d = nc.dram_tensor("wb_d", [16, 128, 640], F32, kind="Internal").ap()

    def sb(self, name, shape, dt):
        return self.es.enter_context(self.nc.sbuf_tensor(name, shape, dt))

    def sbuf(self):
        self.xres = self.sb("xres", [128, DC, 512], F32)
        self.xn = self.sb("xn", [128, DC, 512], BF16)
        self.H = self.sb("H", [128, 48, 512], BF16)
        self.kTp = self.sb("kTp", [128, 16, 512], BF16)
        self.Vp = self.sb("Vp", [128, 4, 2048], BF16)
        self.ring = self.sb("ring", [128, RING], BF16)
        self.WBt = self.sb("WBt", [128, 640], F32)
        self.WBt2 = self.sb("WBt2", [128, 640], F32)
        self.PT = self.sb("PT", [128, 2560], BF16)
        self.sct = self.sb("sct", [128, 640], F32)
        self.sq = self.sb("sq", [128, 2, 512], BF16)
        self.rs = self.sb("rs", [128, 2, 512], F32)
        self.tmpf = self.sb("tmpf", [128, 3, 512], F32)
        self.sg = self.sb("sg", [128, 2, 512], BF16)
        self.CA = self.sb("CA", [128, 1568], F32)
        self.ub = self.CA[:, 0:1056].rearrange("p (a b) -> p a b", a=2)
        self.yt = self.CA[:, 1056:1568]
        self.PT2 = self.CA[:, 0:1280].bitcast(BF16)
        self.uh = self.sb("uh", [128, DC, 4, 2], F32)
        self.memKT = self.sb("memKT", [128, 2, 4, 256], BF16)
        self.memV = self.sb("memV", [128, 2, 2, 512], BF16)
        self.G = self.sb("G", [128, 16, DC], F32)
        self.GM = self.sb("GM", [128, 2, DC], F32)
        self.CW = self.sb("CW", [128, 3, DC], F32)
        self.idf = self.sb("idf", [128, 128], F32)
        self.idb = self.sb("idb", [128, 128], BF16)
        self.ones = self.sb("ones", [128, 128], BF16)
        self.eps = self.sb("eps", [128, 1], F32)
        self.ss = self.sb("ss", [128, 8], F32)
        self.ps = self.es.enter_context(self.nc.psum_tensor("ps", [128, 8, 512], F32))

    def bank(self):
        b = self.bank_i
        self.bank_i = (b + 1) % 7
        return b

    def ve(self):
        if not self.pool_ok:
            return "dve"
        self.alt ^= 1
        return "pool" if self.alt else "dve"

    def join(self, eng, res):
        self.S.op(eng, lambda e: e.nop(), reads=list(res), writes=list(res))

    def Hf(self, a, n):
        return self.H[:, a:a + n, :].rearrange("p a b -> p (a b)").bitcast(F32)

    def Hflat(self, a, n):
        return self.H[:, a:a + n, :].rearrange("p a b -> p (a b)")

    def unit(self, key, kc, nw, parts):
        S = self.S
        U = kc * nw
        if self.ring_ptr + U > RING:
            self.ring_ptr = 0
        a, b = self.ring_ptr, self.ring_ptr + U
        self.ring_ptr = b
        over = [l for l in self.live if l[0] < b and l[1] > a]
        self.live = [l for l in self.live if not (l[0] < b and l[1] > a)]
        ui = self.uidx
        self.uidx += 1
        res = f"wu{ui}"
        oldres = [r for l in over for r in l[2]]
        view = self.ring[:, a:b].rearrange("p (k n) -> p k n", k=kc)
        first = key not in self.unit_off
        if first:
            if self.scr_ptr + U * 128 > self.SCR_EL:
                self.scr_t += 1
                self.scr_ptr = 0
            off = (self.scr_t, self.scr_ptr)
            self.unit_off[key] = off
            self.scr_ptr += U * 128
            myres = []
            for pi, (src, co) in enumerate(parts):
                ncol = src.shape[1]
                r = f"{res}p{pi}"
                myres.append(r)
                dst = view[:, :, co:co + ncol]
                srcv = src.rearrange("(k p) n -> p k n", p=128)
                S.op("pool", (lambda e, dst=dst, srcv=srcv: e.dma_start(out=dst, in_=srcv)),
                     writes=[r] + (oldres if pi == 0 else []), dma=f"pw{ui % NWSEM}")
            scr = bass.AP(self.wscr[off[0]], off[1], [[U, 128], [1, U]])
            flat = self.ring[:, a:b]
            S.op("sp", (lambda e, scr=scr, flat=flat: e.dma_start(out=scr, in_=flat)),
                 reads=myres, writes=[f"scr{key}"], dma=f"s{ui % NWSEM}")
        else:
            off = self.unit_off[key]
            scr = bass.AP(self.wscr[off[0]], off[1], [[U, 128], [1, U]])
            flat = self.ring[:, a:b]
            myres = [res]
            S.op("sp", (lambda e, scr=scr, flat=flat: e.dma_start(out=flat, in_=scr)),
                 reads=[f"scr{key}"], writes=[res] + oldres, dma=f"w{ui % NWSEM}")
        self.live.append((a, b, myres))
        return view, myres

    def mm_group(self, out_ap, pairs, reads, writes):
        n = len(pairs)

        def fn(e, out_ap=out_ap, pairs=pairs, n=n):
            ins = None
            for i, (l, r) in enumerate(pairs):
                ins = e.matmul(out_ap, lhsT=l, rhs=r, start=(i == 0), stop=(i == n - 1))
            return ins
        return self.S.op("pe", fn, reads=reads, writes=writes)

    def mm_kouter(self, groups, kc, rhs_of_k, rhs_res_of_k):
        allreads = sorted(set(r for g in groups for r in g[2]))
        banks = [f"ps{g[3]}" for g in groups]
        for k in range(kc):
            def fn(e, k=k):
                ins = None
                for (out_ap, lf, _, _) in groups:
                    ins = e.matmul(out_ap, lhsT=lf(k), rhs=rhs_of_k(k), start=(k == 0), stop=(k == kc - 1))
                return ins
            self.S.op("pe", fn, reads=[rhs_res_of_k(k)] + allreads, writes=banks)

    def sums_to_rstd(self, bank, nt, scale_mean):
        S = self.S
        ps, rs, eps = self.ps, self.rs, self.eps
        S.op("act", lambda e: e.activation(out=rs[:, 0, 0:nt], in_=ps[:, bank, 0:nt], func=AF.Sqrt,
                                           scale=scale_mean, bias=eps[:, 0:1]),
             reads=[f"ps{bank}"], writes=["rs0"])
        S.op("dve", lambda e: e.reciprocal(out=rs[:, 1, 0:nt], in_=rs[:, 0, 0:nt]), reads=["rs0"], writes=["rs1"])

    def prenorm(self, gi, nt):
        S = self.S
        xres, xn, sq, ones, ps, G, rs = self.xres, self.xn, self.sq, self.ones, self.ps, self.G, self.rs
        bank = 7
        for c in range(DC):
            si = self.sq_i
            self.sq_i ^= 1
            S.op("act", (lambda e, c=c, si=si: e.activation(out=sq[:, si, 0:nt], in_=xres[:, c, 0:nt], func=AF.Square)),
                 reads=[f"xres{c}"], writes=[f"sq{si}"])
            S.op("pe", (lambda e, c=c, si=si: e.matmul(ps[:, bank, 0:nt], lhsT=ones[:], rhs=sq[:, si, 0:nt],
                                                      start=(c == 0), stop=(c == DC - 1))),
                 reads=[f"sq{si}"], writes=[f"ps{bank}"])
        self.sums_to_rstd(bank, nt, 1.0 / D)
        for c in range(DC):
            eng = "dve"
            S.op(eng, (lambda e, c=c: e.scalar_tensor_tensor(out=xn[:, c, 0:nt], in0=xres[:, c, 0:nt],
                                                            scalar=G[:, gi, c:c + 1], in1=rs[:, 1, 0:nt],
                                                            op0=ALU.mult, op1=ALU.mult)),
                 reads=[f"xres{c}", "rs1"], writes=[f"xn{c}"])

    def post_begin(self, gi):
        self.post_bank = 7
        self.post_cnt = 0
        self.post_gi = gi
        self.post_pending = None

    def post_flush(self):
        if self.post_pending is not None:
            fn, si = self.post_pending
            self.S.op("pe", fn, reads=[f"sq{si}"], writes=[f"ps{self.post_bank}"])
            self.post_pending = None

    def post_chunk(self, j, bank, nt):
        S = self.S
        ps, xn, sq, ones = self.ps, self.xn, self.sq, self.ones
        pb = self.post_bank
        gi = self.post_gi
        self.post_flush()
        si = self.sq_i
        self.sq_i ^= 1
        S.op("act", (lambda e: e.activation(out=sq[:, si, 0:nt], in_=ps[:, bank, 0:nt], func=AF.Square)),
             reads=[f"ps{bank}"], writes=[f"sq{si}"])
        S.op("act", (lambda e: e.activation(out=xn[:, j, 0:nt], in_=ps[:, bank, 0:nt], func=AF.Copy, scale=self.G[:, gi, j:j + 1])),
             reads=[f"ps{bank}"], writes=[f"xn{j}"])
        first = self.post_cnt == 0
        last = self.post_cnt == DC - 1
        self.post_cnt += 1
        self.post_pending = ((lambda e: e.matmul(ps[:, pb, 0:nt], lhsT=ones[:], rhs=sq[:, si, 0:nt], start=first, stop=last)), si)

    def post_end(self, gi, nt):
        S = self.S
        xres, xn, G, rs, tmpf = self.xres, self.xn, self.G, self.rs, self.tmpf
        assert gi == self.post_gi
        self.post_flush()
        self.sums_to_rstd(self.post_bank, nt, 1.0 / D)
        pend = None
        for c in range(DC + 1):
            if c < DC:
                ti = self.tmp_i
                self.tmp_i = (ti + 1) % 3
                S.op("dve", (lambda e, c=c, ti=ti: e.tensor_tensor(out=tmpf[:, ti, 0:nt], in0=xn[:, c, 0:nt], in1=rs[:, 1, 0:nt], op=ALU.mult)),
                     reads=[f"xn{c}", "rs1"], writes=[f"tmpf{ti}"])
            if pend is not None:
                pc, pti = pend
                eng = "pool" if (self.pool_ok and pc % 4 == 3) else "dve"
                S.op(eng, (lambda e, pc=pc, pti=pti: e.tensor_tensor(out=xres[:, pc, 0:nt], in0=xres[:, pc, 0:nt],
                                                                    in1=tmpf[:, pti, 0:nt], op=ALU.add)),
                     reads=[f"xres{pc}", f"tmpf{pti}"], writes=[f"xres{pc}"])
            pend = (c, ti) if c < DC else None

    def ffn(self, l, which, nt):
        S = self.S
        w_in = (self.w_f1i if which == 1 else self.w_f2i)[l]
        w_out = (self.w_f1o if which == 1 else self.w_f2o)[l]
        gpre, gpost = (0, 1) if which == 1 else (6, 7)
        ps, xn, H, sg = self.ps, self.xn, self.H, self.sg
        self.prenorm(l * 8 + gpre, nt)
        NPRE = 3
        pre = []
        for c in range(NPRE):
            u, ures = self.unit(f"f{which}i{l}_{c}", DC, 256,
                                [(w_in[:, c * 128:(c + 1) * 128], 0), (w_in[:, DFF + c * 128:DFF + (c + 1) * 128], 128)])
            pre.append((u, ures, self.bank(), self.bank()))
        groups = []
        for (u, ures, bg, bu) in pre:
            groups.append((ps[:, bg, 0:nt], (lambda k, u=u: u[:, k, 0:128]), ures, bg))
            groups.append((ps[:, bu, 0:nt], (lambda k, u=u: u[:, k, 128:256]), ures, bu))
        self.mm_kouter(groups, DC, lambda k: xn[:, k, 0:nt], lambda k: f"xn{k}")
        for c in range(FC):
            if c < NPRE:
                u, ures, bg, bu = pre[c]
            else:
                u, ures = self.unit(f"f{which}i{l}_{c}", DC, 256,
                                    [(w_in[:, c * 128:(c + 1) * 128], 0), (w_in[:, DFF + c * 128:DFF + (c + 1) * 128], 128)])
                bg, bu = self.bank(), self.bank()
                self.mm_group(ps[:, bg, 0:nt], [(u[:, k, 0:128], xn[:, k, 0:nt]) for k in range(DC)],
                              reads=ures + rn("xn", 0, DC), writes=[f"ps{bg}"])
                self.mm_group(ps[:, bu, 0:nt], [(u[:, k, 128:256], xn[:, k, 0:nt]) for k in range(DC)],
                              reads=ures + rn("xn", 0, DC), writes=[f"ps{bu}"])
            si = self.sg_i
            self.sg_i ^= 1
            S.op("act", (lambda e, bg=bg, si=si: e.activation(out=sg[:, si, 0:nt], in_=ps[:, bg, 0:nt], func=AF.Silu)),
                 reads=[f"ps{bg}"], writes=[f"sg{si}"])
            S.op("dve", (lambda e, bu=bu, si=si, c=c: e.tensor_tensor(out=H[:, c, 0:nt], in0=sg[:, si, 0:nt],
                                                                     in1=ps[:, bu, 0:nt], op=ALU.mult)),
                 reads=[f"ps{bu}", f"sg{si}"], writes=[f"H{c}"])
        self.post_begin(l * 8 + gpost)
        for j in range(DC):
            u, ures = self.unit(f"f{which}o{l}_{j}", FC, 128, [(w_out[:, j * 128:(j + 1) * 128], 0)])
            b = self.bank()
            self.mm_group(ps[:, b, 0:nt], [(u[:, k, :], H[:, k, 0:nt]) for k in range(FC)],
                          reads=ures + rn("H", 0, FC), writes=[f"ps{b}"])
            self.post_chunk(j, b, nt)
        self.post_end(l * 8 + gpost, nt)

    def proj_out(self, key, w, kchunks, hslot0, gi, nt):
        ps, H = self.ps, self.H
        self.post_begin(gi)
        for ub_ in range(8):
            u, ures = self.unit(f"{key}_{ub_}", kchunks, 256, [(w[:, ub_ * 256:(ub_ + 1) * 256], 0)])
            for jj in range(2):
                j = ub_ * 2 + jj
                b = self.bank()
                self.mm_group(ps[:, b, 0:nt], [(u[:, k, jj * 128:(jj + 1) * 128], H[:, hslot0 + k, 0:nt]) for k in range(kchunks)],
                              reads=ures + rn("H", hslot0, kchunks), writes=[f"ps{b}"])
                self.post_chunk(j, b, nt)
        self.post_end(gi, nt)

    def conv(self, tile):
        S = self.S
        nt, nseg = tile["nt"], tile["nseg"]
        L = nt // nseg
        ps, xn, H, ub, uh, CW, tmpf, yt = self.ps, self.xn, self.H, self.ub, self.uh, self.CW, self.tmpf, self.yt
        self.prenorm(2, nt)
        w = self.w_ci
        NPRE = 2
        pre = []
        groups = []
        for c in range(NPRE):
            u, ures = self.unit(f"ci_{c}", DC, 384, [(w[:, c * 128:(c + 1) * 128], 0),
                                                    (w[:, D + c * 128:D + (c + 1) * 128], 128),
                                                    (w[:, 2 * D + c * 128:2 * D + (c + 1) * 128], 256)])
            bks = (self.bank(), self.bank(), self.bank())
            pre.append((u, ures, bks))
            for bnk, co in zip(bks, (0, 128, 256)):
                groups.append((ps[:, bnk, 0:nt], (lambda k, u=u, co=co: u[:, k, co:co + 128]), ures, bnk))
        self.mm_kouter(groups, DC, lambda k: xn[:, k, 0:nt], lambda k: f"xn{k}")
        for c in range(DC):
            if c < NPRE:
                u, ures, (bb, bc_, bx) = pre[c]
            else:
                u, ures = self.unit(f"ci_{c}", DC, 384, [(w[:, c * 128:(c + 1) * 128], 0),
                                                        (w[:, D + c * 128:D + (c + 1) * 128], 128),
                                                        (w[:, 2 * D + c * 128:2 * D + (c + 1) * 128], 256)])
                bb, bc_, bx = self.bank(), self.bank(), self.bank()
                for bnk, co in ((bb, 0), (bc_, 128), (bx, 256)):
                    self.mm_group(ps[:, bnk, 0:nt], [(u[:, k, co:co + 128], xn[:, k, 0:nt]) for k in range(DC)],
                                  reads=ures + rn("xn", 0, DC), writes=[f"ps{bnk}"])
            ui = self.ub_i
            self.ub_i ^= 1
            uv = ub[:, ui, 0:nseg * (L + 2)].rearrange("p (s l) -> p s l", s=nseg)
            S.op("dve", (lambda e, uv=uv, c=c: e.tensor_copy(out=uv[:, :, 0:2], in_=uh[:, c, 0:nseg, :])),
                 reads=[f"uh{c}"], writes=[f"ub{ui}h"])
            ti = self.tmp_i
            self.tmp_i = (ti + 1) % 3
            S.op("act", (lambda e, bc_=bc_, ti=ti: e.activation(out=tmpf[:, ti, 0:nt], in_=ps[:, bc_, 0:nt], func=AF.Copy)),
                 reads=[f"ps{bc_}"], writes=[f"tmpf{ti}"])
            t3 = tmpf[:, ti, 0:nt].rearrange("p (s l) -> p s l", s=nseg)
            px = ps[:, bx, 0:nt].rearrange("p (s l) -> p s l", s=nseg)
            S.op("dve", (lambda e, uv=uv, t3=t3, px=px: e.tensor_tensor(out=uv[:, :, 2:L + 2], in0=t3, in1=px, op=ALU.mult)),
                 reads=[f"tmpf{ti}", f"ps{bx}"], writes=[f"ub{ui}"])
            y3 = yt[:, 0:nt].rearrange("p (s l) -> p s l", s=nseg)
            eng = "dve"
            S.op(eng, (lambda e, uv=uv, y3=y3, c=c: e.tensor_scalar(out=y3, in0=uv[:, :, 0:L], scalar1=CW[:, 0, c:c + 1],
                                                                   scalar2=0.0, op0=ALU.mult, op1=ALU.add)),
                 reads=[f"ub{ui}", f"ub{ui}h"], writes=["yt"])
            S.op(eng, (lambda e, uv=uv, y3=y3, c=c: e.scalar_tensor_tensor(out=y3, in0=uv[:, :, 1:L + 1], scalar=CW[:, 1, c:c + 1],
                                                                          in1=y3, op0=ALU.mult, op1=ALU.add)),
                 reads=[f"ub{ui}", f"ub{ui}h", "yt"], writes=["yt"])
            S.op(eng, (lambda e, uv=uv, y3=y3, c=c: e.scalar_tensor_tensor(out=y3, in0=uv[:, :, 2:L + 2], scalar=CW[:, 2, c:c + 1],
                                                                          in1=y3, op0=ALU.mult, op1=ALU.add)),
                 reads=[f"ub{ui}", "yt"], writes=["yt"])
            S.op("dve", (lambda e, bb=bb, c=c: e.tensor_tensor(out=H[:, c, 0:nt], in0=yt[:, 0:nt], in1=ps[:, bb, 0:nt], op=ALU.mult)),
                 reads=["yt", f"ps{bb}"], writes=[f"H{c}"])
            S.op(self.ve(), (lambda e, uv=uv, c=c: e.tensor_copy(out=uh[:, c, 0:nseg, :], in_=uv[:, :, L:L + 2])),
                 reads=[f"ub{ui}"], writes=[f"uh{c}"])
        if tile["kind"] == "s" or tile["t"] == 3:
            for sgi in range(nseg):
                dst = (self.o_conv_s[sgi] if tile["kind"] == "s" else self.o_conv_p[tile["b"]])
                dstv = dst.rearrange("t (c p) -> p c t", p=128)
                for c in range(DC):
                    S.op("act", (lambda e, dstv=dstv, c=c, sgi=sgi: e.dma_start(out=dstv[:, c, :], in_=uh[:, c, sgi, :],
                                                                               allow_slow_non_contiguous=True)),
                         reads=[f"uh{c}"], writes=[], dma="oc")
        self.proj_out("co", self.w_co, DC, 0, 3, nt)

    def attn(self, tile):
        S = self.S
        nt, nseg, kind = tile["nt"], tile["nseg"], tile["kind"]
        ps, xn, H, PT, sct, WBt, kTp, Vp, ones, tmpf = self.ps, self.xn, self.H, self.PT, self.sct, self.WBt, self.kTp, self.Vp, self.ones, self.tmpf
        self.prenorm(8 + 2, nt)
        w = self.w_qkv
        want_out = kind == "s" or tile["t"] == 3
        nsub = nt // 128 if kind == "p" else nseg
        rows = 128 if kind == "p" else LS

        def qslot(h):
            return H[:, h, 0:nt]

        def kslot(h):
            return H[:, 16 + h, 0:nt] if kind == "p" else H[:, h, 256:256 + nt]

        def kres(h):
            return f"H{16 + h}" if kind == "p" else f"H{h}"

        def qres(h):
            return f"H{h}"

        def vcur(s):
            return self.Hflat(32 + 4 * s, 4)[0:rows, :]

        NPRE = 3
        preq = []
        groups = []
        for ub_ in range(NPRE):
            u, ures = self.unit(f"qkvq_{ub_}", DC, 256, [(w[:, ub_ * 256:(ub_ + 1) * 256], 0)])
            bks = (self.bank(), self.bank())
            preq.append((u, ures, bks))
            for jj in range(2):
                groups.append((ps[:, bks[jj], 0:nt], (lambda k, u=u, jj=jj: u[:, k, jj * 128:(jj + 1) * 128]), ures, bks[jj]))
        self.mm_kouter(groups, DC, lambda k: xn[:, k, 0:nt], lambda k: f"xn{k}")
        for which, slotf, resf, base in (("q", qslot, qres, 0), ("k", kslot, kres, D)):
            for ub_ in range(8):
                ispre = which == "q" and ub_ < NPRE
                if ispre:
                    u, ures, bks = preq[ub_]
                else:
                    u, ures = self.unit(f"qkv{which}_{ub_}", DC, 256, [(w[:, base + ub_ * 256:base + (ub_ + 1) * 256], 0)])
                for jj in range(2):
                    h = ub_ * 2 + jj
                    if ispre:
                        b = bks[jj]
                    else:
                        b = self.bank()
                        self.mm_group(ps[:, b, 0:nt], [(u[:, k, jj * 128:(jj + 1) * 128], xn[:, k, 0:nt]) for k in range(DC)],
                                      reads=ures + rn("xn", 0, DC), writes=[f"ps{b}"])
                    dst = slotf(h)
                    S.op("act", (lambda e, dst=dst, b=b: e.activation(out=dst, in_=ps[:, b, 0:nt], func=AF.Copy)),
                         reads=[f"ps{b}"], writes=[resf(h)])
                if which == "k" and want_out:
                    for s in range(nsub):
                        b = self.bank()
                        self.mm_group(ps[0:rows, b, 0:256], [(xn[:, k, s * rows:(s + 1) * rows], u[:, k, :]) for k in range(DC)],
                                      reads=ures + rn("xn", 0, DC), writes=[f"ps{b}"])
                        ti = self.tmp_i
                        self.tmp_i = (ti + 1) % 3
                        S.op("act", (lambda e, b=b, ti=ti: e.activation(out=tmpf[0:rows, ti, 0:256], in_=ps[0:rows, b, 0:256], func=AF.Copy)),
                             reads=[f"ps{b}"], writes=[f"tmpf{ti}"])
                        if kind == "p":
                            dst = self.o_bk_p[tile["b"], s * 128:(s + 1) * 128, ub_ * 256:(ub_ + 1) * 256]
                        else:
                            dst = self.o_bk_s[s * LS:(s + 1) * LS, ub_ * 256:(ub_ + 1) * 256]
                        S.op("act", (lambda e, dst=dst, ti=ti: e.dma_start(out=dst, in_=tmpf[0:rows, ti, 0:256])),
                             reads=[f"tmpf{ti}"], writes=[], dma=f"ot{ti}")
        for ub_ in range(8):
            u, ures = self.unit(f"qkvv_{ub_}", DC, 256, [(w[:, 2 * D + ub_ * 256:2 * D + (ub_ + 1) * 256], 0)])
            for s in range(nsub):
                b = self.bank()
                self.mm_group(ps[0:rows, b, 0:256], [(xn[:, k, s * rows:(s + 1) * rows], u[:, k, :]) for k in range(DC)],
                              reads=ures + rn("xn", 0, DC), writes=[f"ps{b}"])
                vdst = vcur(s)[:, ub_ * 256:(ub_ + 1) * 256]
                S.op("dve", (lambda e, vdst=vdst, b=b: e.tensor_copy(out=vdst, in_=ps[0:rows, b, 0:256])),
                     reads=[f"ps{b}"], writes=[f"Vc{s}_{ub_}"])
                if want_out:
                    ti = self.tmp_i
                    self.tmp_i = (ti + 1) % 3
                    S.op("act", (lambda e, b=b, ti=ti: e.activation(out=tmpf[0:rows, ti, 0:256], in_=ps[0:rows, b, 0:256], func=AF.Copy)),
                         reads=[f"ps{b}"], writes=[f"tmpf{ti}"])
                    if kind == "p":
                        dst = self.o_bv_p[tile["b"], s * 128:(s + 1) * 128, ub_ * 256:(ub_ + 1) * 256]
                    else:
                        dst = self.o_bv_s[s * LS:(s + 1) * LS, ub_ * 256:(ub_ + 1) * 256]
                    S.op("act", (lambda e, dst=dst, ti=ti: e.dma_start(out=dst, in_=tmpf[0:rows, ti, 0:256])),
                         reads=[f"tmpf{ti}"], writes=[], dma=f"ot{ti}")
        vres_all = lambda s: [f"Vc{s}_{i}" for i in range(8)]

        PTs = [PT, self.PT2]

        def core_A(h, q_ap, qr, items, buf):
            PTb = PTs[buf]
            WBt, wres = (self.WBt, "WBt") if h % 2 == 0 else (self.WBt2, "WBt2")
            S.op("sp", (lambda e, h=h, WBt=WBt: e.dma_start(out=WBt[:], in_=self.wb_d[h])), reads=["wbd"], writes=[wres], dma=f"wb{h % 2}")
            off = 0
            pts = []
            for (kT, kr, V, vr, wc, q0, qn, nk) in items:
                b = self.bank()
                S.op("pe", (lambda e, b=b, kT=kT, q0=q0, qn=qn, nk=nk: e.matmul(ps[0:nk, b, 0:qn], lhsT=kT, rhs=q_ap[:, q0:q0 + qn],
                                                                              start=True, stop=True)),
                     reads=kr + qr, writes=[f"ps{b}"])
                self.sct_i ^= 1
                scb, scr = ((sct, "sct") if self.sct_i else (self.rs[:, 0, :], "rs0"))
                S.op("dve", (lambda e, b=b, wc=wc, qn=qn, nk=nk, scb=scb: e.scalar_tensor_tensor(out=scb[0:nk, 0:qn], in0=ps[0:nk, b, 0:qn], scalar=SCALE,
                                                                                                 in1=WBt[0:nk, wc:wc + qn], op0=ALU.mult, op1=ALU.add)),
                     reads=[f"ps{b}", wres], writes=[scr])
                pt = PTb[0:nk, off:off + qn]
                pres = [f"PT{buf}b{k}" for k in range(off // 128, (off + qn - 1) // 128 + 1)]
                S.op("act", (lambda e, pt=pt, qn=qn, nk=nk, scb=scb: e.activation(out=pt, in_=scb[0:nk, 0:qn], func=AF.Exp)),
                     reads=[scr], writes=pres)
                pts.append((pt, pres))
                off += qn
            return (h, items, pts)

        def core_B(state, nq, out_ap, out_res):
            h, items, pts = state
            bo, bd = self.bank(), self.bank()
            n = len(items)

            def pv(e):
                ins = None
                for i, (kT, kr, V, vr, wc, q0, qn, nk) in enumerate(items):
                    ins = e.matmul(ps[:, bo, q0:q0 + qn], lhsT=V, rhs=pts[i][0], start=(i == 0), stop=(i == n - 1))
                return ins

            def dn(e):
                ins = None
                for i, (kT, kr, V, vr, wc, q0, qn, nk) in enumerate(items):
                    ins = e.matmul(ps[:, bd, q0:q0 + qn], lhsT=ones[0:nk, :], rhs=pts[i][0], start=(i == 0), stop=(i == n - 1))
                return ins
            allv = [r for it in items for r in it[3]]
            allp = sorted(set(r for p in pts for r in p[1]))
            S.op("pe", pv, reads=allv + allp, writes=[f"ps{bo}"])
            S.op("pe", dn, reads=allp, writes=[f"ps{bd}"])
            ti = self.tmp_i
            self.tmp_i = (ti + 1) % 3
            S.op("dve", (lambda e, bd=bd, ti=ti: e.reciprocal(out=tmpf[:, ti, 0:nq], in_=ps[:, bd, 0:nq])), reads=[f"ps{bd}"], writes=[f"tmpf{ti}"])
            S.op("dve", (lambda e, bo=bo, ti=ti: e.tensor_tensor(out=out_ap, in0=ps[:, bo, 0:nq], in1=tmpf[:, ti, 0:nq], op=ALU.mult)),
                 reads=[f"ps{bo}", f"tmpf{ti}"], writes=[out_res])

        if kind == "p":
            t = tile["t"]
            pend = None
            for h in range(16):
                items = []
                order = [("c", 0)] + ([("p", i) for i in range(4)] if t > 0 else []) + [("c", i) for i in range(1, 4)]
                for (src, i) in order:
                    if src == "c":
                        kT = H[:, 16 + h, i * 128:(i + 1) * 128]
                        V = self.Hflat(32 + 4 * i, 4)[:, h * 128:(h + 1) * 128]
                        items.append((kT, [kres(h)], V, vres_all(i), 0, i * 128, (4 - i) * 128, 128))
                    else:
                        kT = kTp[:, h, i * 128:(i + 1) * 128]
                        V = Vp[:, i, h * 128:(h + 1) * 128]
                        items.append((kT, ["kTp"], V, ["Vp"], (4 - i) * 128, 0, (i + 1) * 128, 128))
                st = core_A(h, H[:, h, 0:nt], [qres(h)], items, h % 2)
                if pend is not None:
                    core_B(pend, nt, H[:, pend[0], 0:nt], qres(pend[0]))
                pend = st
            core_B(pend, nt, H[:, pend[0], 0:nt], qres(pend[0]))
            if t < 3:
                for h in range(16):
                    S.op(self.ve(), (lambda e, h=h: e.tensor_copy(out=kTp[:, h, :], in_=H[:, 16 + h, :])),
                         reads=[kres(h)], writes=["kTp"])
                for i in range(4):
                    S.op(self.ve(), (lambda e, i=i: e.tensor_copy(out=Vp[:, i, :], in_=self.Hflat(32 + 4 * i, 4))),
                         reads=vres_all(i), writes=["Vp"])
        else:
            for s in range(nseg):
                kst = self.Hflat(16, 16).rearrange("p (i d) -> p i d", i=4)
                S.op("pool", (lambda e, s=s, kst=kst: e.dma_start(out=kst, in_=self.c_bk[s].rearrange("(i p) d -> p i d", p=128))),
                     reads=[], writes=rn("H", 16, 16), dma="ck")
                S.op("pool", (lambda e, s=s: e.dma_start(out=Vp[:], in_=self.c_bv[s].rearrange("(i p) d -> p i d", p=128))),
                     reads=[], writes=["Vp"], dma="cv")
                cnt = 0
                for i in range(4):
                    for h4 in range(4):
                        b = self.bank()
                        pbf = ps[:, b, 0:256].bitcast(BF16)

                        def tr(e, i=i, h4=h4, pbf=pbf, kst=kst):
                            ins = None
                            for hh in range(4):
                                h = h4 * 4 + hh
                                ins = e.transpose(pbf[:, hh * 128:(hh + 1) * 128], kst[:, i, h * 128:(h + 1) * 128], self.idb[:])
                            return ins
                        S.op("pe", tr, reads=rn("H", 16, 16), writes=[f"ps{b}"])
                        eng = "act" if cnt % 2 else "dve"
                        cnt += 1
                        dst = kTp[:, h4 * 4:h4 * 4 + 4, i * 128:(i + 1) * 128]
                        src = pbf.rearrange("p (h k) -> p h k", h=4)
                        if eng == "act":
                            S.op("act", (lambda e, dst=dst, src=src: e.activation(out=dst, in_=src, func=AF.Copy)),
                                 reads=[f"ps{b}"], writes=["kTp"])
                        else:
                            S.op("dve", (lambda e, dst=dst, src=src: e.tensor_copy(out=dst, in_=src)),
                                 reads=[f"ps{b}"], writes=["kTp"])
                pend = None
                for h in range(16):
                    items = []
                    for i in range(4):
                        items.append((kTp[:, h, i * 128:(i + 1) * 128], ["kTp"], Vp[:, i, h * 128:(h + 1) * 128], ["Vp"],
                                      (4 - i) * 128, 0, LS, 128))
                    kT = H[:, h, 256 + s * LS:256 + (s + 1) * LS]
                    V = vcur(s)[:, h * 128:(h + 1) * 128]
                    items.append((kT, [kres(h)], V, vres_all(s), 0, 0, LS, LS))
                    st = core_A(h, H[:, h, s * LS:(s + 1) * LS], [qres(h)], items, h % 2)
                    if pend is not None:
                        core_B(pend, LS, H[:, pend[0], s * LS:(s + 1) * LS], qres(pend[0]))
                    pend = st
                core_B(pend, LS, H[:, pend[0], s * LS:(s + 1) * LS], qres(pend[0]))
        self.proj_out("ao", self.w_ao, DC, 0, 8 + 3, nt)

    def mem_attn(self, l, tile):
        S = self.S
        nt, nseg, kind = tile["nt"], tile["nseg"], tile["kind"]
        ps, xn, H, PT, sct, ones, memKT, memV = self.ps, self.xn, self.H, self.PT, self.sct, self.ones, self.memKT, self.memV
        self.prenorm(l * 8 + 4, nt)
        w = self.w_mq[l]
        preq = []
        groups = []
        for ub_ in range(2):
            u, ures = self.unit(f"mq{l}_{ub_}", DC, 256, [(w[:, ub_ * 256:(ub_ + 1) * 256], 0)])
            bks = (self.bank(), self.bank())
            preq.append(bks)
            for jj in range(2):
                groups.append((ps[:, bks[jj], 0:nt], (lambda k, u=u, jj=jj: u[:, k, jj * 128:(jj + 1) * 128]), ures, bks[jj]))
        self.mm_kouter(groups, DC, lambda k: xn[:, k, 0:nt], lambda k: f"xn{k}")
        for ub_ in range(2):
            for jj in range(2):
                hm = ub_ * 2 + jj
                b = preq[ub_][jj]
                S.op("act", (lambda e, hm=hm, b=b: e.activation(out=H[:, hm, 0:nt], in_=ps[:, b, 0:nt], func=AF.Copy)),
                     reads=[f"ps{b}"], writes=[f"H{hm}"])
        L = nt // nseg
        for s in range(nseg):
            if kind == "s":
                self.load_mem_cache(l, s)
            pend = None
            for hm in range(4):
                q = H[:, hm, s * L:(s + 1) * L]
                buf = hm % 2
                PTb = [PT, self.PT2][buf]
                pts = []
                for mt in range(2):
                    b = self.bank()
                    S.op("pe", (lambda e, b=b, mt=mt, hm=hm, q=q: e.matmul(ps[:, b, 0:L], lhsT=memKT[:, l, hm, mt * 128:(mt + 1) * 128], rhs=q,
                                                                       start=True, stop=True)),
                         reads=[f"mKT{l}", f"H{hm}"], writes=[f"ps{b}"])
                    pt = PTb[:, mt * 512:mt * 512 + L]
                    S.op("act", (lambda e, b=b, pt=pt: e.activation(out=pt, in_=ps[:, b, 0:L], func=AF.Exp, scale=SCALE)),
                         reads=[f"ps{b}"], writes=rn(f"PT{buf}b", mt * 4, 4))
                    pts.append(pt)
                st = (hm, pts, buf)
                if pend is not None:
                    self.mem_B(l, pend, s, L)
                pend = st
            self.mem_B(l, pend, s, L)
        wo = self.w_mo[l]
        self.post_begin(l * 8 + 5)
        for ub_ in range(2):
            u, ures = self.unit(f"mo{l}_{ub_}", 4, 1024, [(wo[:, ub_ * 1024:(ub_ + 1) * 1024], 0)])
            for jj in range(8):
                j = ub_ * 8 + jj
                b = self.bank()
                self.mm_group(ps[:, b, 0:nt], [(u[:, k, jj * 128:(jj + 1) * 128], H[:, 4 + k, 0:nt]) for k in range(4)],
                              reads=ures + rn("H", 4, 4), writes=[f"ps{b}"])
                self.post_chunk(j, b, nt)
        self.post_end(l * 8 + 5, nt)

    def mem_B(self, l, st, s, L):
        S = self.S
        ps, H, ones, memV, tmpf = self.ps, self.H, self.ones, self.memV, self.tmpf
        hm, pts, buf = st
        bo, bd = self.bank(), self.bank()

        def pv(e):
            e.matmul(ps[:, bo, 0:L], lhsT=memV[:, l, 0, hm * 128:(hm + 1) * 128], rhs=pts[0], start=True, stop=False)
            return e.matmul(ps[:, bo, 0:L], lhsT=memV[:, l, 1, hm * 128:(hm + 1) * 128], rhs=pts[1], start=False, stop=True)

        def dn(e):
            e.matmul(ps[:, bd, 0:L], lhsT=ones[:], rhs=pts[0], start=True, stop=False)
            return e.matmul(ps[:, bd, 0:L], lhsT=ones[:], rhs=pts[1], start=False, stop=True)
        S.op("pe", pv, reads=[f"mV{l}"] + rn(f"PT{buf}b", 0, 8), writes=[f"ps{bo}"])
        S.op("pe", dn, reads=rn(f"PT{buf}b", 0, 8), writes=[f"ps{bd}"])
        ti = self.tmp_i
        self.tmp_i = (ti + 1) % 3
        S.op("dve", (lambda e: e.reciprocal(out=tmpf[:, ti, 0:L], in_=ps[:, bd, 0:L])), reads=[f"ps{bd}"], writes=[f"tmpf{ti}"])
        S.op("dve", (lambda e: e.tensor_tensor(out=H[:, 4 + hm, s * L:(s + 1) * L], in0=ps[:, bo, 0:L],
                                                in1=tmpf[:, ti, 0:L], op=ALU.mult)),
             reads=[f"ps{bo}", f"tmpf{ti}"], writes=[f"H{4 + hm}"])

    def kT_from_tokmajor(self, l, src_bf, src_res):
        S = self.S
        ps, memKT = self.ps, self.memKT
        for mt in range(2):
            b = self.bank()
            pbf = ps[:, b, 0:256].bitcast(BF16)

            def tr(e, mt=mt, pbf=pbf):
                ins = None
                for hm in range(4):
                    ins = e.transpose(pbf[:, hm * 128:(hm + 1) * 128], src_bf[:, mt, hm * 128:(hm + 1) * 128], self.idb[:])
                return ins
            S.op("pe", tr, reads=src_res, writes=[f"ps{b}"])
            S.op("dve", (lambda e, mt=mt, pbf=pbf: e.tensor_copy(out=memKT[:, l, :, mt * 128:(mt + 1) * 128],
                                                                in_=pbf.rearrange("p (h k) -> p h k", h=4))),
                 reads=[f"ps{b}"], writes=[f"mKT{l}"])

    def load_mem_cache(self, l, s):
        S = self.S
        kst = self.Hflat(8, 2).rearrange("p (m d) -> p m d", m=2)
        S.op("pool", (lambda e: e.dma_start(out=kst, in_=self.c_mk[l, s].rearrange("(m p) d -> p m d", p=128))),
             reads=[], writes=rn("H", 8, 2), dma="mk")
        S.op("pool", (lambda e: e.dma_start(out=self.memV[:, l], in_=self.c_mv[l, s].rearrange("(m p) d -> p m d", p=128))),
             reads=[], writes=[f"mV{l}"], dma="mv")
        self.kT_from_tokmajor(l, kst, rn("H", 8, 2))

    def mem_project(self, b_):
        S = self.S
        ps, H, ss, GM, tmpf = self.ps, self.H, self.ss, self.GM, self.tmpf
        xs = self.Hf(0, 16).rearrange("p (s d) -> p s d", s=2)
        junk = self.Hflat(16, 4)
        mn = self.Hflat(20, 8).rearrange("p (s d) -> p s d", s=2)
        mnT = self.Hflat(32, 8).rearrange("p (c t) -> p c t", c=DC)
        mT = self.Hflat(40, 8).rearrange("p (c t) -> p c t", c=DC)
        S.op("dve", lambda e: e.memset(ss[:, 0:2], 0.0), writes=["ss0", "ss1"])
        for s in range(2):
            S.op("sp", (lambda e, s=s: e.dma_start(out=xs[:, s, :], in_=self.mem_p[b_, s * 128:(s + 1) * 128, :])),
                 reads=[], writes=rn("H", 8 * s, 8), dma=f"xl{s}")
            S.op("act", (lambda e, s=s: e.activation(out=junk, in_=xs[:, s, :], func=AF.Square, accum_out=ss[:, s:s + 1])),
                 reads=rn("H", 8 * s, 8), writes=rn("H", 16, 4) + [f"ss{s}"])
            S.op("act", (lambda e, s=s: e.activation(out=ss[:, 2 + s:3 + s], in_=ss[:, s:s + 1], func=AF.Sqrt, scale=1.0 / D, bias=self.eps[:, 0:1])),
                 reads=[f"ss{s}"], writes=[f"ss{2 + s}"])
            S.op("dve", (lambda e, s=s: e.reciprocal(out=ss[:, 4 + s:5 + s], in_=ss[:, 2 + s:3 + s])), reads=[f"ss{2 + s}"], writes=[f"ss{4 + s}"])
            S.op("dve", (lambda e, s=s: e.tensor_scalar(out=mn[:, s, :], in0=xs[:, s, :], scalar1=ss[:, 4 + s:5 + s], scalar2=0.0, op0=ALU.mult, op1=ALU.add)),
                 reads=rn("H", 8 * s, 8) + [f"ss{4 + s}"], writes=rn("H", 20 + 4 * s, 4))
            if self.sub < 2:
                continue
            for c4 in range(4):
                b = self.bank()
                pbf = ps[:, b, 0:256].bitcast(BF16)

                def tr(e, s=s, c4=c4, pbf=pbf):
                    ins = None
                    for cc in range(4):
                        c = c4 * 4 + cc
                        ins = e.transpose(pbf[:, cc * 128:(cc + 1) * 128], mn[:, s, c * 128:(c + 1) * 128], self.idb[:])
                    return ins
                S.op("pe", tr, reads=rn("H", 20 + 4 * s, 4), writes=[f"ps{b}"])
                S.op("dve", (lambda e, s=s, c4=c4, pbf=pbf: e.tensor_copy(out=mnT[:, c4 * 4:c4 * 4 + 4, s * 128:(s + 1) * 128],
                                                                         in_=pbf.rearrange("p (c t) -> p c t", c=4))),
                     reads=[f"ps{b}"], writes=rn("H", 32, 8))
        if self.sub < 3:
            return
        for l in range(2):
            for c in range(DC):
                S.op("dve", (lambda e, c=c, l=l: e.tensor_scalar(out=mT[:, c, :], in0=mnT[:, c, :], scalar1=GM[:, l, c:c + 1], scalar2=0.0, op0=ALU.mult, op1=ALU.add)),
                     reads=rn("H", 32, 8), writes=rn("H", 40, 8))
            kb = self.Hflat(16, 2).rearrange("p (m d) -> p m d", m=2)
            for ub_ in range(4):
                if self.sub < 4:
                    continue
                u, ures = self.unit(f"mkv{l}_{ub_}", DC, 256, [(self.w_mkv[l][:, ub_ * 256:(ub_ + 1) * 256], 0)])
                for s in range(2):
                    if self.sub < 5:
                        continue
                    b = self.bank()
                    self.mm_group(ps[:, b, 0:256], [(mT[:, k, s * 128:(s + 1) * 128], u[:, k, :]) for k in range(DC)],
                                  reads=ures + rn("H", 40, 8), writes=[f"ps{b}"])
                    ti = self.tmp_i
                    self.tmp_i = (ti + 1) % 3
                    S.op("act", (lambda e, b=b, ti=ti: e.activation(out=tmpf[:, ti, 0:256], in_=ps[:, b, 0:256], func=AF.Copy)),
                         reads=[f"ps{b}"], writes=[f"tmpf{ti}"])
                    if ub_ < 2:
                        dst = self.o_mk_p[l, b_, s * 128:(s + 1) * 128, ub_ * 256:(ub_ + 1) * 256]
                        bdst = kb[:, s, ub_ * 256:(ub_ + 1) * 256]
                        bres = [f"H{16 + s}"]
                    else:
                        dst = self.o_mv_p[l, b_, s * 128:(s + 1) * 128, (ub_ - 2) * 256:(ub_ - 1) * 256]
                        bdst = self.memV[:, l, s, (ub_ - 2) * 256:(ub_ - 1) * 256]
                        bres = [f"mV{l}"]
                    if self.sub >= 6:
                        S.op("act", (lambda e, dst=dst, ti=ti: e.dma_start(out=dst, in_=tmpf[:, ti, 0:256])),
                             reads=[f"tmpf{ti}"], writes=[], dma=f"ot{ti}")
                    S.op("dve", (lambda e, bdst=bdst, b=b: e.tensor_copy(out=bdst, in_=ps[:, b, 0:256])),
                         reads=[f"ps{b}"], writes=bres)
            if self.sub >= 7:
                self.kT_from_tokmajor(l, kb, rn("H", 16, 2))

    def load_x(self, tile):
        S = self.S
        ps, xres = self.ps, self.xres
        nt = tile["nt"]
        nsub = nt // 128
        for s in range(nsub):
            xs = self.Hf(8 * s, 8)
            if tile["kind"] == "p":
                src = self.x_p[tile["b"], tile["t"] * 512 + s * 128: tile["t"] * 512 + (s + 1) * 128, :]
            else:
                src = self.x_s[s * 128:(s + 1) * 128, :]
            S.op("sp", (lambda e, xs=xs, src=src: e.dma_start(out=xs, in_=src)), reads=[], writes=rn("H", 8 * s, 8), dma=f"xl{s}")
            for c4 in range(4):
                b = self.bank()

                def tr(e, xs=xs, c4=c4, b=b):
                    ins = None
                    for cc in range(4):
                        c = c4 * 4 + cc
                        ins = e.transpose(ps[:, b, cc * 128:(cc + 1) * 128], xs[:, c * 128:(c + 1) * 128], self.idf[:])
                    return ins
                S.op("pe", tr, reads=rn("H", 8 * s, 8), writes=[f"ps{b}"])
                eng = "act" if c4 % 2 else "dve"
                dst = xres[:, c4 * 4:c4 * 4 + 4, s * 128:(s + 1) * 128]
                src3 = ps[:, b, :].rearrange("p (c t) -> p c t", c=4)
                if eng == "act":
                    S.op("act", (lambda e, dst=dst, src3=src3: e.activation(out=dst, in_=src3, func=AF.Copy)),
                         reads=[f"ps{b}"], writes=rn("xres", c4 * 4, 4))
                else:
                    S.op("dve", (lambda e, dst=dst, src3=src3: e.tensor_copy(out=dst, in_=src3)),
                         reads=[f"ps{b}"], writes=rn("xres", c4 * 4, 4))

    def store_y(self, tile):
        S = self.S
        ps, xres = self.ps, self.xres
        nt = tile["nt"]
        nsub = nt // 128
        for s in range(nsub):
            ys = self.Hf(8 * s, 8)
            for c4 in range(4):
                b = self.bank()

                def tr(e, c4=c4, b=b, s=s):
                    ins = None
                    for cc in range(4):
                        c = c4 * 4 + cc
                        ins = e.transpose(ps[:, b, cc * 128:(cc + 1) * 128], xres[:, c, s * 128:(s + 1) * 128], self.idf[:])
                    return ins
                S.op("pe", tr, reads=rn("xres", c4 * 4, 4), writes=[f"ps{b}"])
                dst = ys[:, c4 * 512:(c4 + 1) * 512]
                if c4 % 2:
                    S.op("act", (lambda e, dst=dst, b=b: e.activation(out=dst, in_=ps[:, b, :], func=AF.Copy)),
                         reads=[f"ps{b}"], writes=rn("H", 8 * s + 2 * c4, 2))
                else:
                    S.op("dve", (lambda e, dst=dst, b=b: e.tensor_copy(out=dst, in_=ps[:, b, :])),
                         reads=[f"ps{b}"], writes=rn("H", 8 * s + 2 * c4, 2))
            if tile["kind"] == "p":
                dstd = self.y_p[tile["b"], tile["t"] * 512 + s * 128: tile["t"] * 512 + (s + 1) * 128, :]
            else:
                dstd = self.y_s[s * 128:(s + 1) * 128, :]
            S.op("act", (lambda e, ys=ys, dstd=dstd: e.dma_start(out=dstd, in_=ys)), reads=rn("H", 8 * s, 8), writes=[], dma=f"ys{s}")

    def prologue(self):
        S = self.S
        nc = self.nc
        S.op("sp", lambda e: e.dma_start(out=self.idf[:], in_=self.ident_in[:, :]), writes=["idf"], dma="c0")
        S.op("dve", lambda e: e.tensor_copy(out=self.idb[:], in_=self.idf[:]), reads=["idf"], writes=["idb"])
        S.op("dve", lambda e: e.memset(self.ones[:], 1.0), writes=["ones"])
        S.op("dve", lambda e: e.memset(self.eps[:], EPS), writes=["eps"])
        S.op("dve", lambda e: e.memset(self.uh[:].rearrange("p a b c -> p (a b c)"), 0.0), writes=rn("uh", 0, DC))
        for l in range(2):
            for n in range(8):
                S.op("sp", (lambda e, l=l, n=n: e.dma_start(out=self.G[:, l * 8 + n, :], in_=self.g_norm[l, n].rearrange("(c p) -> p c", p=128),
                                                           allow_slow_non_contiguous=True)), writes=[f"G{l * 8 + n}"], dma="c1")
            S.op("sp", (lambda e, l=l: e.dma_start(out=self.GM[:, l, :], in_=self.g_mem[l].rearrange("(c p) -> p c", p=128),
                                                  allow_slow_non_contiguous=True)), writes=[f"GM{l}"], dma="c1")
        for k in range(3):
            S.op("sp", (lambda e, k=k: e.dma_start(out=self.CW[:, k, :], in_=self.w_cdw[k].rearrange("(c p) -> p c", p=128),
                                                  allow_slow_non_contiguous=True)), writes=[f"CW{k}"], dma="c1")
        allc = [f"G{i}" for i in range(16)] + ["GM0", "GM1", "CW0", "CW1", "CW2"]
        self.join("dve", allc)
        for gi in (1, 7, 9, 15):
            S.op("dve", (lambda e, gi=gi: e.tensor_scalar(out=self.G[:, gi, :], in0=self.G[:, gi, :], scalar1=0.5, scalar2=0.0, op0=ALU.mult, op1=ALU.add)),
                 reads=[f"G{gi}"], writes=[f"G{gi}"])
        S.op("dve", lambda e: e.memset(self.ss[:], 0.0), reads=allc + ["ones", "eps", "idb", "idf"], writes=["consts"] + [f"ss{i}" for i in range(8)])
        for eng in ("pe", "act", "pool"):
            S.op(eng, lambda e: e.nop(), reads=["consts"])
        if self.stage < 2:
            return
        VPB = self.Hf(0, 48).rearrange("p (h l) -> p h l", h=16)
        S.op("sp", lambda e: e.dma_start(out=VPB[:, :, 0:257], in_=bass.AP(self.relb.tensor, 0, [[0, 128], [257, 16], [1, 257]])),
             writes=rn("H", 0, 48), dma="c2")
        S.op("dve", lambda e: e.tensor_copy(out=VPB[:, :, 257:768], in_=VPB[:, :, 256:257].broadcast_to([128, 16, 511])),
             reads=rn("H", 0, 48), writes=["vpb"])
        S.op("sp", lambda e: e.dma_start(out=bass.AP(self.vp_d, 0, [[16 * 768, 128], [1, 16 * 768]]), in_=self.Hf(0, 48)),
             reads=["vpb"] + rn("H", 0, 48), writes=["vpd"], dma="c3")
        WBA = self.xres[:].rearrange("p a b -> p (a b)")[:, 0:8 * 640].rearrange("p (h c) -> p h c", h=8)
        for half in range(2):
            S.op("sp", (lambda e, half=half: e.dma_start(out=WBA, in_=bass.AP(self.vp_d, half * 8 * 768 + 128,
                                                                             [[16 * 768 - 1, 128], [768, 8], [1, 640]]))),
                 reads=["vpd"], writes=["wba"], dma="c4")
            S.op("dve", lambda e: e.memset(WBA[64:128, :, 0:64], NEG), reads=["wba"], writes=["wba"])
            S.op("dve", lambda e: e.memset(WBA[0:64, :, 576:640], NEG), reads=["wba"], writes=["wba"])
            S.op("sp", (lambda e, half=half: e.dma_start(out=self.wb_d[half * 8:(half + 1) * 8].rearrange("h p c -> p h c"), in_=WBA)),
                 reads=["wba"], writes=["wbd"], dma="c5")
        S.op("dve", lambda e: e.nop(), reads=["wbd", "wba", "vpb"], writes=rn("xres", 0, DC) + rn("H", 0, 48))

    def build(self):
        self.dram()
        self.sbuf()
        S = self.S
        self.prologue()
        tiles = []
        for b in range(NPS):
            for t in range(4):
                tiles.append(dict(kind="p", b=b, t=t, nt=512, nseg=1))
        tiles.append(dict(kind="s", nt=256, nseg=NSS))
        if self.cfg_tiles is not None:
            tiles = [tiles[i] for i in self.cfg_tiles]
        for ti_, tile in enumerate(tiles):
            self.pool_ok = ti_ > 0
            nt = tile["nt"]
            if self.stage < 3:
                break
            if tile["kind"] == "p" and tile["t"] == 0:
                self.mem_project(tile["b"])
                S.op("dve", lambda e: e.memset(self.uh[:].rearrange("p a b c -> p (a b c)"), 0.0), writes=rn("uh", 0, DC))
            if tile["kind"] == "s":
                self.join("dve", rn("uh", 0, DC))
                for c in range(DC):
                    S.op("sp", (lambda e, c=c: e.dma_start(out=self.uh[:, c, :, :],
                                                          in_=self.st_conv[:, :, c * 128:(c + 1) * 128].rearrange("s t p -> p s t"),
                                                          allow_slow_non_contiguous=True)),
                         writes=[f"uh{c}"], dma="hs")
                self.join("dve", rn("uh", 0, DC))
            if self.stage < 4:
                break
            self.load_x(tile)
            for l in range(self.nlayers):
                self.ffn(l, 1, nt)
                if l == 0:
                    self.conv(tile)
                else:
                    self.attn(tile)
                self.mem_attn(l, tile)
                self.ffn(l, 2, nt)
            self.store_y(tile)
        finals = [(k, 16 * v) for k, v in S.dcnt.items()]
        nc = self.nc
        sems = {}
        for k in list(S.dcnt.keys()):
            sems["d:" + k] = self.es.enter_context(nc.semaphore("d_" + k))
        for k in ("pe", "act", "dve", "pool"):
            sems["e:" + k] = self.es.enter_context(nc.semaphore("e_" + k))
        engmap = {"pe": "tensor", "act": "scalar", "dve": "vector", "pool": "gpsimd", "sp": "sync"}
        with nc.Block() as block:
            for ek, attr in engmap.items():
                items = S.q[ek]

                def body(e, items=items, ek=ek):
                    for waits, fn, tok in items:
                        for sk, v in waits:
                            e.wait_ge(sems[sk], v)
                        ins = fn(e)
                        ins.then_inc(sems[tok[0]], 16 if tok[0].startswith("d:") else 1)
                    if ek == "sp":
                        for k, v in finals:
                            e.wait_ge(sems["d:" + k], v)
                getattr(block, attr)(body)
        return nc


_CACHE = {}


def kernel(x_prompt, x_sample, state_conv, cache_band_k, cache_band_v, cache_mem_k, cache_mem_v,
           mem_prompt, g_norm, g_mem, w_ffn1_in, w_ffn1_out, w_ffn2_in, w_ffn2_out,
           w_conv_in, w_conv_dw, w_conv_out, w_attn_qkv, rel_bias, w_attn_o,
           w_mem_q, w_mem_kv, w_mem_o):
    f = lambda a: np.ascontiguousarray(np.asarray(a), dtype=np.float32)
    B = Builder()
    nc = B.build()
    x_prompt, x_sample = f(x_prompt), f(x_sample)
    shared = dict(
        g_norm=f(g_norm), g_mem=f(g_mem), w_f1i=f(w_ffn1_in), w_f1o=f(w_ffn1_out), w_f2i=f(w_ffn2_in), w_f2o=f(w_ffn2_out),
        w_ci=f(w_conv_in)[0], w_cdw=f(w_conv_dw)[0], w_co=f(w_conv_out)[0], w_qkv=f(w_attn_qkv)[0], relb=f(rel_bias)[0],
        w_ao=f(w_attn_o)[0], w_mq=f(w_mem_q), w_mkv=f(w_mem_kv), w_mo=f(w_mem_o),
        ident_in=np.eye(128, dtype=np.float32),
    )
    state_conv, cache_band_k, cache_band_v = f(state_conv), f(cache_band_k), f(cache_band_v)
    cache_mem_k, cache_mem_v, mem_prompt = f(cache_mem_k), f(cache_mem_v), f(mem_prompt)
    in_maps = []
    for i in range(NCORES):
        ps_, ss_ = slice(NPS * i, NPS * (i + 1)), slice(NSS * i, NSS * (i + 1))
        m = dict(shared)
        m["x_p"] = x_prompt[ps_]
        m["x_s"] = x_sample[ss_].reshape(NSS * LS, D)
        m["st_conv"] = state_conv[0, ss_]
        m["c_bk"] = cache_band_k[0, ss_].reshape(NSS, 512, D)
        m["c_bv"] = cache_band_v[0, ss_].reshape(NSS, 512, D)
        m["c_mk"] = cache_mem_k[:, ss_].reshape(2, NSS, NMEM, MW)
        m["c_mv"] = cache_mem_v[:, ss_].reshape(2, NSS, NMEM, MW)
        m["mem_p"] = mem_prompt[ps_]
        in_maps.append({k: np.ascontiguousarray(v) for k, v in m.items()})
    res = run_bass_kernel_spmd(nc, in_maps, core_ids=list(range(NCORES)))
    R = res.results
    cat = lambda k, ax=0: np.concatenate([np.asarray(r[k]) for r in R], axis=ax)
    y_prompt = cat("y_p")
    y_sample = cat("y_s").reshape(NCORES * NSS, LS, D)
    conv_p = cat("o_conv_p")[None]
    bk_p = cat("o_bk_p").reshape(1, NCORES * NPS, 512, 16, 128)
    bv_p = cat("o_bv_p").reshape(1, NCORES * NPS, 512, 16, 128)
    mk_p = cat("o_mk_p", 1).reshape(2, NCORES * NPS, NMEM, 4, 128)
    mv_p = cat("o_mv_p", 1).reshape(2, NCORES * NPS, NMEM, 4, 128)
    conv_s = cat("o_conv_s")[None]
    bk_s = cat("o_bk_s").reshape(1, NCORES * NSS, LS, 16, 128)
    bv_s = cat("o_bv_s").reshape(1, NCORES * NSS, LS, 16, 128)
    return (y_prompt, y_sample, conv_p, bk_p, bv_p, mk_p, mv_p, conv_s, bk_s, bv_s)
```

```python
import numpy as np
from contextlib import ExitStack
import concourse.bass as bass
import concourse.mybir as mybir
from concourse.bass_utils import run_bass_kernel_spmd

F32 = mybir.dt.float32
BF16 = mybir.dt.bfloat16
AF = mybir.ActivationFunctionType
ALU = mybir.AluOpType

NCORES = 8
D = 2048
DC = 16
DFF = 5632
FC = 44
SEQ = 2048
NPS = 2
NSS = 4
LS = 64
NMEM = 256
MW = 512
EPS = 1e-6
RING = 18432
NWSEM = 8
SCALE = 128 ** -0.5
NEG = -1e30


class Sched:
    ENG = ("pe", "act", "dve", "pool", "sp")

    def __init__(self):
        self.q = {k: [] for k in self.ENG}
        self.ecnt = {k: 0 for k in self.ENG}
        self.dcnt = {}
        self.seen = {k: {} for k in self.ENG}
        self.W = {}
        self.R = {}

    def op(self, eng, fn, reads=(), writes=(), dma=None):
        waits = {}
        isdma = dma is not None
        writes = list(writes) + [r for r in reads if r.startswith("ps") and r not in writes]

        def need(tok, kind):
            sk, val, peng = tok
            if not isdma and not sk.startswith("d:") and peng == eng:
                if eng == "pe":
                    return
            if self.seen[eng].get(sk, 0) >= val:
                return
            if waits.get(sk, 0) < val:
                waits[sk] = val

        for r in reads:
            t = self.W.get(r)
            if t is not None:
                need(t, "raw")
        for w in writes:
            t = self.W.get(w)
            if t is not None:
                need(t, "waw")
            for sk, (v, pe) in self.R.get(w, {}).items():
                need((sk, v, pe), "war")
        for sk, v in waits.items():
            self.seen[eng][sk] = v
        if isdma:
            self.dcnt[dma] = self.dcnt.get(dma, 0) + 1
            tok = ("d:" + dma, 16 * self.dcnt[dma], eng)
        else:
            self.ecnt[eng] += 1
            tok = ("e:" + eng, self.ecnt[eng], eng)
        for r in reads:
            self.R.setdefault(r, {})[tok[0]] = (tok[1], eng)
        for w in writes:
            self.W[w] = tok
            self.R[w] = {}
        self.q[eng].append((list(waits.items()), fn, tok))
        return tok


def rn(prefix, a, n=1):
    return [f"{prefix}{i}" for i in range(a, a + n)]


class Builder:
    def __init__(self, tiles=None, nlayers=2, stage=99):
        self.stage = stage
        import os as _os
        self.sub = int(_os.environ.get("KSUB", "99"))
        self.nc = bass.Bass("TRN2", target_bir_lowering=False)
        self.cfg_tiles = tiles
        self.nlayers = nlayers
        self.S = Sched()
        self.es = ExitStack()
        self.bank_i = 0
        self.pool_ok = False
        self.alt = 0
        self.uidx = 0
        self.ring_ptr = 0
        self.live = []
        self.unit_off = {}
        self.scr_ptr = 0
        self.tmp_i = 0
        self.sq_i = 0
        self.sg_i = 0
        self.ub_i = 0
        self.sct_i = 0

    def dram(self):
        nc = self.nc
        I = lambda n, s: nc.dram_tensor(n, s, F32, kind="ExternalInput").ap()
        O = lambda n, s: nc.dram_tensor(n, s, F32, kind="ExternalOutput").ap()
        self.x_p = I("x_p", [NPS, SEQ, D])
        self.x_s = I("x_s", [NSS * LS, D])
        self.st_conv = I("st_conv", [NSS, 2, D])
        self.c_bk = I("c_bk", [NSS, 512, D])
        self.c_bv = I("c_bv", [NSS, 512, D])
        self.c_mk = I("c_mk", [2, NSS, NMEM, MW])
        self.c_mv = I("c_mv", [2, NSS, NMEM, MW])
        self.mem_p = I("mem_p", [NPS, NMEM, D])
        self.g_norm = I("g_norm", [2, 8, D])
        self.g_mem = I("g_mem", [2, D])
        self.w_f1i = I("w_f1i", [2, D, 2 * DFF])
        self.w_f1o = I("w_f1o", [2, DFF, D])
        self.w_f2i = I("w_f2i", [2, D, 2 * DFF])
        self.w_f2o = I("w_f2o", [2, DFF, D])
        self.w_ci = I("w_ci", [D, 3 * D])
        self.w_cdw = I("w_cdw", [3, D])
        self.w_co = I("w_co", [D, D])
        self.w_qkv = I("w_qkv", [D, 3 * D])
        self.relb = I("relb", [16, 257])
        self.w_ao = I("w_ao", [D, D])
        self.w_mq = I("w_mq", [2, D, MW])
        self.w_mkv = I("w_mkv", [2, D, 2 * MW])
        self.w_mo = I("w_mo", [2, MW, D])
        self.ident_in = I("ident_in", [128, 128])
        self.y_p = O("y_p", [NPS, SEQ, D])
        self.y_s = O("y_s", [NSS * LS, D])
        self.o_conv_p = O("o_conv_p", [NPS, 2, D])
        self.o_bk_p = O("o_bk_p", [NPS, 512, D])
        self.o_bv_p = O("o_bv_p", [NPS, 512, D])
        self.o_mk_p = O("o_mk_p", [2, NPS, NMEM, MW])
        self.o_mv_p = O("o_mv_p", [2, NPS, NMEM, MW])
        self.o_conv_s = O("o_conv_s", [NSS, 2, D])
        self.o_bk_s = O("o_bk_s", [NSS * LS, D])
        self.o_bv_s = O("o_bv_s", [NSS * LS, D])
        per_layer = 2 * (DC * 2 * DFF + DFF * D) + (DC * 3 * D + D * D // 8) * 0
        tot = 2 * (2 * (D * 2 * DFF + DFF * D) + D * MW + D * 2 * MW + MW * D) + 2 * (D * 3 * D + D * D)
        self.SCR_EL = 60 * 1024 * 1024
        self.wscr = [nc.dram_tensor(f"wscr{i}", [self.SCR_EL // 128, 128], BF16, kind="Internal") for i in range(4)]
        self.scr_t = 0
        self.vp_d = nc.dram_tensor("vp_d", [128, 16 * 768], F32, kind="Internal")
        self.wb_d = nc.dram_tensor("wb_d", [16, 128, 640], F32, kind="Internal").ap()

    def sb(self, name, shape, dt):
        return self.es.enter_context(self.nc.sbuf_tensor(name, shape, dt))

    def sbuf(self):
        self.xres = self.sb("xres", [128, DC, 512], F32)
        self.xn = self.sb("xn", [128, DC, 512], BF16)
        self.H = self.sb("H", [128, 48, 512], BF16)
        self.kTp = self.sb("kTp", [128, 16, 512], BF16)
        self.Vp = self.sb("Vp", [128, 4, 2048], BF16)
        self.ring = self.sb("ring", [128, RING], BF16)
        self.WBt = self.sb("WBt", [128, 640], F32)
        self.WBt2 = self.sb("WBt2", [128, 640], F32)
        self.PT = self.sb("PT", [128, 2560], BF16)
        self.sct = self.sb("sct", [128, 640], F32)
        self.sq = self.sb("sq", [128, 2, 512], BF16)
        self.rs = self.sb("rs", [128, 2, 512], F32)
        self.tmpf = self.sb("tmpf", [128, 3, 512], F32)
        self.sg = self.sb("sg", [128, 2, 512], BF16)
        self.CA = self.sb("CA", [128, 1568], F32)
        self.ub = self.CA[:, 0:1056].rearrange("p (a b) -> p a b", a=2)
        self.yt = self.CA[:, 1056:1568]
        self.PT2 = self.CA[:, 0:1280].bitcast(BF16)
        self.uh = self.sb("uh", [128, DC, 4, 2], F32)
        self.memKT = self.sb("memKT", [128, 2, 4, 256], BF16)
        self.memV = self.sb("memV", [128, 2, 2, 512], BF16)
        self.G = self.sb("G", [128, 16, DC], F32)
        self.GM = self.sb("GM", [128, 2, DC], F32)
        self.CW = self.sb("CW", [128, 3, DC], F32)
        self.idf = self.sb("idf", [128, 128], F32)
        self.idb = self.sb("idb", [128, 128], BF16)
        self.ones = self.sb("ones", [128, 128], BF16)
        self.eps = self.sb("eps", [128, 1], F32)
        self.ss = self.sb("ss", [128, 8], F32)
        self.ps = self.es.enter_context(self.nc.psum_tensor("ps", [128, 8, 512], F32))

    def bank(self):
        b = self.bank_i
        self.bank_i = (b + 1) % 7
        return b

    def ve(self):
        if not self.pool_ok:
            return "dve"
        self.alt ^= 1
        return "pool" if self.alt else "dve"

    def join(self, eng, res):
        self.S.op(eng, lambda e: e.nop(), reads=list(res), writes=list(res))

    def Hf(self, a, n):
        return self.H[:, a:a + n, :].rearrange("p a b -> p (a b)").bitcast(F32)

    def Hflat(self, a, n):
        return self.H[:, a:a + n, :].rearrange("p a b -> p (a b)")

    def unit(self, key, kc, nw, parts):
        S = self.S
        U = kc * nw
        if self.ring_ptr + U > RING:
            self.ring_ptr = 0
        a, b = self.ring_ptr, self.ring_ptr + U
        self.ring_ptr = b
        over = [l for l in self.live if l[0] < b and l[1] > a]
        self.live = [l for l in self.live if not (l[0] < b and l[1] > a)]
        ui = self.uidx
        self.uidx += 1
        res = f"wu{ui}"
        oldres = [r for l in over for r in l[2]]
        view = self.ring[:, a:b].rearrange("p (k n) -> p k n", k=kc)
        first = key not in self.unit_off
        if first:
            if self.scr_ptr + U * 128 > self.SCR_EL:
                self.scr_t += 1
                self.scr_ptr = 0
            off = (self.scr_t, self.scr_ptr)
            self.unit_off[key] = off
            self.scr_ptr += U * 128
            myres = []
            for pi, (src, co) in enumerate(parts):
                ncol = src.shape[1]
                r = f"{res}p{pi}"
                myres.append(r)
                dst = view[:, :, co:co + ncol]
                srcv = src.rearrange("(k p) n -> p k n", p=128)
                S.op("pool", (lambda e, dst=dst, srcv=srcv: e.dma_start(out=dst, in_=srcv)),
                     writes=[r] + (oldres if pi == 0 else []), dma=f"pw{ui % NWSEM}")
            scr = bass.AP(self.wscr[off[0]], off[1], [[U, 128], [1, U]])
            flat = self.ring[:, a:b]
            S.op("sp", (lambda e, scr=scr, flat=flat: e.dma_start(out=scr, in_=flat)),
                 reads=myres, writes=[f"scr{key}"], dma=f"s{ui % NWSEM}")
        else:
            off = self.unit_off[key]
            scr = bass.AP(self.wscr[off[0]], off[1], [[U, 128], [1, U]])
            flat = self.ring[:, a:b]
            myres = [res]
            S.op("sp", (lambda e, scr=scr, flat=flat: e.dma_start(out=flat, in_=scr)),
                 reads=[f"scr{key}"], writes=[res] + oldres, dma=f"w{ui % NWSEM}")
        self.live.append((a, b, myres))
        return view, myres

    def mm_group(self, out_ap, pairs, reads, writes):
        n = len(pairs)

        def fn(e, out_ap=out_ap, pairs=pairs, n=n):
            ins = None
            for i, (l, r) in enumerate(pairs):
                ins = e.matmul(out_ap, lhsT=l, rhs=r, start=(i == 0), stop=(i == n - 1))
            return ins
        return self.S.op("pe", fn, reads=reads, writes=writes)

    def mm_kouter(self, groups, kc, rhs_of_k, rhs_res_of_k):
        allreads = sorted(set(r for g in groups for r in g[2]))
        banks = [f"ps{g[3]}" for g in groups]
        for k in range(kc):
            def fn(e, k=k):
                ins = None
                for (out_ap, lf, _, _) in groups:
                    ins = e.matmul(out_ap, lhsT=lf(k), rhs=rhs_of_k(k), start=(k == 0), stop=(k == kc - 1))
                return ins
            self.S.op("pe", fn, reads=[rhs_res_of_k(k)] + allreads, writes=banks)

    def sums_to_rstd(self, bank, nt, scale_mean):
        S = self.S
        ps, rs, eps = self.ps, self.rs, self.eps
        S.op("act", lambda e: e.activation(out=rs[:, 1, 0:nt], in_=ps[:, bank, 0:nt], func=AF.Ln,
                                           scale=scale_mean, bias=eps[:, 0:1]),
             reads=[f"ps{bank}"], writes=["rs1"])
        S.op("act", lambda e: e.activation(out=rs[:, 1, 0:nt], in_=rs[:, 1, 0:nt], func=AF.Exp, scale=-0.5),
             reads=["rs1"], writes=["rs1"])

    def prenorm(self, gi, nt):
        S = self.S
        xres, xn, sq, ones, ps, G, rs = self.xres, self.xn, self.sq, self.ones, self.ps, self.G, self.rs
        bank = 7
        for c in range(DC):
            si = self.sq_i
            self.sq_i ^= 1
            S.op("act", (lambda e, c=c, si=si: e.activation(out=sq[:, si, 0:nt], in_=xres[:, c, 0:nt], func=AF.Square)),
                 reads=[f"xres{c}"], writes=[f"sq{si}"])
            S.op("pe", (lambda e, c=c, si=si: e.matmul(ps[:, bank, 0:nt], lhsT=ones[:], rhs=sq[:, si, 0:nt],
                                                      start=(c == 0), stop=(c == DC - 1))),
                 reads=[f"sq{si}"], writes=[f"ps{bank}"])
        self.sums_to_rstd(bank, nt, 1.0 / D)
        for c in range(DC):
            eng = "dve"
            S.op(eng, (lambda e, c=c: e.scalar_tensor_tensor(out=xn[:, c, 0:nt], in0=xres[:, c, 0:nt],
                                                            scalar=G[:, gi, c:c + 1], in1=rs[:, 1, 0:nt],
                                                            op0=ALU.mult, op1=ALU.mult)),
                 reads=[f"xres{c}", "rs1"], writes=[f"xn{c}"])

    def post_begin(self, gi):
        self.post_bank = 7
        self.post_cnt = 0
        self.post_gi = gi
        self.post_pending = None

    def post_flush(self):
        if self.post_pending is not None:
            fn, si = self.post_pending
            self.S.op("pe", fn, reads=[f"sq{si}"], writes=[f"ps{self.post_bank}"])
            self.post_pending = None

    def post_chunk(self, j, bank, nt):
        S = self.S
        ps, xn, sq, ones = self.ps, self.xn, self.sq, self.ones
        pb = self.post_bank
        gi = self.post_gi
        self.post_flush()
        si = self.sq_i
        self.sq_i ^= 1
        S.op("act", (lambda e: e.activation(out=sq[:, si, 0:nt], in_=ps[:, bank, 0:nt], func=AF.Square)),
             reads=[f"ps{bank}"], writes=[f"sq{si}"])
        S.op("act", (lambda e: e.activation(out=xn[:, j, 0:nt], in_=ps[:, bank, 0:nt], func=AF.Copy, scale=self.G[:, gi, j:j + 1])),
             reads=[f"ps{bank}"], writes=[f"xn{j}"])
        first = self.post_cnt == 0
        last = self.post_cnt == DC - 1
        self.post_cnt += 1
        self.post_pending = ((lambda e: e.matmul(ps[:, pb, 0:nt], lhsT=ones[:], rhs=sq[:, si, 0:nt], start=first, stop=last)), si)

    def post_end(self, gi, nt):
        S = self.S
        xres, xn, G, rs, tmpf = self.xres, self.xn, self.G, self.rs, self.tmpf
        assert gi == self.post_gi
        self.post_flush()
        self.sums_to_rstd(self.post_bank, nt, 1.0 / D)
        pend = None
        for c in range(DC + 1):
            if c < DC:
                ti = self.tmp_i
                self.tmp_i = (ti + 1) % 3
                S.op("dve", (lambda e, c=c, ti=ti: e.tensor_tensor(out=tmpf[:, ti, 0:nt], in0=xn[:, c, 0:nt], in1=rs[:, 1, 0:nt], op=ALU.mult)),
                     reads=[f"xn{c}", "rs1"], writes=[f"tmpf{ti}"])
            if pend is not None:
                pc, pti = pend
                eng = "pool" if (self.pool_ok and pc % 4 == 3) else "dve"
                S.op(eng, (lambda e, pc=pc, pti=pti: e.tensor_tensor(out=xres[:, pc, 0:nt], in0=xres[:, pc, 0:nt],
                                                                    in1=tmpf[:, pti, 0:nt], op=ALU.add)),
                     reads=[f"xres{pc}", f"tmpf{pti}"], writes=[f"xres{pc}"])
            pend = (c, ti) if c < DC else None

    def ffn(self, l, which, nt):
        S = self.S
        w_in = (self.w_f1i if which == 1 else self.w_f2i)[l]
        w_out = (self.w_f1o if which == 1 else self.w_f2o)[l]
        gpre, gpost = (0, 1) if which == 1 else (6, 7)
        ps, xn, H, sg = self.ps, self.xn, self.H, self.sg
        self.prenorm(l * 8 + gpre, nt)
        NPRE = 3
        pre = []
        for c in range(NPRE):
            u, ures = self.unit(f"f{which}i{l}_{c}", DC, 256,
                                [(w_in[:, c * 128:(c + 1) * 128], 0), (w_in[:, DFF + c * 128:DFF + (c + 1) * 128], 128)])
            pre.append((u, ures, self.bank(), self.bank()))
        groups = []
        for (u, ures, bg, bu) in pre:
            groups.append((ps[:, bg, 0:nt], (lambda k, u=u: u[:, k, 0:128]), ures, bg))
            groups.append((ps[:, bu, 0:nt], (lambda k, u=u: u[:, k, 128:256]), ures, bu))
        self.mm_kouter(groups, DC, lambda k: xn[:, k, 0:nt], lambda k: f"xn{k}")
        for c in range(FC):
            if c < NPRE:
                u, ures, bg, bu = pre[c]
            else:
                u, ures = self.unit(f"f{which}i{l}_{c}", DC, 256,
                                    [(w_in[:, c * 128:(c + 1) * 128], 0), (w_in[:, DFF + c * 128:DFF + (c + 1) * 128], 128)])
                bg, bu = self.bank(), self.bank()
                self.mm_group(ps[:, bg, 0:nt], [(u[:, k, 0:128], xn[:, k, 0:nt]) for k in range(DC)],
                              reads=ures + rn("xn", 0, DC), writes=[f"ps{bg}"])
                self.mm_group(ps[:, bu, 0:nt], [(u[:, k, 128:256], xn[:, k, 0:nt]) for k in range(DC)],
                              reads=ures + rn("xn", 0, DC), writes=[f"ps{bu}"])
            si = self.sg_i
            self.sg_i ^= 1
            S.op("act", (lambda e, bg=bg, si=si: e.activation(out=sg[:, si, 0:nt], in_=ps[:, bg, 0:nt], func=AF.Silu)),
                 reads=[f"ps{bg}"], writes=[f"sg{si}"])
            S.op("dve", (lambda e, bu=bu, si=si, c=c: e.tensor_tensor(out=H[:, c, 0:nt], in0=sg[:, si, 0:nt],
                                                                     in1=ps[:, bu, 0:nt], op=ALU.mult)),
                 reads=[f"ps{bu}", f"sg{si}"], writes=[f"H{c}"])
        self.post_begin(l * 8 + gpost)
        for j in range(DC):
            u, ures = self.unit(f"f{which}o{l}_{j}", FC, 128, [(w_out[:, j * 128:(j + 1) * 128], 0)])
            b = self.bank()
            self.mm_group(ps[:, b, 0:nt], [(u[:, k, :], H[:, k, 0:nt]) for k in range(FC)],
                          reads=ures + rn("H", 0, FC), writes=[f"ps{b}"])
            self.post_chunk(j, b, nt)
        self.post_end(l * 8 + gpost, nt)

    def proj_out(self, key, w, kchunks, hslot0, gi, nt):
        ps, H = self.ps, self.H
        self.post_begin(gi)
        for ub_ in range(8):
            u, ures = self.unit(f"{key}_{ub_}", kchunks, 256, [(w[:, ub_ * 256:(ub_ + 1) * 256], 0)])
            for jj in range(2):
                j = ub_ * 2 + jj
                b = self.bank()
                self.mm_group(ps[:, b, 0:nt], [(u[:, k, jj * 128:(jj + 1) * 128], H[:, hslot0 + k, 0:nt]) for k in range(kchunks)],
                              reads=ures + rn("H", hslot0, kchunks), writes=[f"ps{b}"])
                self.post_chunk(j, b, nt)
        self.post_end(gi, nt)

    def conv(self, tile):
        S = self.S
        nt, nseg = tile["nt"], tile["nseg"]
        L = nt // nseg
        ps, xn, H, ub, uh, CW, tmpf, yt = self.ps, self.xn, self.H, self.ub, self.uh, self.CW, self.tmpf, self.yt
        self.prenorm(2, nt)
        w = self.w_ci
        NPRE = 2
        pre = []
        groups = []
        for c in range(NPRE):
            u, ures = self.unit(f"ci_{c}", DC, 384, [(w[:, c * 128:(c + 1) * 128], 0),
                                                    (w[:, D + c * 128:D + (c + 1) * 128], 128),
                                                    (w[:, 2 * D + c * 128:2 * D + (c + 1) * 128], 256)])
            bks = (self.bank(), self.bank(), self.bank())
            pre.append((u, ures, bks))
            for bnk, co in zip(bks, (0, 128, 256)):
                groups.append((ps[:, bnk, 0:nt], (lambda k, u=u, co=co: u[:, k, co:co + 128]), ures, bnk))
        self.mm_kouter(groups, DC, lambda k: xn[:, k, 0:nt], lambda k: f"xn{k}")
        for c in range(DC):
            if c < NPRE:
                u, ures, (bb, bc_, bx) = pre[c]
            else:
                u, ures = self.unit(f"ci_{c}", DC, 384, [(w[:, c * 128:(c + 1) * 128], 0),
                                                        (w[:, D + c * 128:D + (c + 1) * 128], 128),
                                                        (w[:, 2 * D + c * 128:2 * D + (c + 1) * 128], 256)])
                bb, bc_, bx = self.bank(), self.bank(), self.bank()
                for bnk, co in ((bb, 0), (bc_, 128), (bx, 256)):
                    self.mm_group(ps[:, bnk, 0:nt], [(u[:, k, co:co + 128], xn[:, k, 0:nt]) for k in range(DC)],
                                  reads=ures + rn("xn", 0, DC), writes=[f"ps{bnk}"])
            ui = self.ub_i
            self.ub_i ^= 1
            uv = ub[:, ui, 0:nseg * (L + 2)].rearrange("p (s l) -> p s l", s=nseg)
            S.op("dve", (lambda e, uv=uv, c=c: e.tensor_copy(out=uv[:, :, 0:2], in_=uh[:, c, 0:nseg, :])),
                 reads=[f"uh{c}"], writes=[f"ub{ui}h"])
            ti = self.tmp_i
            self.tmp_i = (ti + 1) % 3
            S.op("act", (lambda e, bc_=bc_, ti=ti: e.activation(out=tmpf[:, ti, 0:nt], in_=ps[:, bc_, 0:nt], func=AF.Copy)),
                 reads=[f"ps{bc_}"], writes=[f"tmpf{ti}"])
            t3 = tmpf[:, ti, 0:nt].rearrange("p (s l) -> p s l", s=nseg)
            px = ps[:, bx, 0:nt].rearrange("p (s l) -> p s l", s=nseg)
            S.op("dve", (lambda e, uv=uv, t3=t3, px=px: e.tensor_tensor(out=uv[:, :, 2:L + 2], in0=t3, in1=px, op=ALU.mult)),
                 reads=[f"tmpf{ti}", f"ps{bx}"], writes=[f"ub{ui}"])
            y3 = yt[:, 0:nt].rearrange("p (s l) -> p s l", s=nseg)
            eng = "dve"
            S.op(eng, (lambda e, uv=uv, y3=y3, c=c: e.tensor_scalar(out=y3, in0=uv[:, :, 0:L], scalar1=CW[:, 0, c:c + 1],
                                                                   scalar2=0.0, op0=ALU.mult, op1=ALU.add)),
                 reads=[f"ub{ui}", f"ub{ui}h"], writes=["yt"])
            S.op(eng, (lambda e, uv=uv, y3=y3, c=c: e.scalar_tensor_tensor(out=y3, in0=uv[:, :, 1:L + 1], scalar=CW[:, 1, c:c + 1],
                                                                          in1=y3, op0=ALU.mult, op1=ALU.add)),
                 reads=[f"ub{ui}", f"ub{ui}h", "yt"], writes=["yt"])
            S.op(eng, (lambda e, uv=uv, y3=y3, c=c: e.scalar_tensor_tensor(out=y3, in0=uv[:, :, 2:L + 2], scalar=CW[:, 2, c:c + 1],
                                                                          in1=y3, op0=ALU.mult, op1=ALU.add)),
                 reads=[f"ub{ui}", "yt"], writes=["yt"])
            S.op("dve", (lambda e, bb=bb, c=c: e.tensor_tensor(out=H[:, c, 0:nt], in0=yt[:, 0:nt], in1=ps[:, bb, 0:nt], op=ALU.mult)),
                 reads=["yt", f"ps{bb}"], writes=[f"H{c}"])
            S.op(self.ve(), (lambda e, uv=uv, c=c: e.tensor_copy(out=uh[:, c, 0:nseg, :], in_=uv[:, :, L:L + 2])),
                 reads=[f"ub{ui}"], writes=[f"uh{c}"])
        if tile["kind"] == "s" or tile["t"] == 3:
            for sgi in range(nseg):
                dst = (self.o_conv_s[sgi] if tile["kind"] == "s" else self.o_conv_p[tile["b"]])
                dstv = dst.rearrange("t (c p) -> p c t", p=128)
                for c in range(DC):
                    S.op("act", (lambda e, dstv=dstv, c=c, sgi=sgi: e.dma_start(out=dstv[:, c, :], in_=uh[:, c, sgi, :],
                                                                               allow_slow_non_contiguous=True)),
                         reads=[f"uh{c}"], writes=[], dma="oc")
        self.proj_out("co", self.w_co, DC, 0, 3, nt)

    def attn(self, tile):
        S = self.S
        nt, nseg, kind = tile["nt"], tile["nseg"], tile["kind"]
        ps, xn, H, PT, sct, WBt, kTp, Vp, ones, tmpf = self.ps, self.xn, self.H, self.PT, self.sct, self.WBt, self.kTp, self.Vp, self.ones, self.tmpf
        self.prenorm(8 + 2, nt)
        w = self.w_qkv
        want_out = kind == "s" or tile["t"] == 3
        nsub = nt // 128 if kind == "p" else nseg
        rows = 128 if kind == "p" else LS

        def qslot(h):
            return H[:, h, 0:nt]

        def kslot(h):
            return H[:, 16 + h, 0:nt] if kind == "p" else H[:, h, 256:256 + nt]

        def kres(h):
            return f"H{16 + h}" if kind == "p" else f"H{h}"

        def qres(h):
            return f"H{h}"

        def vcur(s):
            return self.Hflat(32 + 4 * s, 4)[0:rows, :]

        NPRE = 3
        preq = []
        groups = []
        for ub_ in range(NPRE):
            u, ures = self.unit(f"qkvq_{ub_}", DC, 256, [(w[:, ub_ * 256:(ub_ + 1) * 256], 0)])
            bks = (self.bank(), self.bank())
            preq.append((u, ures, bks))
            for jj in range(2):
                groups.append((ps[:, bks[jj], 0:nt], (lambda k, u=u, jj=jj: u[:, k, jj * 128:(jj + 1) * 128]), ures, bks[jj]))
        self.mm_kouter(groups, DC, lambda k: xn[:, k, 0:nt], lambda k: f"xn{k}")
        for which, slotf, resf, base in (("q", qslot, qres, 0), ("k", kslot, kres, D)):
            for ub_ in range(8):
                ispre = which == "q" and ub_ < NPRE
                if ispre:
                    u, ures, bks = preq[ub_]
                else:
                    u, ures = self.unit(f"qkv{which}_{ub_}", DC, 256, [(w[:, base + ub_ * 256:base + (ub_ + 1) * 256], 0)])
                for jj in range(2):
                    h = ub_ * 2 + jj
                    if ispre:
                        b = bks[jj]
                    else:
                        b = self.bank()
                        self.mm_group(ps[:, b, 0:nt], [(u[:, k, jj * 128:(jj + 1) * 128], xn[:, k, 0:nt]) for k in range(DC)],
                                      reads=ures + rn("xn", 0, DC), writes=[f"ps{b}"])
                    dst = slotf(h)
                    S.op("act", (lambda e, dst=dst, b=b: e.activation(out=dst, in_=ps[:, b, 0:nt], func=AF.Copy)),
                         reads=[f"ps{b}"], writes=[resf(h)])
                if which == "k" and want_out:
                    for s in range(nsub):
                        b = self.bank()
                        self.mm_group(ps[0:rows, b, 0:256], [(xn[:, k, s * rows:(s + 1) * rows], u[:, k, :]) for k in range(DC)],
                                      reads=ures + rn("xn", 0, DC), writes=[f"ps{b}"])
                        ti = self.tmp_i
                        self.tmp_i = (ti + 1) % 3
                        S.op("act", (lambda e, b=b, ti=ti: e.activation(out=tmpf[0:rows, ti, 0:256], in_=ps[0:rows, b, 0:256], func=AF.Copy)),
                             reads=[f"ps{b}"], writes=[f"tmpf{ti}"])
                        if kind == "p":
                            dst = self.o_bk_p[tile["b"], s * 128:(s + 1) * 128, ub_ * 256:(ub_ + 1) * 256]
                        else:
                            dst = self.o_bk_s[s * LS:(s + 1) * LS, ub_ * 256:(ub_ + 1) * 256]
                        S.op("act", (lambda e, dst=dst, ti=ti: e.dma_start(out=dst, in_=tmpf[0:rows, ti, 0:256])),
                             reads=[f"tmpf{ti}"], writes=[], dma=f"ot{ti}")
        for ub_ in range(8):
            u, ures = self.unit(f"qkvv_{ub_}", DC, 256, [(w[:, 2 * D + ub_ * 256:2 * D + (ub_ + 1) * 256], 0)])
            for s in range(nsub):
                b = self.bank()
                self.mm_group(ps[0:rows, b, 0:256], [(xn[:, k, s * rows:(s + 1) * rows], u[:, k, :]) for k in range(DC)],
                              reads=ures + rn("xn", 0, DC), writes=[f"ps{b}"])
                vdst = vcur(s)[:, ub_ * 256:(ub_ + 1) * 256]
                S.op("dve", (lambda e, vdst=vdst, b=b: e.tensor_copy(out=vdst, in_=ps[0:rows, b, 0:256])),
                     reads=[f"ps{b}"], writes=[f"Vc{s}_{ub_}"])
                if want_out:
                    ti = self.tmp_i
                    self.tmp_i = (ti + 1) % 3
                    S.op("act", (lambda e, b=b, ti=ti: e.activation(out=tmpf[0:rows, ti, 0:256], in_=ps[0:rows, b, 0:256], func=AF.Copy)),
                         reads=[f"ps{b}"], writes=[f"tmpf{ti}"])
                    if kind == "p":
                        dst = self.o_bv_p[tile["b"], s * 128:(s + 1) * 128, ub_ * 256:(ub_ + 1) * 256]
                    else:
                        dst = self.o_bv_s[s * LS:(s + 1) * LS, ub_ * 256:(ub_ + 1) * 256]
                    S.op("act", (lambda e, dst=dst, ti=ti: e.dma_start(out=dst, in_=tmpf[0:rows, ti, 0:256])),
                         reads=[f"tmpf{ti}"], writes=[], dma=f"ot{ti}")
        vres_all = lambda s: [f"Vc{s}_{i}" for i in range(8)]

        PTs = [PT, self.PT2]

        def core_A(h, q_ap, qr, items, buf):
            PTb = PTs[buf]
            WBt, wres = (self.WBt, "WBt") if h % 2 == 0 else (self.WBt2, "WBt2")
            S.op("sp", (lambda e, h=h, WBt=WBt: e.dma_start(out=WBt[:], in_=self.wb_d[h])), reads=["wbd"], writes=[wres], dma=f"wb{h % 2}")
            off = 0
            pts = []
            for (kT, kr, V, vr, wc, q0, qn, nk) in items:
                b = self.bank()
                S.op("pe", (lambda e, b=b, kT=kT, q0=q0, qn=qn, nk=nk: e.matmul(ps[0:nk, b, 0:qn], lhsT=kT, rhs=q_ap[:, q0:q0 + qn],
                                                                              start=True, stop=True)),
                     reads=kr + qr, writes=[f"ps{b}"])
                self.sct_i ^= 1
                scb, scr = ((sct, "sct") if self.sct_i else (self.rs[:, 0, :], "rs0"))
                S.op("dve", (lambda e, b=b, wc=wc, qn=qn, nk=nk, scb=scb: e.scalar_tensor_tensor(out=scb[0:nk, 0:qn], in0=ps[0:nk, b, 0:qn], scalar=SCALE,
                                                                                                 in1=WBt[0:nk, wc:wc + qn], op0=ALU.mult, op1=ALU.add)),
                     reads=[f"ps{b}", wres], writes=[scr])
                pt = PTb[0:nk, off:off + qn]
                pres = [f"PT{buf}b{k}" for k in range(off // 128, (off + qn - 1) // 128 + 1)]
                S.op("act", (lambda e, pt=pt, qn=qn, nk=nk, scb=scb: e.activation(out=pt, in_=scb[0:nk, 0:qn], func=AF.Exp)),
                     reads=[scr], writes=pres)
                pts.append((pt, pres))
                off += qn
            return (h, items, pts)

        def core_B(state, nq, out_ap, out_res):
            h, items, pts = state
            bo, bd = self.bank(), self.bank()
            n = len(items)

            def pv(e):
                ins = None
                for i, (kT, kr, V, vr, wc, q0, qn, nk) in enumerate(items):
                    ins = e.matmul(ps[:, bo, q0:q0 + qn], lhsT=V, rhs=pts[i][0], start=(i == 0), stop=(i == n - 1))
                return ins

            def dn(e):
                ins = None
                for i, (kT, kr, V, vr, wc, q0, qn, nk) in enumerate(items):
                    ins = e.matmul(ps[:, bd, q0:q0 + qn], lhsT=ones[0:nk, :], rhs=pts[i][0], start=(i == 0), stop=(i == n - 1))
                return ins
            allv = [r for it in items for r in it[3]]
            allp = sorted(set(r for p in pts for r in p[1]))
            S.op("pe", pv, reads=allv + allp, writes=[f"ps{bo}"])
            S.op("pe", dn, reads=allp, writes=[f"ps{bd}"])
            ti = self.tmp_i
            self.tmp_i = (ti + 1) % 3
            S.op("dve", (lambda e, bd=bd, ti=ti: e.reciprocal(out=tmpf[:, ti, 0:nq], in_=ps[:, bd, 0:nq])), reads=[f"ps{bd}"], writes=[f"tmpf{ti}"])
            S.op("dve", (lambda e, bo=bo, ti=ti: e.tensor_tensor(out=out_ap, in0=ps[:, bo, 0:nq], in1=tmpf[:, ti, 0:nq], op=ALU.mult)),
                 reads=[f"ps{bo}", f"tmpf{ti}"], writes=[out_res])

        if kind == "p":
            t = tile["t"]
            pend = None
            for h in range(16):
                items = []
                order = [("c", 0)] + ([("p", i) for i in range(4)] if t > 0 else []) + [("c", i) for i in range(1, 4)]
                for (src, i) in order:
                    if src == "c":
                        kT = H[:, 16 + h, i * 128:(i + 1) * 128]
                        V = self.Hflat(32 + 4 * i, 4)[:, h * 128:(h + 1) * 128]
                        items.append((kT, [kres(h)], V, vres_all(i), 0, i * 128, (4 - i) * 128, 128))
                    else:
                        kT = kTp[:, h, i * 128:(i + 1) * 128]
                        V = Vp[:, i, h * 128:(h + 1) * 128]
                        items.append((kT, ["kTp"], V, ["Vp"], (4 - i) * 128, 0, (i + 1) * 128, 128))
                st = core_A(h, H[:, h, 0:nt], [qres(h)], items, h % 2)
                if pend is not None:
                    core_B(pend, nt, H[:, pend[0], 0:nt], qres(pend[0]))
                pend = st
            core_B(pend, nt, H[:, pend[0], 0:nt], qres(pend[0]))
            if t < 3:
                for h in range(16):
                    S.op(self.ve(), (lambda e, h=h: e.tensor_copy(out=kTp[:, h, :], in_=H[:, 16 + h, :])),
                         reads=[kres(h)], writes=["kTp"])
                for i in range(4):
                    S.op(self.ve(), (lambda e, i=i: e.tensor_copy(out=Vp[:, i, :], in_=self.Hflat(32 + 4 * i, 4))),
                         reads=vres_all(i), writes=["Vp"])
        else:
            for s in range(nseg):
                kst = self.Hflat(16, 16).rearrange("p (i d) -> p i d", i=4)
                S.op("pool", (lambda e, s=s, kst=kst: e.dma_start(out=kst, in_=self.c_bk[s].rearrange("(i p) d -> p i d", p=128))),
                     reads=[], writes=rn("H", 16, 16), dma="ck")
                S.op("pool", (lambda e, s=s: e.dma_start(out=Vp[:], in_=self.c_bv[s].rearrange("(i p) d -> p i d", p=128))),
                     reads=[], writes=["Vp"], dma="cv")
                cnt = 0
                for i in range(4):
                    for h4 in range(4):
                        b = self.bank()
                        pbf = ps[:, b, 0:256].bitcast(BF16)

                        def tr(e, i=i, h4=h4, pbf=pbf, kst=kst):
                            ins = None
                            for hh in range(4):
                                h = h4 * 4 + hh
                                ins = e.transpose(pbf[:, hh * 128:(hh + 1) * 128], kst[:, i, h * 128:(h + 1) * 128], self.idb[:])
                            return ins
                        S.op("pe", tr, reads=rn("H", 16, 16), writes=[f"ps{b}"])
                        eng = "act" if cnt % 2 else "dve"
                        cnt += 1
                        dst = kTp[:, h4 * 4:h4 * 4 + 4, i * 128:(i + 1) * 128]
                        src = pbf.rearrange("p (h k) -> p h k", h=4)
                        if eng == "act":
                            S.op("act", (lambda e, dst=dst, src=src: e.activation(out=dst, in_=src, func=AF.Copy)),
                                 reads=[f"ps{b}"], writes=["kTp"])
                        else:
                            S.op("dve", (lambda e, dst=dst, src=src: e.tensor_copy(out=dst, in_=src)),
                                 reads=[f"ps{b}"], writes=["kTp"])
                pend = None
                for h in range(16):
                    items = []
                    for i in range(4):
                        items.append((kTp[:, h, i * 128:(i + 1) * 128], ["kTp"], Vp[:, i, h * 128:(h + 1) * 128], ["Vp"],
                                      (4 - i) * 128, 0, LS, 128))
                    kT = H[:, h, 256 + s * LS:256 + (s + 1) * LS]
                    V = vcur(s)[:, h * 128:(h + 1) * 128]
                    items.append((kT, [kres(h)], V, vres_all(s), 0, 0, LS, LS))
                    st = core_A(h, H[:, h, s * LS:(s + 1) * LS], [qres(h)], items, h % 2)
                    if pend is not None:
                        core_B(pend, LS, H[:, pend[0], s * LS:(s + 1) * LS], qres(pend[0]))
                    pend = st
                core_B(pend, LS, H[:, pend[0], s * LS:(s + 1) * LS], qres(pend[0]))
        self.proj_out("ao", self.w_ao, DC, 0, 8 + 3, nt)

    def mem_attn(self, l, tile):
        S = self.S
        nt, nseg, kind = tile["nt"], tile["nseg"], tile["kind"]
        ps, xn, H, PT, sct, ones, memKT, memV = self.ps, self.xn, self.H, self.PT, self.sct, self.ones, self.memKT, self.memV
        self.prenorm(l * 8 + 4, nt)
        w = self.w_mq[l]
        preq = []
        groups = []
        for ub_ in range(2):
            u, ures = self.unit(f"mq{l}_{ub_}", DC, 256, [(w[:, ub_ * 256:(ub_ + 1) * 256], 0)])
            bks = (self.bank(), self.bank())
            preq.append(bks)
            for jj in range(2):
                groups.append((ps[:, bks[jj], 0:nt], (lambda k, u=u, jj=jj: u[:, k, jj * 128:(jj + 1) * 128]), ures, bks[jj]))
        self.mm_kouter(groups, DC, lambda k: xn[:, k, 0:nt], lambda k: f"xn{k}")
        for ub_ in range(2):
            for jj in range(2):
                hm = ub_ * 2 + jj
                b = preq[ub_][jj]
                S.op("act", (lambda e, hm=hm, b=b: e.activation(out=H[:, hm, 0:nt], in_=ps[:, b, 0:nt], func=AF.Copy)),
                     reads=[f"ps{b}"], writes=[f"H{hm}"])
        L = nt // nseg
        for s in range(nseg):
            if kind == "s":
                self.load_mem_cache(l, s)
            pend = None
            for hm in range(4):
                q = H[:, hm, s * L:(s + 1) * L]
                buf = hm % 2
                PTb = [PT, self.PT2][buf]
                pts = []
                for mt in range(2):
                    b = self.bank()
                    S.op("pe", (lambda e, b=b, mt=mt, hm=hm, q=q: e.matmul(ps[:, b, 0:L], lhsT=memKT[:, l, hm, mt * 128:(mt + 1) * 128], rhs=q,
                                                                       start=True, stop=True)),
                         reads=[f"mKT{l}", f"H{hm}"], writes=[f"ps{b}"])
                    pt = PTb[:, mt * 512:mt * 512 + L]
                    S.op("act", (lambda e, b=b, pt=pt: e.activation(out=pt, in_=ps[:, b, 0:L], func=AF.Exp, scale=SCALE)),
                         reads=[f"ps{b}"], writes=rn(f"PT{buf}b", mt * 4, 4))
                    pts.append(pt)
                st = (hm, pts, buf)
                if pend is not None:
                    self.mem_B(l, pend, s, L)
                pend = st
            self.mem_B(l, pend, s, L)
        wo = self.w_mo[l]
        self.post_begin(l * 8 + 5)
        for ub_ in range(2):
            u, ures = self.unit(f"mo{l}_{ub_}", 4, 1024, [(wo[:, ub_ * 1024:(ub_ + 1) * 1024], 0)])
            for jj in range(8):
                j = ub_ * 8 + jj
                b = self.bank()
                self.mm_group(ps[:, b, 0:nt], [(u[:, k, jj * 128:(jj + 1) * 128], H[:, 4 + k, 0:nt]) for k in range(4)],
                              reads=ures + rn("H", 4, 4), writes=[f"ps{b}"])
                self.post_chunk(j, b, nt)
        self.post_end(l * 8 + 5, nt)

    def mem_B(self, l, st, s, L):
        S = self.S
        ps, H, ones, memV, tmpf = self.ps, self.H, self.ones, self.memV, self.tmpf
        hm, pts, buf = st
        bo, bd = self.bank(), self.bank()

        def pv(e):
            e.matmul(ps[:, bo, 0:L], lhsT=memV[:, l, 0, hm * 128:(hm + 1) * 128], rhs=pts[0], start=True, stop=False)
            return e.matmul(ps[:, bo, 0:L], lhsT=memV[:, l, 1, hm * 128:(hm + 1) * 128], rhs=pts[1], start=False, stop=True)

        def dn(e):
            e.matmul(ps[:, bd, 0:L], lhsT=ones[:], rhs=pts[0], start=True, stop=False)
            return e.matmul(ps[:, bd, 0:L], lhsT=ones[:], rhs=pts[1], start=False, stop=True)
        S.op("pe", pv, reads=[f"mV{l}"] + rn(f"PT{buf}b", 0, 8), writes=[f"ps{bo}"])
        S.op("pe", dn, reads=rn(f"PT{buf}b", 0, 8), writes=[f"ps{bd}"])
        ti = self.tmp_i
        self.tmp_i = (ti + 1) % 3
        S.op("dve", (lambda e: e.reciprocal(out=tmpf[:, ti, 0:L], in_=ps[:, bd, 0:L])), reads=[f"ps{bd}"], writes=[f"tmpf{ti}"])
        S.op("dve", (lambda e: e.tensor_tensor(out=H[:, 4 + hm, s * L:(s + 1) * L], in0=ps[:, bo, 0:L],
                                                in1=tmpf[:, ti, 0:L], op=ALU.mult)),
             reads=[f"ps{bo}", f"tmpf{ti}"], writes=[f"H{4 + hm}"])

    def kT_from_tokmajor(self, l, src_bf, src_res):
        S = self.S
        ps, memKT = self.ps, self.memKT
        for mt in range(2):
            b = self.bank()
            pbf = ps[:, b, 0:256].bitcast(BF16)

            def tr(e, mt=mt, pbf=pbf):
                ins = None
                for hm in range(4):
                    ins = e.transpose(pbf[:, hm * 128:(hm + 1) * 128], src_bf[:, mt, hm * 128:(hm + 1) * 128], self.idb[:])
                return ins
            S.op("pe", tr, reads=src_res, writes=[f"ps{b}"])
            S.op("dve", (lambda e, mt=mt, pbf=pbf: e.tensor_copy(out=memKT[:, l, :, mt * 128:(mt + 1) * 128],
                                                                in_=pbf.rearrange("p (h k) -> p h k", h=4))),
                 reads=[f"ps{b}"], writes=[f"mKT{l}"])

    def load_mem_cache(self, l, s):
        S = self.S
        kst = self.Hflat(8, 2).rearrange("p (m d) -> p m d", m=2)
        S.op("pool", (lambda e: e.dma_start(out=kst, in_=self.c_mk[l, s].rearrange("(m p) d -> p m d", p=128))),
             reads=[], writes=rn("H", 8, 2), dma="mk")
        S.op("pool", (lambda e: e.dma_start(out=self.memV[:, l], in_=self.c_mv[l, s].rearrange("(m p) d -> p m d", p=128))),
             reads=[], writes=[f"mV{l}"], dma="mv")
        self.kT_from_tokmajor(l, kst, rn("H", 8, 2))

    def mem_project(self, b_):
        S = self.S
        ps, H, ss, GM, tmpf = self.ps, self.H, self.ss, self.GM, self.tmpf
        xs = self.Hf(0, 16).rearrange("p (s d) -> p s d", s=2)
        junk = self.Hflat(16, 4)
        mn = self.Hflat(20, 8).rearrange("p (s d) -> p s d", s=2)
        mnT = self.Hflat(32, 8).rearrange("p (c t) -> p c t", c=DC)
        mT = self.Hflat(40, 8).rearrange("p (c t) -> p c t", c=DC)
        S.op("dve", lambda e: e.memset(ss[:, 0:2], 0.0), writes=["ss0", "ss1"])
        for s in range(2):
            S.op("sp", (lambda e, s=s: e.dma_start(out=xs[:, s, :], in_=self.mem_p[b_, s * 128:(s + 1) * 128, :])),
                 reads=[], writes=rn("H", 8 * s, 8), dma=f"xl{s}")
            S.op("act", (lambda e, s=s: e.activation(out=junk, in_=xs[:, s, :], func=AF.Square, accum_out=ss[:, s:s + 1])),
                 reads=rn("H", 8 * s, 8), writes=rn("H", 16, 4) + [f"ss{s}"])
            S.op("act", (lambda e, s=s: e.activation(out=ss[:, 2 + s:3 + s], in_=ss[:, s:s + 1], func=AF.Sqrt, scale=1.0 / D, bias=self.eps[:, 0:1])),
                 reads=[f"ss{s}"], writes=[f"ss{2 + s}"])
            S.op("dve", (lambda e, s=s: e.reciprocal(out=ss[:, 4 + s:5 + s], in_=ss[:, 2 + s:3 + s])), reads=[f"ss{2 + s}"], writes=[f"ss{4 + s}"])
            S.op("dve", (lambda e, s=s: e.tensor_scalar(out=mn[:, s, :], in0=xs[:, s, :], scalar1=ss[:, 4 + s:5 + s], scalar2=0.0, op0=ALU.mult, op1=ALU.add)),
                 reads=rn("H", 8 * s, 8) + [f"ss{4 + s}"], writes=rn("H", 20 + 4 * s, 4))
            if self.sub < 2:
                continue
            for c4 in range(4):
                b = self.bank()
                pbf = ps[:, b, 0:256].bitcast(BF16)

                def tr(e, s=s, c4=c4, pbf=pbf):
                    ins = None
                    for cc in range(4):
                        c = c4 * 4 + cc
                        ins = e.transpose(pbf[:, cc * 128:(cc + 1) * 128], mn[:, s, c * 128:(c + 1) * 128], self.idb[:])
                    return ins
                S.op("pe", tr, reads=rn("H", 20 + 4 * s, 4), writes=[f"ps{b}"])
                S.op("dve", (lambda e, s=s, c4=c4, pbf=pbf: e.tensor_copy(out=mnT[:, c4 * 4:c4 * 4 + 4, s * 128:(s + 1) * 128],
                                                                         in_=pbf.rearrange("p (c t) -> p c t", c=4))),
                     reads=[f"ps{b}"], writes=rn("H", 32, 8))
        if self.sub < 3:
            return
        for l in range(2):
            for c in range(DC):
                S.op("dve", (lambda e, c=c, l=l: e.tensor_scalar(out=mT[:, c, :], in0=mnT[:, c, :], scalar1=GM[:, l, c:c + 1], scalar2=0.0, op0=ALU.mult, op1=ALU.add)),
                     reads=rn("H", 32, 8), writes=rn("H", 40, 8))
            kb = self.Hflat(16, 2).rearrange("p (m d) -> p m d", m=2)
            for ub_ in range(4):
                if self.sub < 4:
                    continue
                u, ures = self.unit(f"mkv{l}_{ub_}", DC, 256, [(self.w_mkv[l][:, ub_ * 256:(ub_ + 1) * 256], 0)])
                for s in range(2):
                    if self.sub < 5:
                        continue
                    b = self.bank()
                    self.mm_group(ps[:, b, 0:256], [(mT[:, k, s * 128:(s + 1) * 128], u[:, k, :]) for k in range(DC)],
                                  reads=ures + rn("H", 40, 8), writes=[f"ps{b}"])
                    ti = self.tmp_i
                    self.tmp_i = (ti + 1) % 3
                    S.op("act", (lambda e, b=b, ti=ti: e.activation(out=tmpf[:, ti, 0:256], in_=ps[:, b, 0:256], func=AF.Copy)),
                         reads=[f"ps{b}"], writes=[f"tmpf{ti}"])
                    if ub_ < 2:
                        dst = self.o_mk_p[l, b_, s * 128:(s + 1) * 128, ub_ * 256:(ub_ + 1) * 256]
                        bdst = kb[:, s, ub_ * 256:(ub_ + 1) * 256]
                        bres = [f"H{16 + s}"]
                    else:
                        dst = self.o_mv_p[l, b_, s * 128:(s + 1) * 128, (ub_ - 2) * 256:(ub_ - 1) * 256]
                        bdst = self.memV[:, l, s, (ub_ - 2) * 256:(ub_ - 1) * 256]
                        bres = [f"mV{l}"]
                    if self.sub >= 6:
                        S.op("act", (lambda e, dst=dst, ti=ti: e.dma_start(out=dst, in_=tmpf[:, ti, 0:256])),
                             reads=[f"tmpf{ti}"], writes=[], dma=f"ot{ti}")
                    S.op("dve", (lambda e, bdst=bdst, b=b: e.tensor_copy(out=bdst, in_=ps[:, b, 0:256])),
                         reads=[f"ps{b}"], writes=bres)
            if self.sub >= 7:
                self.kT_from_tokmajor(l, kb, rn("H", 16, 2))

    def load_x(self, tile):
        S = self.S
        ps, xres = self.ps, self.xres
        nt = tile["nt"]
        nsub = nt // 128
        for s in range(nsub):
            hs0 = 32 + 8 * (s % 2)
            xs = self.Hf(hs0, 8)
            if tile["kind"] == "p":
                src = self.x_p[tile["b"], tile["t"] * 512 + s * 128: tile["t"] * 512 + (s + 1) * 128, :]
            else:
                src = self.x_s[s * 128:(s + 1) * 128, :]
            S.op("sp", (lambda e, xs=xs, src=src: e.dma_start(out=xs, in_=src)), reads=[], writes=rn("H", hs0, 8), dma=f"xl{s}")
            for c4 in range(4):
                b = self.bank()

                def tr(e, xs=xs, c4=c4, b=b):
                    ins = None
                    for cc in range(4):
                        c = c4 * 4 + cc
                        ins = e.transpose(ps[:, b, cc * 128:(cc + 1) * 128], xs[:, c * 128:(c + 1) * 128], self.idf[:])
                    return ins
                S.op("pe", tr, reads=rn("H", hs0, 8), writes=[f"ps{b}"])
                eng = "act" if c4 % 2 else "dve"
                dst = xres[:, c4 * 4:c4 * 4 + 4, s * 128:(s + 1) * 128]
                src3 = ps[:, b, :].rearrange("p (c t) -> p c t", c=4)
                if eng == "act":
                    S.op("act", (lambda e, dst=dst, src3=src3: e.activation(out=dst, in_=src3, func=AF.Copy)),
                         reads=[f"ps{b}"], writes=rn("xres", c4 * 4, 4))
                else:
                    S.op("dve", (lambda e, dst=dst, src3=src3: e.tensor_copy(out=dst, in_=src3)),
                         reads=[f"ps{b}"], writes=rn("xres", c4 * 4, 4))

    def store_y(self, tile):
        S = self.S
        ps, xres = self.ps, self.xres
        nt = tile["nt"]
        nsub = nt // 128
        for s in range(nsub):
            ys = self.Hf(8 * s, 8)
            for c4 in range(4):
                b = self.bank()

                def tr(e, c4=c4, b=b, s=s):
                    ins = None
                    for cc in range(4):
                        c = c4 * 4 + cc
                        ins = e.transpose(ps[:, b, cc * 128:(cc + 1) * 128], xres[:, c, s * 128:(s + 1) * 128], self.idf[:])
                    return ins
                S.op("pe", tr, reads=rn("xres", c4 * 4, 4), writes=[f"ps{b}"])
                dst = ys[:, c4 * 512:(c4 + 1) * 512]
                if c4 % 2:
                    S.op("act", (lambda e, dst=dst, b=b: e.activation(out=dst, in_=ps[:, b, :], func=AF.Copy)),
                         reads=[f"ps{b}"], writes=rn("H", 8 * s + 2 * c4, 2))
                else:
                    S.op("dve", (lambda e, dst=dst, b=b: e.tensor_copy(out=dst, in_=ps[:, b, :])),
                         reads=[f"ps{b}"], writes=rn("H", 8 * s + 2 * c4, 2))
            if tile["kind"] == "p":
                dstd = self.y_p[tile["b"], tile["t"] * 512 + s * 128: tile["t"] * 512 + (s + 1) * 128, :]
            else:
                dstd = self.y_s[s * 128:(s + 1) * 128, :]
            S.op("act", (lambda e, ys=ys, dstd=dstd: e.dma_start(out=dstd, in_=ys)), reads=rn("H", 8 * s, 8), writes=[], dma=f"ys{s}")

    def prologue(self):
        S = self.S
        nc = self.nc
        S.op("sp", lambda e: e.dma_start(out=self.idf[:], in_=self.ident_in[:, :]), writes=["idf"], dma="c0")
        S.op("dve", lambda e: e.tensor_copy(out=self.idb[:], in_=self.idf[:]), reads=["idf"], writes=["idb"])
        S.op("dve", lambda e: e.memset(self.ones[:], 1.0), writes=["ones"])
        S.op("dve", lambda e: e.memset(self.eps[:], EPS), writes=["eps"])
        S.op("dve", lambda e: e.memset(self.uh[:].rearrange("p a b c -> p (a b c)"), 0.0), writes=rn("uh", 0, DC))
        for l in range(2):
            for n in range(8):
                S.op("sp", (lambda e, l=l, n=n: e.dma_start(out=self.G[:, l * 8 + n, :], in_=self.g_norm[l, n].rearrange("(c p) -> p c", p=128),
                                                           allow_slow_non_contiguous=True)), writes=[f"G{l * 8 + n}"], dma="c1")
            S.op("sp", (lambda e, l=l: e.dma_start(out=self.GM[:, l, :], in_=self.g_mem[l].rearrange("(c p) -> p c", p=128),
                                                  allow_slow_non_contiguous=True)), writes=[f"GM{l}"], dma="c1")
        for k in range(3):
            S.op("sp", (lambda e, k=k: e.dma_start(out=self.CW[:, k, :], in_=self.w_cdw[k].rearrange("(c p) -> p c", p=128),
                                                  allow_slow_non_contiguous=True)), writes=[f"CW{k}"], dma="c1")
        allc = [f"G{i}" for i in range(16)] + ["GM0", "GM1", "CW0", "CW1", "CW2"]
        self.join("dve", allc)
        for gi in (1, 7, 9, 15):
            S.op("dve", (lambda e, gi=gi: e.tensor_scalar(out=self.G[:, gi, :], in0=self.G[:, gi, :], scalar1=0.5, scalar2=0.0, op0=ALU.mult, op1=ALU.add)),
                 reads=[f"G{gi}"], writes=[f"G{gi}"])
        S.op("dve", lambda e: e.memset(self.ss[:], 0.0), reads=allc + ["ones", "eps", "idb", "idf"], writes=["consts"] + [f"ss{i}" for i in range(8)])
        for eng in ("pe", "act", "pool"):
            S.op(eng, lambda e: e.nop(), reads=["consts"])
        if self.stage < 2:
            return
        VPB = self.Hf(0, 48).rearrange("p (h l) -> p h l", h=16)
        S.op("sp", lambda e: e.dma_start(out=VPB[:, :, 0:257], in_=bass.AP(self.relb.tensor, 0, [[0, 128], [257, 16], [1, 257]])),
             writes=rn("H", 0, 48), dma="c2")
        S.op("dve", lambda e: e.tensor_copy(out=VPB[:, :, 257:768], in_=VPB[:, :, 256:257].broadcast_to([128, 16, 511])),
             reads=rn("H", 0, 48), writes=["vpb"])
        S.op("sp", lambda e: e.dma_start(out=bass.AP(self.vp_d, 0, [[16 * 768, 128], [1, 16 * 768]]), in_=self.Hf(0, 48)),
             reads=["vpb"] + rn("H", 0, 48), writes=["vpd"], dma="c3")
        WBA = self.xres[:].rearrange("p a b -> p (a b)")[:, 0:8 * 640].rearrange("p (h c) -> p h c", h=8)
        for half in range(2):
            S.op("sp", (lambda e, half=half: e.dma_start(out=WBA, in_=bass.AP(self.vp_d, half * 8 * 768 + 128,
                                                                             [[16 * 768 - 1, 128], [768, 8], [1, 640]]))),
                 reads=["vpd"], writes=["wba"], dma="c4")
            S.op("dve", lambda e: e.memset(WBA[64:128, :, 0:64], NEG), reads=["wba"], writes=["wba"])
            S.op("dve", lambda e: e.memset(WBA[0:64, :, 576:640], NEG), reads=["wba"], writes=["wba"])
            S.op("sp", (lambda e, half=half: e.dma_start(out=self.wb_d[half * 8:(half + 1) * 8].rearrange("h p c -> p h c"), in_=WBA)),
                 reads=["wba"], writes=["wbd"], dma="c5")
        S.op("dve", lambda e: e.nop(), reads=["wbd", "wba", "vpb"], writes=rn("xres", 0, DC) + rn("H", 0, 48))

    def build(self):
        self.dram()
        self.sbuf()
        S = self.S
        self.prologue()
        tiles = []
        for b in range(NPS):
            for t in range(4):
                tiles.append(dict(kind="p", b=b, t=t, nt=512, nseg=1))
        tiles.append(dict(kind="s", nt=256, nseg=NSS))
        if self.cfg_tiles is not None:
            tiles = [tiles[i] for i in self.cfg_tiles]
        for ti_, tile in enumerate(tiles):
            self.pool_ok = ti_ > 0
            nt = tile["nt"]
            if self.stage < 3:
                break
            if tile["kind"] == "p" and tile["t"] == 0:
                self.mem_project(tile["b"])
                S.op("dve", lambda e: e.memset(self.uh[:].rearrange("p a b c -> p (a b c)"), 0.0), writes=rn("uh", 0, DC))
            if tile["kind"] == "s":
                self.join("dve", rn("uh", 0, DC))
                for c in range(DC):
                    S.op("sp", (lambda e, c=c: e.dma_start(out=self.uh[:, c, :, :],
                                                          in_=self.st_conv[:, :, c * 128:(c + 1) * 128].rearrange("s t p -> p s t"),
                                                          allow_slow_non_contiguous=True)),
                         writes=[f"uh{c}"], dma="hs")
                self.join("dve", rn("uh", 0, DC))
            if self.stage < 4:
                break
            self.load_x(tile)
            for l in range(self.nlayers):
                self.ffn(l, 1, nt)
                if l == 0:
                    self.conv(tile)
                else:
                    self.attn(tile)
                self.mem_attn(l, tile)
                self.ffn(l, 2, nt)
            self.store_y(tile)
        finals = [(k, 16 * v) for k, v in S.dcnt.items()]
        nc = self.nc
        sems = {}
        for k in list(S.dcnt.keys()):
            sems["d:" + k] = self.es.enter_context(nc.semaphore("d_" + k))
        for k in ("pe", "act", "dve", "pool"):
            sems["e:" + k] = self.es.enter_context(nc.semaphore("e_" + k))
        engmap = {"pe": "tensor", "act": "scalar", "dve": "vector", "pool": "gpsimd", "sp": "sync"}
        with nc.Block() as block:
            for ek, attr in engmap.items():
                items = S.q[ek]

                def body(e, items=items, ek=ek):
                    for waits, fn, tok in items:
                        for sk, v in waits:
                            e.wait_ge(sems[sk], v)
                        ins = fn(e)
                        ins.then_inc(sems[tok[0]], 16 if tok[0].startswith("d:") else 1)
                    if ek == "sp":
                        for k, v in finals:
                            e.wait_ge(sems["d:" + k], v)
                getattr(block, attr)(body)
        return nc


_CACHE = {}


def kernel(x_prompt, x_sample, state_conv, cache_band_k, cache_band_v, cache_mem_k, cache_mem_v,
           mem_prompt, g_norm, g_mem, w_ffn1_in, w_ffn1_out, w_ffn2_in, w_ffn2_out,
           w_conv_in, w_conv_dw, w_conv_out, w_attn_qkv, rel_bias, w_attn_o,
           w_mem_q, w_mem_kv, w_mem_o):
    f = lambda a: np.ascontiguousarray(np.asarray(a), dtype=np.float32)
    B = Builder()
    nc = B.build()
    x_prompt, x_sample = f(x_prompt), f(x_sample)
    shared = dict(
        g_norm=f(g_norm), g_mem=f(g_mem), w_f1i=f(w_ffn1_in), w_f1o=f(w_ffn1_out), w_f2i=f(w_ffn2_in), w_f2o=f(w_ffn2_out),
        w_ci=f(w_conv_in)[0], w_cdw=f(w_conv_dw)[0], w_co=f(w_conv_out)[0], w_qkv=f(w_attn_qkv)[0], relb=f(rel_bias)[0],
        w_ao=f(w_attn_o)[0], w_mq=f(w_mem_q), w_mkv=f(w_mem_kv), w_mo=f(w_mem_o),
        ident_in=np.eye(128, dtype=np.float32),
    )
    state_conv, cache_band_k, cache_band_v = f(state_conv), f(cache_band_k), f(cache_band_v)
    cache_mem_k, cache_mem_v, mem_prompt = f(cache_mem_k), f(cache_mem_v), f(mem_prompt)
    in_maps = []
    for i in range(NCORES):
        ps_, ss_ = slice(NPS * i, NPS * (i + 1)), slice(NSS * i, NSS * (i + 1))
        m = dict(shared)
        m["x_p"] = x_prompt[ps_]
        m["x_s"] = x_sample[ss_].reshape(NSS * LS, D)
        m["st_conv"] = state_conv[0, ss_]
        m["c_bk"] = cache_band_k[0, ss_].reshape(NSS, 512, D)
        m["c_bv"] = cache_band_v[0, ss_].reshape(NSS, 512, D)
        m["c_mk"] = cache_mem_k[:, ss_].reshape(2, NSS, NMEM, MW)
        m["c_mv"] = cache_mem_v[:, ss_].reshape(2, NSS, NMEM, MW)
        m["mem_p"] = mem_prompt[ps_]
        in_maps.append({k: np.ascontiguousarray(v) for k, v in m.items()})
    res = run_bass_kernel_spmd(nc, in_maps, core_ids=list(range(NCORES)))
    R = res.results
    cat = lambda k, ax=0: np.concatenate([np.asarray(r[k]) for r in R], axis=ax)
    y_prompt = cat("y_p")
    y_sample = cat("y_s").reshape(NCORES * NSS, LS, D)
    conv_p = cat("o_conv_p")[None]
    bk_p = cat("o_bk_p").reshape(1, NCORES * NPS, 512, 16, 128)
    bv_p = cat("o_bv_p").reshape(1, NCORES * NPS, 512, 16, 128)
    mk_p = cat("o_mk_p", 1).reshape(2, NCORES * NPS, NMEM, 4, 128)
    mv_p = cat("o_mv_p", 1).reshape(2, NCORES * NPS, NMEM, 4, 128)
    conv_s = cat("o_conv_s")[None]
    bk_s = cat("o_bk_s").reshape(1, NCORES * NSS, LS, 16, 128)
    bv_s = cat("o_bv_s").reshape(1, NCORES * NSS, LS, 16, 128)
    return (y_prompt, y_sample, conv_p, bk_p, bv_p, mk_p, mv_p, conv_s, bk_s, bv_s)
```
